# Optimizing a Trainium2 kernel written in Bass

```python
import math
import jax
import jax.numpy as jnp
from jax import lax
import numpy as np

D_MODEL = 1024
BATCH = 8
SEQ = 2048
DEPTH = 1
DEC_BATCH = 128
DEC_SEQ = 4
PAST_LEN = 16384
PAGE_SIZE = 128

GLA_HEADS = 4
GLA_DK = D_MODEL // 8
GLA_DV = D_MODEL // 4
GLA_LOWRANK = 16
GLA_TAU = 16.0
GLA_CHUNK = 16
RET_HEADS = 4
RET_DK = D_MODEL // 4
RET_DV = D_MODEL // 2
RET_CHUNK = 128
ROPE_BASE = 10000.0

GK = GLA_HEADS * GLA_DK
GV = GLA_HEADS * GLA_DV
RK = RET_HEADS * RET_DK
RV = RET_HEADS * RET_DV
IN_SIZES = (GK, GK, GV, GV, GLA_LOWRANK, RK, RK, RV, RV, D_MODEL, D_MODEL)
D_IN = 2 * GK + 2 * GV + GLA_LOWRANK + 2 * RK + 2 * RV + 2 * D_MODEL

DEEPNORM_ALPHA = (2.0 * DEPTH) ** 0.25
DEEPNORM_BETA = (8.0 * DEPTH) ** -0.25
LN_EPS = 1e-5
HEAD_NORM_EPS = 1e-5

kernel_name = 'gla_retnet_gated_hybrid_step'

F32 = jnp.float32


def _split_in(p):
    cuts = [int(c) for c in np.cumsum(IN_SIZES)[:-1]]
    return jnp.split(p, cuts, axis=-1)


def _layernorm(x, g, b):
    x32 = x.astype(F32)
    mu = jnp.mean(x32, -1, keepdims=True)
    var = jnp.mean(jnp.square(x32 - mu), -1, keepdims=True)
    return ((x32 - mu) * lax.rsqrt(var + LN_EPS)).astype(x.dtype) * g + b


def _head_rmsnorm(o):
    B, T, H, DV = o.shape
    o = o * lax.rsqrt(jnp.mean(jnp.square(o), -1, keepdims=True) + HEAD_NORM_EPS)
    return o.reshape(B, T, H * DV)


def _head_groupnorm(o):
    B, T, H, DV = o.shape
    mu = jnp.mean(o, -1, keepdims=True)
    var = jnp.mean(jnp.square(o - mu), -1, keepdims=True)
    return ((o - mu) * lax.rsqrt(var + HEAD_NORM_EPS)).reshape(B, T, H * DV)


def _rotary(x, pos):
    half = x.shape[-1] // 2
    inv_freq = ROPE_BASE ** (-jnp.arange(half, dtype=F32) / half)
    ang = pos.astype(F32)[:, None] * inv_freq[None, :]
    cos = jnp.cos(ang)[None, :, None, :]
    sin = jnp.sin(ang)[None, :, None, :]
    x32 = x.astype(F32)
    x1, x2 = x32[..., :half], x32[..., half:]
    return jnp.concatenate([x1 * cos - x2 * sin, x1 * sin + x2 * cos], axis=-1)


def _to_chunks(a, L):
    B, T, H, d = a.shape
    return a.astype(F32).reshape(B, T // L, L, H, d).transpose(1, 0, 3, 2, 4)


def _from_chunks(o):
    N, B, H, L, d = o.shape
    return o.transpose(1, 0, 3, 2, 4).reshape(B, N * L, H, d)


def _gla(q, k, v, log_alpha, s0):
    T = q.shape[1]
    L = math.gcd(T, GLA_CHUNK)
    causal = jnp.tril(jnp.ones((L, L), dtype=bool))

    def step(S, inp):
        qc, kc, vc, gc = inp
        b = jnp.cumsum(gc, axis=2)
        q_dec = qc * jnp.exp(b)
        k_dec = kc * jnp.exp(-b)
        A = jnp.where(causal, jnp.einsum('bhtd,bhsd->bhts', q_dec, k_dec), 0.0)
        o = jnp.einsum('bhtd,bhdv->bhtv', q_dec, S) + jnp.einsum('bhts,bhsv->bhtv', A, vc)
        b_last = b[:, :, -1:, :]
        S_new = jnp.exp(b_last[:, :, 0, :])[..., None] * S + jnp.einsum(
            'bhsd,bhsv->bhdv', kc * jnp.exp(b_last - b), vc)
        return S_new, o

    S, o = lax.scan(step, s0.astype(F32),
                    (_to_chunks(q, L), _to_chunks(k, L), _to_chunks(v, L), _to_chunks(log_alpha, L)))
    return _from_chunks(o), S


def _retention(q, k, v, s0):
    T, H = q.shape[1], q.shape[2]
    L = math.gcd(T, RET_CHUNK)
    log_gamma = jnp.log1p(-jnp.exp2(-5.0 - jnp.arange(H, dtype=F32)))
    idx = jnp.arange(L, dtype=F32)
    diff = idx[:, None] - idx[None, :]
    decay_intra = jnp.where(diff[None] >= 0, jnp.exp(jnp.maximum(diff, 0.0)[None] * log_gamma[:, None, None]), 0.0)
    decay_read = jnp.exp((idx + 1.0)[None, :] * log_gamma[:, None])
    decay_write = jnp.exp((L - 1.0 - idx)[None, :] * log_gamma[:, None])
    decay_chunk = jnp.exp(L * log_gamma)

    def step(S, inp):
        qc, kc, vc = inp
        A = jnp.einsum('bhtd,bhsd->bhts', qc, kc) * decay_intra
        o = jnp.einsum('bhts,bhsv->bhtv', A, vc) + jnp.einsum(
            'bhtd,bhdv->bhtv', qc, S) * decay_read[None, :, :, None]
        S_new = decay_chunk[None, :, None, None] * S + jnp.einsum(
            'bhsd,bhsv->bhdv', kc * decay_write[None, :, :, None], vc)
        return S_new, o

    S, o = lax.scan(step, s0.astype(F32), (_to_chunks(q, L), _to_chunks(k, L), _to_chunks(v, L)))
    return _from_chunks(o), S


def _layer(x, c, s_gla, s_ret, pos, w_ada, b_ada, w_in, w_lr2, b_lr2, gla_norm_g, ret_norm_g,
           w_branch_gla, w_branch_ret, w_out, ln_g, ln_b):
    B, T, _ = x.shape
    shift, scale, gate = jnp.split(c @ w_ada + b_ada, 3, axis=-1)
    h = x * (1.0 + scale[:, None, :]) + shift[:, None, :]
    qg, kg, vg, zg, lr, qr, kr, vr, zr, mg, mr = _split_in(h @ w_in)

    log_alpha = jax.nn.log_sigmoid((lr @ w_lr2 + b_lr2).astype(F32)) / GLA_TAU
    o_g, s_gla_new = _gla(qg.reshape(B, T, GLA_HEADS, GLA_DK) * (GLA_DK ** -0.5),
                          kg.reshape(B, T, GLA_HEADS, GLA_DK),
                          vg.reshape(B, T, GLA_HEADS, GLA_DV),
                          log_alpha.reshape(B, T, GLA_HEADS, GLA_DK), s_gla)
    o_g = _head_rmsnorm(o_g).astype(x.dtype) * gla_norm_g * jax.nn.silu(zg)

    o_r, s_ret_new = _retention(_rotary(qr.reshape(B, T, RET_HEADS, RET_DK), pos),
                                _rotary(kr.reshape(B, T, RET_HEADS, RET_DK), pos) * (RET_DK ** -0.5),
                                vr.reshape(B, T, RET_HEADS, RET_DV), s_ret)
    o_r = _head_groupnorm(o_r).astype(x.dtype) * ret_norm_g * jax.nn.silu(zr)

    merged = jax.nn.sigmoid(mg) * (o_g @ w_branch_gla) + jax.nn.sigmoid(mr) * (o_r @ w_branch_ret)
    y = _layernorm(DEEPNORM_ALPHA * x + gate[:, None, :] * (merged @ w_out), ln_g, ln_b)
    return y, s_gla_new.astype(s_gla.dtype), s_ret_new.astype(s_ret.dtype)


def setup_inputs(seed: int = 0) -> dict:
    key = jax.random.key(seed)
    ks = jax.random.split(key, 20)
    nrm = jax.random.normal
    col_scale = jnp.concatenate([
        jnp.full((s,), sc, dtype=F32) for s, sc in zip(
            IN_SIZES, (1.0, 1.0, DEEPNORM_BETA, 1.0, 1.0, 1.0, 1.0, DEEPNORM_BETA, 1.0, 1.0, 1.0))])
    return {
        'x_prompt': nrm(ks[0], (BATCH, SEQ, D_MODEL), F32),
        'x_sample': nrm(ks[1], (DEC_BATCH, DEC_SEQ, D_MODEL), F32),
        'state_gla': 0.1 * nrm(ks[2], (DEPTH, DEC_BATCH, GLA_HEADS, GLA_DK, GLA_DV), F32),
        'state_ret': 0.1 * nrm(ks[3], (DEPTH, DEC_BATCH, RET_HEADS, RET_DK, RET_DV), F32),
        'c_prompt': nrm(ks[4], (BATCH, D_MODEL), F32),
        'c_sample': nrm(ks[5], (DEC_BATCH, D_MODEL), F32),
        'w_ada': 0.5 * D_MODEL ** -0.5 * nrm(ks[6], (DEPTH, D_MODEL, 3 * D_MODEL), F32),
        'b_ada': 0.02 * nrm(ks[7], (DEPTH, 3 * D_MODEL), F32),
        'w_in': D_MODEL ** -0.5 * nrm(ks[8], (DEPTH, D_MODEL, D_IN), F32) * col_scale,
        'w_lr2': GLA_LOWRANK ** -0.5 * nrm(ks[9], (DEPTH, GLA_LOWRANK, GK), F32),
        'b_lr2': 0.1 * nrm(ks[10], (DEPTH, GK), F32),
        'gla_norm_g': 1.0 + 0.02 * nrm(ks[11], (DEPTH, GV), F32),
        'ret_norm_g': 1.0 + 0.02 * nrm(ks[12], (DEPTH, RV), F32),
        'w_branch_gla': DEEPNORM_BETA * GV ** -0.5 * nrm(ks[13], (DEPTH, GV, D_MODEL), F32),
        'w_branch_ret': DEEPNORM_BETA * RV ** -0.5 * nrm(ks[14], (DEPTH, RV, D_MODEL), F32),
        'w_out': DEEPNORM_BETA * D_MODEL ** -0.5 * nrm(ks[15], (DEPTH, D_MODEL, D_MODEL), F32),
        'ln_g': 1.0 + 0.02 * nrm(ks[16], (DEPTH, D_MODEL), F32),
        'ln_b': 0.02 * nrm(ks[17], (DEPTH, D_MODEL), F32),
    }


def reference(x_prompt, x_sample, state_gla, state_ret, c_prompt, c_sample, w_ada, b_ada, w_in,
              w_lr2, b_lr2, gla_norm_g, ret_norm_g, w_branch_gla, w_branch_ret, w_out, ln_g, ln_b):
    n_prompt, t_prompt = x_prompt.shape[0], x_prompt.shape[1]
    pos_prompt = jnp.arange(t_prompt, dtype=jnp.int32)
    pos_sample = PAST_LEN + jnp.arange(x_sample.shape[1], dtype=jnp.int32)
    y_p, y_s = x_prompt, x_sample
    gla_p, ret_p, gla_s, ret_s = [], [], [], []
    for l in range(DEPTH):
        weights = (w_ada[l], b_ada[l], w_in[l], w_lr2[l], b_lr2[l], gla_norm_g[l], ret_norm_g[l],
                   w_branch_gla[l], w_branch_ret[l], w_out[l], ln_g[l], ln_b[l])
        zero_gla = jnp.zeros((n_prompt, GLA_HEADS, GLA_DK, GLA_DV), x_prompt.dtype)
        zero_ret = jnp.zeros((n_prompt, RET_HEADS, RET_DK, RET_DV), x_prompt.dtype)
        y_p, sg_p, sr_p = _layer(y_p, c_prompt, zero_gla, zero_ret, pos_prompt, *weights)
        y_s, sg_s, sr_s = _layer(y_s, c_sample, state_gla[l], state_ret[l], pos_sample, *weights)
        gla_p.append(sg_p)
        ret_p.append(sr_p)
        gla_s.append(sg_s)
        ret_s.append(sr_s)
    return (y_p, y_s, jnp.stack(gla_p), jnp.stack(ret_p), jnp.stack(gla_s), jnp.stack(ret_s))
```

```python
import math
from contextlib import ExitStack

import numpy as np
import concourse.bass as bass
import concourse.mybir as mybir
from concourse.bass_utils import run_bass_kernel_spmd

F32 = mybir.dt.float32
BF16 = mybir.dt.bfloat16
AF = mybir.ActivationFunctionType
ALU = mybir.AluOpType

NCORES = 8
D = 1024
TP = 2048
NS = 16
TS = 4
NTOK = TP + NS * TS
GQ, GK, GV, GZ, LR, RQ, RK, RV, RZ, MG, MR = 0, 512, 1024, 2048, 3072, 3088, 4112, 5136, 7184, 9232, 10256
DIN = 11280
ALPHA = 2.0 ** 0.25
EPS = 1e-5
PASSES = [list(range(0, 6)), list(range(6, 13)), list(range(13, 16)) + ["s"]]
WSLOT = 8 * 1536


class Buf:
    __slots__ = ("w", "r", "name", "dead")

    def __init__(self, name="", prev=None):
        self.w = None
        self.r = {}
        self.name = name
        self.dead = False
        if prev is not None:
            self.w = prev.w
            self.r = prev.r
            prev.dead = True


class Prog:
    CE = ("tensor", "vector", "scalar", "gpsimd")
    ENG = ("tensor", "vector", "scalar", "gpsimd", "sync")

    def __init__(self, nc, stack):
        self.nc = nc
        self.stack = stack
        self.q = {e: [] for e in self.ENG}
        self.sems = []
        self.cnt = []
        self.esem = {}
        for e in self.CE:
            self.esem[e] = self._newsem("s_" + e)
        self.seen = {e: {} for e in self.ENG}
        self.lanes = {}

    def _newsem(self, name):
        h = self.stack.enter_context(self.nc.semaphore(name))
        self.sems.append(h)
        self.cnt.append(0)
        return len(self.sems) - 1

    def lane(self, name):
        if name not in self.lanes:
            self.lanes[name] = self._newsem("l_" + name)
        return self.lanes[name]

    def _waits(self, eng, R, W):
        need = {}
        for b in list(R) + list(W):
            assert not b.dead, f"use of recycled buffer {b.name}"

        def add(s, v):
            if need.get(s, 0) < v:
                need[s] = v
        for b in R:
            if b.w is not None:
                add(*b.w)
        for b in W:
            if b.w is not None:
                add(*b.w)
            for s, v in b.r.items():
                add(s, v)
        out = []
        seen = self.seen[eng]
        own = self.esem.get(eng) if eng == "tensor" else None
        for s, v in need.items():
            if s == own:
                continue
            if seen.get(s, 0) < v:
                seen[s] = v
                out.append((s, v))
        return out

    def _commit(self, ev, R, W):
        s, v = ev
        for b in R:
            if b.r.get(s, 0) < v:
                b.r[s] = v
        for b in W:
            b.w = ev
            b.r = {}

    def op(self, eng, fn, R=(), W=()):
        waits = self._waits(eng, R, W)
        s = self.esem[eng]
        self.cnt[s] += 1
        ev = (s, self.cnt[s])
        self.q[eng].append((waits, fn, (s, 1), False))
        self._commit(ev, R, W)
        return ev

    def dma(self, eng, lane, fn, R=(), W=(), n=1):
        waits = self._waits(eng, R, W)
        s = self.lane(lane)
        self.cnt[s] += 16 * n
        ev = (s, self.cnt[s])
        self.q[eng].append((waits, fn, (s, 16), True))
        self._commit(ev, R, W)
        return ev

    def seal(self, lane, bufs):
        s = self.lane(lane)
        for b in bufs:
            b.w = (s, self.cnt[s])

    def barrier(self, skip=()):
        sk = {self.lanes[n] for n in skip if n in self.lanes}
        allev = [(s, c) for s, c in enumerate(self.cnt) if c > 0 and s not in sk]
        for e in self.ENG:
            seen = self.seen[e]
            waits = []
            for s, v in allev:
                if seen.get(s, 0) < v:
                    seen[s] = v
                    waits.append((s, v))
            if waits:
                self.q[e].append((waits, None, None, False))

    def emit(self):
        nc = self.nc
        sems = self.sems

        def replay(name, e):
            for waits, fn, inc, is_dma in self.q[name]:
                for s, v in waits:
                    e.wait_ge(sems[s], v)
                if fn is None:
                    continue
                if is_dma:
                    s, amt = inc
                    fn(e, lambda ins, _s=s, _a=amt: ins.then_inc(sems[_s], _a))
                else:
                    ins = fn(e)
                    ins.then_inc(sems[inc[0]], inc[1])

        with nc.Block() as block:
            @block.tensor
            def _(e):
                replay("tensor", e)

            @block.vector
            def _(e):
                replay("vector", e)

            @block.scalar
            def _(e):
                replay("scalar", e)

            @block.gpsimd
            def _(e):
                replay("gpsimd", e)

            @block.sync
            def _(e):
                replay("sync", e)


class Ring:
    def __init__(self, nc, stack, name, shape, dt, n):
        self.t = [stack.enter_context(nc.sbuf_tensor(f"{name}{i}", list(shape), dt)) for i in range(n)]
        self.b = [Buf(f"{name}{i}") for i in range(n)]
        self.i = 0
        self.n = n

    def next(self):
        i = self.i
        self.i = (i + 1) % self.n
        self.b[i] = Buf(self.b[i].name, prev=self.b[i])
        return self.t[i], self.b[i]


class TileInfo:
    def __init__(self, key, col):
        self.key = key
        self.samp = (key == "s")
        self.kind = 1 if self.samp else 0
        self.nt = 64 if self.samp else 128
        self.tok0 = TP if self.samp else key * 128
        self.col = col


def build_program(debug=None):
    nc = bass.Bass("TRN2", target_bir_lowering=False)
    din = lambda n, s: nc.dram_tensor(n, list(s), F32, kind="ExternalInput").ap()
    dout = lambda n, s: nc.dram_tensor(n, list(s), F32, kind="ExternalOutput").ap()
    x_p = din("x_p", [TP, D])
    x_s = din("x_s", [64, D])
    c_all = din("c_all", [192, D])
    sg_in = din("sg_in", [NS, 4, 128, 256])
    sr_in = din("sr_in", [NS, 4, 256, 512])
    w_ada = din("w_ada", [D, 3 * D])
    b_ada = din("b_ada", [3 * D])
    w_in = din("w_in", [D, DIN])
    w_lr2 = din("w_lr2", [16, 512])
    b_lr2 = din("b_lr2", [512])
    gng = din("gla_norm_g", [1024])
    rng_ = din("ret_norm_g", [2048])
    wbg = din("w_branch_gla", [1024, D])
    wbr = din("w_branch_ret", [2048, D])
    w_out = din("w_out", [D, D])
    ln_g = din("ln_g", [D])
    ln_b = din("ln_b", [D])
    rope = din("rope", [128, 2, NTOK])
    rmask = din("rmask", [128, 4, 2, 128])
    rdread = din("rdread", [128, 4, 2, 128])
    rdwrite = din("rdwrite", [128, 8])
    tri = din("tri", [128, 2, 272])
    colmask = din("colmask", [128, 16, 64])
    rowmask = din("rowmask", [128, 16])
    ident = din("ident", [128, 128])
    y_p = dout("y_p", [TP, D])
    y_s = dout("y_s", [64, D])
    sgp = dout("sgp", [4, 128, 256])
    srp = dout("srp", [4, 256, 512])
    sgs = dout("sgs", [NS, 4, 128, 256])
    srs = dout("srs", [NS, 4, 256, 512])
    dbg_outs = {}

    w_in_v = w_in.rearrange("(kc p) n -> p kc n", p=128)
    w_ada_v = w_ada.rearrange("(kc p) n -> p kc n", p=128)
    wbg_v = wbg.rearrange("(kc p) n -> p kc n", p=128)
    wbr_v = wbr.rearrange("(kc p) n -> p kc n", p=128)
    w_out_v = w_out.rearrange("(kc p) n -> p kc n", p=128)

    lg = [math.log1p(-2.0 ** (-5 - h)) for h in range(4)]
    dchunk = [[math.exp(128 * lg[h]), math.exp(4 * lg[h])] for h in range(4)]

    with ExitStack() as stack:
        P = Prog(nc, stack)

        def sb(name, shape, dt, st=stack):
            return st.enter_context(nc.sbuf_tensor(name, list(shape), dt))

        banks = [stack.enter_context(nc.psum_tensor(f"pb{i}", [128, 512], F32)) for i in range(8)]
        banks_bf = [b.bitcast(BF16) for b in banks]
        bbuf = [Buf(f"pb{i}") for i in range(8)]
        ring_i = [0]

        main_banks = [[0, 1, 2, 3, 4]]
        oring_i = [0]

        def psum():
            mb = main_banks[0]
            i = mb[ring_i[0] % len(mb)]
            ring_i[0] += 1
            bbuf[i] = Buf(bbuf[i].name, prev=bbuf[i])
            return i, bbuf[i]

        ring7_i = [0]

        def psum7():
            i = ring7_i[0]
            ring7_i[0] = (i + 1) % 8
            bbuf[i] = Buf(bbuf[i].name, prev=bbuf[i])
            return i, bbuf[i]

        def psum_o():
            i = 5 + oring_i[0]
            oring_i[0] = (oring_i[0] + 1) % 2
            bbuf[i] = Buf(bbuf[i].name, prev=bbuf[i])
            return i, bbuf[i]

        def dbg(name, ap, buf, shape, dt=F32):
            if debug is None or name not in debug:
                return
            d = nc.dram_tensor("dbg_" + name, list(shape), dt, kind="ExternalOutput").ap()
            dbg_outs[name] = d
            P.dma("sync", "dbg", lambda e, inc: inc(e.dma_start(out=d, in_=ap)), R=[buf])
            P.seal("dbg", [])

        def mm(out_ap, pairs, R, W):
            def fn(e):
                n = len(pairs)
                for i, (l, r) in enumerate(pairs):
                    ins = e.matmul(out=out_ap, lhsT=l, rhs=r, start=(i == 0), stop=(i == n - 1))
                return ins
            P.op("tensor", fn, R=R, W=W)

        def tr(specs, R, W):
            def fn(e):
                for o, i, idn in specs:
                    ins = e.transpose(out=o, in_=i, identity=idn)
                return ins
            P.op("tensor", fn, R=R, W=W)

        def V(fn, R, W):
            P.op("vector", fn, R=R, W=W)

        def A(fn, R, W):
            P.op("scalar", fn, R=R, W=W)

        def G(fn, R, W):
            P.op("gpsimd", fn, R=R, W=W)

        ident_f = sb("ident_f", [128, 128], F32)
        ident_b = sb("ident_b", [128, 128], BF16)
        tri_sb = sb("tri_sb", [128, 2, 272], F32)
        cmask_sb = sb("cmask_sb", [128, 16, 64], BF16)
        rowm_sb = sb("rowm_sb", [128, 16], F32)
        rdw_sb = sb("rdw_sb", [128, 8], F32)
        blr_bc = sb("blr_bc", [128, 512], F32)
        wlr2_b = sb("wlr2_b", [16, 512], BF16)
        gnT = sb("gnT", [128, 24], F32)
        badaT = sb("badaT", [128, 24], F32)
        mhalf = sb("mhalf", [128, 1], F32)
        adaT = sb("adaT", [128, 16, 65], F32)
        cT = sb("cT", [128, 8, 192], BF16)
        S_gla = sb("S_gla", [128, 4, 256], F32)
        S_ret = sb("S_ret", [128, 4, 2, 512], F32)
        bS_gla = [Buf(f"Sg{h}") for h in range(4)]
        bS_ret = [Buf(f"Sr{h}") for h in range(4)]
        wslot = [sb(f"wslot{i}", [128, WSLOT], BF16) for i in range(2)]
        bw = [Buf("w0"), Buf("w1")]
        wi = [0]

        def next_w():
            i = wi[0]
            wi[0] = 1 - i
            return i

        xcnt = [0]
        bconst = Buf("const")
        bada = Buf("ada")
        bcT = Buf("cT")

        def ld(dst, src, **kw):
            P.dma("sync", "c", lambda e, inc: inc(e.dma_start(out=dst, in_=src, **kw)), W=[bconst])

        ld(ident_f[:, :], ident[:, :])
        ld(tri_sb[:, :, :], tri[:, :, :])
        ld(rowm_sb[:, :], rowmask[:, :])
        ld(rdw_sb[:, :], rdwrite[:, :])
        ld(blr_bc[:, :], b_lr2.partition_broadcast(128))
        ld(gnT[:, 0:8], gng.rearrange("(m p) -> p m", p=128), allow_slow_non_contiguous=True)
        ld(gnT[:, 8:24], rng_.rearrange("(m p) -> p m", p=128), allow_slow_non_contiguous=True)
        ld(badaT[:, :], b_ada.rearrange("(m p) -> p m", p=128), allow_slow_non_contiguous=True)
        P.dma("gpsimd", "cw", lambda e, inc: inc(e.dma_start(out=ident_b[:, :], in_=ident[:, :])), W=[bconst])
        P.dma("gpsimd", "cw", lambda e, inc: inc(e.dma_start(out=wlr2_b[:, :], in_=w_lr2[:, :])), W=[bconst])
        P.dma("gpsimd", "cw", lambda e, inc: inc(e.dma_start(out=cmask_sb[:, :, :], in_=colmask[:, :, :])), W=[bconst])
        P.barrier()
        G(lambda e: e.memset(mhalf[:, :], -0.5), [], [bconst])
        V(lambda e: e.tensor_scalar(out=gnT[:, 0:8], in0=gnT[:, 0:8], scalar1=0.5, scalar2=None, op0=ALU.mult), [bconst], [bconst])

        with ExitStack() as st0:
            cA = sb("cA", [64, D], F32, st0)
            cB = sb("cB", [128, D], F32, st0)
            bc = Buf("c")
            P.dma("sync", "c2", lambda e, inc: inc(e.dma_start(out=cA[:, :], in_=c_all[0:64, :])), W=[bc])
            P.dma("sync", "c2", lambda e, inc: inc(e.dma_start(out=cB[:, :], in_=c_all[64:192, :])), W=[bc])
            P.seal("c2", [bc])
            for k2 in range(4):
                bi, bb = psum()
                specs = []
                for j in range(2):
                    kc = 2 * k2 + j
                    specs.append((banks[bi][:, j * 192:j * 192 + 64], cA[:, kc * 128:(kc + 1) * 128], ident_f[0:64, 0:64]))
                    specs.append((banks[bi][:, j * 192 + 64:j * 192 + 192], cB[:, kc * 128:(kc + 1) * 128], ident_f[:, :]))
                tr(specs, [bc, bconst], [bb])
                A(lambda e, bi=bi, k2=k2: e.activation(
                    out=cT[:, 2 * k2:2 * k2 + 2, :], in_=banks[bi][:, 0:384].rearrange("p (j n) -> p j n", j=2),
                    func=AF.Copy), [bb], [bcT])
            for j in range(4):
                wsl = next_w()
                wv = wslot[wsl][:, 0:4096].rearrange("p (k n) -> p k n", k=8)
                P.dma("gpsimd", f"w{wsl}", lambda e, inc, wv=wv, j=j: inc(e.dma_start(
                    out=wv, in_=w_ada_v[:, :, j * 512:(j + 1) * 512])), W=[bw[wsl]])
                for mm_ in range(4):
                    m = 4 * j + mm_
                    bi, bb = psum()
                    mm(banks[bi][:, 0:65], [(wv[:, kc, mm_ * 128:(mm_ + 1) * 128], cT[:, kc, 0:65]) for kc in range(8)],
                       [bw[wsl], bcT], [bb])
                    V(lambda e, bi=bi, m=m: e.tensor_scalar(
                        out=adaT[:, m, :], in0=banks[bi][:, 0:65], scalar1=badaT[:, m:m + 1],
                        scalar2=(1.0 if m >= 8 else 0.0), op0=ALU.add, op1=ALU.add), [bb, bconst], [bada])
            P.barrier()

        def do_pass(pi, pkeys):
            tiles = []
            col = 0
            for k in pkeys:
                t = TileInfo(k, col)
                tiles.append(t)
                col += t.nt
            NP = col
            groups = []
            cur = []
            for t in tiles:
                if t.samp:
                    if cur:
                        groups.append(cur)
                    groups.append([t])
                    cur = []
                else:
                    cur.append(t)
                    if len(cur) == 4:
                        groups.append(cur)
                        cur = []
            if cur:
                groups.append(cur)
            has_s = any(t.samp for t in tiles)
            main_banks[0] = [0, 1, 2, 3, 4] if has_s else [0, 1, 2, 3, 4, 7]
            ptiles = [t for t in tiles if not t.samp]
            first_is_zero = (ptiles[0].key == 0)
            last_pass = (ptiles[-1].key == 15)

            def issue_unit_w(u):
                gla = u < 4
                h = u % 4
                wsl = next_w()
                if gla:
                    ncol = 784
                    segs = [(0, GQ + h * 128, 128), (128, GK + h * 128, 128), (256, GV + h * 256, 256),
                            (512, GZ + h * 256, 256), (768, LR, 16)]
                else:
                    ncol = 1536
                    segs = [(0, RQ + h * 256, 256), (256, RK + h * 256, 256), (512, RV + h * 512, 512),
                            (1024, RZ + h * 512, 512)]
                wv = wslot[wsl][:, 0:8 * ncol].rearrange("p (k n) -> p k n", k=8)

                def wload(e, inc):
                    for d0, s0, n in segs:
                        inc(e.dma_start(out=wv[:, :, d0:d0 + n], in_=w_in_v[:, :, s0:s0 + n]))
                P.dma("gpsimd", f"w{wsl}", wload, W=[bw[wsl]], n=len(segs))
                return wsl, wv
            pre_w = {0: issue_unit_w(0), 1: issue_unit_w(1)}

            with ExitStack() as stp:
                hT = sb(f"hT{pi}", [128, 8, NP], BF16, stp)
                oT = sb(f"oT{pi}", [128, 24, NP], BF16, stp)
                bhT = Buf("hT")
                boT = [Buf(f"oT{u}") for u in range(8)]

                with ExitStack() as stq:
                    tmpr = Ring(nc, stq, f"htmp{pi}", [128, 64], F32, 2) if has_s else None
                    xring = Ring(nc, stq, f"xtq{pi}", [128, D], F32, 4)
                    xqc = [0]

                    def do_pt(t):
                        xt, bx = xring.next()
                        xsrc = x_s[0:64, :] if t.samp else x_p[t.tok0:t.tok0 + 128, :]
                        xi = xqc[0] % 4
                        xqc[0] += 1
                        nt = t.nt
                        P.dma("sync", f"xq{xi}", lambda e, inc: inc(e.dma_start(out=xt[0:nt, :], in_=xsrc)), W=[bx])
                        for half in range(2):
                            bi, bb = psum()
                            tr([(banks[bi][:, j * 128:j * 128 + nt], xt[0:nt, (half * 4 + j) * 128:(half * 4 + j + 1) * 128],
                                 ident_f[0:nt, 0:nt]) for j in range(4)], [bx, bconst], [bb])
                            for j in range(4):
                                kc = half * 4 + j
                                src = banks[bi][:, j * 128:j * 128 + nt]
                                dst = hT[:, kc, t.col:t.col + nt]
                                if not t.samp:
                                    if j % 2 == 0:
                                        A(lambda e, src=src, dst=dst, kc=kc: e.activation(
                                            out=dst, in_=src, func=AF.Identity, scale=adaT[:, 8 + kc, 64:65],
                                            bias=adaT[:, kc, 64:65]), [bb, bada], [bhT])
                                    else:
                                        V(lambda e, src=src, dst=dst, kc=kc: e.tensor_scalar(
                                            out=dst, in0=src, scalar1=adaT[:, 8 + kc, 64:65], scalar2=adaT[:, kc, 64:65],
                                            op0=ALU.mult, op1=ALU.add), [bb, bada], [bhT])
                                else:
                                    tm, btm = tmpr.next()
                                    V(lambda e, src=src, tm=tm, kc=kc: e.tensor_tensor(
                                        out=tm[:, :], in0=src, in1=adaT[:, 8 + kc, 0:64], op=ALU.mult), [bb, bada], [btm])
                                    V(lambda e, dst=dst, tm=tm, kc=kc: e.tensor_tensor(
                                        out=dst, in0=tm[:, :], in1=adaT[:, kc, 0:64], op=ALU.add), [btm, bada], [bhT])
                    for t in tiles:
                        do_pt(t)
                    P.barrier(skip=("w0", "w1"))

                with ExitStack() as stu:
                    rope_sb = sb(f"rope{pi}", [128, 2, NP], F32, stu)
                    brope = Buf("rope")
                    for cs in range(2):
                        for t in tiles:
                            P.dma("sync", "rope", lambda e, inc, cs=cs, t=t: inc(e.dma_start(
                                out=rope_sb[:, cs, t.col:t.col + t.nt], in_=rope[:, cs, t.tok0:t.tok0 + t.nt])), W=[brope])
                    P.seal("rope", [brope])
                    f32g = Ring(nc, stu, f"f32g{pi}", [128, 512], F32, 4)
                    bfg = Ring(nc, stu, f"bfg{pi}", [128, 2, 512], BF16, 6)
                    lrT_r = Ring(nc, stu, f"lrT{pi}", [16, 512], BF16, 2)
                    sm128 = Ring(nc, stu, f"sm{pi}", [128, 128], F32, 4)
                    vbf_r = Ring(nc, stu, f"vbf{pi}", [128, 512], BF16, 2)
                    th_r = Ring(nc, stu, f"th{pi}", [128, 512], F32, 1)
                    zs_r = Ring(nc, stu, f"zs{pi}", [128, 512], F32, 3)
                    am_r = Ring(nc, stu, f"am{pi}", [128, 128], BF16, 2)
                    kw_r = Ring(nc, stu, f"kw{pi}", [128, 256], BF16, 2)
                    sbf_r = Ring(nc, stu, f"sbf{pi}", [128, 2, 512], BF16, 3)
                    on_r = Ring(nc, stu, f"on{pi}", [128, 512], F32, 3)
                    t512 = Ring(nc, stu, f"t512{pi}", [128, 512], F32, 2)
                    st_r = Ring(nc, stu, f"st{pi}", [128, 16], F32, 4)
                    dsd_r = Ring(nc, stu, f"dsd{pi}", [128, 16], F32, 8)
                    mk_r = Ring(nc, stu, f"mk{pi}", [128, 2, 2, 128], F32, 2)
                    if has_s:
                        sin_r = Ring(nc, stu, f"sin{pi}", [128, 2, 512], F32, 4)
                        qm_r = Ring(nc, stu, f"qm{pi}", [128, 2, 64], BF16, 2)
                        sinb_r = Ring(nc, stu, f"sinb{pi}", [128, 2, 512], BF16, 2)
                        km_r = Ring(nc, stu, f"km{pi}", [64, 256], BF16, 2)
                        oint = sb(f"oint{pi}", [64, 512], F32, stu)
                        boint = Buf("oint")
                    lane_ctr = {"sin": 0, "sout": 0, "mk": 0}
                    if has_s:
                        bfg_s = Ring(nc, stu, f"bfgs{pi}", [128, 2, 64], BF16, 6)
                        dsd_s = Ring(nc, stu, f"dsds{pi}", [128, 16], F32, 2)
                        vbf_s = Ring(nc, stu, f"vbfs{pi}", [64, 512], BF16, 2)
                        zs_s = Ring(nc, stu, f"zss{pi}", [64, 512], F32, 2)
                        am_s = Ring(nc, stu, f"ams{pi}", [64, 64], BF16, 2)
                        kw_s = Ring(nc, stu, f"kws{pi}", [64, 256], BF16, 2)
                    seq_per_step = [NS]

                    def do_unit(u):
                        gla = u < 4
                        h = u % 4
                        dkc = 1 if gla else 2
                        dv = 256 if gla else 512
                        ochunk0 = (h * 2) if gla else (8 + h * 4)
                        nch = dv // 128
                        wsl, wv = pre_w.pop(u) if u in pre_w else issue_unit_w(u)
                        bwu = bw[wsl]
                        mk = bmk = None
                        if not gla:
                            mk, bmk = mk_r.next()
                            ml = lane_ctr["mk"] % 2
                            lane_ctr["mk"] += 1

                            def mkload(e, inc):
                                inc(e.dma_start(out=mk[:, 0, :, :], in_=rmask[:, h, :, :]))
                                inc(e.dma_start(out=mk[:, 1, :, :], in_=rdread[:, h, :, :]))
                            P.dma("sync", f"mk{ml}", mkload, W=[bmk], n=2)
                        bS = bS_gla[h] if gla else bS_ret[h]
                        us = {"valid": not first_is_zero, "sbf": None, "bsbf": None}

                        def init_sbf():
                            sbf0, bsbf0 = sbf_r.next()
                            us["sbf"], us["bsbf"] = sbf0, bsbf0
                            if gla:
                                A(lambda e: e.activation(out=sbf0[:, 0, 0:256], in_=S_gla[:, h, :], func=AF.Copy), [bS], [bsbf0])
                            else:
                                A(lambda e: e.activation(out=sbf0[:, :, :], in_=S_ret[:, h, :, :], func=AF.Copy), [bS], [bsbf0])

                        def do_group(grp, staged=False):
                            g0 = grp[0].col
                            NG = sum(t.nt for t in grp)
                            gs = slice(g0, g0 + NG)
                            dsd_of = {}
                            lrT = blrT = Ep = bEp = En = bEn = Er = bEr = None
                            qA = bqA = kA = bkA = qS = bqS = kW = bkW = None
                            stages = []
                            if gla:
                                def st0():
                                    nonlocal lrT, blrT, Ep, bEp, En, bEn, Er, bEr
                                    bi0, bb0 = psum()
                                    mm(banks[bi0][0:16, 0:NG], [(wv[:, kc, 768:784], hT[:, kc, gs]) for kc in range(8)], [bwu, bhT], [bb0])
                                    lrT, blrT = lrT_r.next()
                                    A(lambda e: e.activation(out=lrT[:, 0:NG], in_=banks[bi0][0:16, 0:NG], func=AF.Copy), [bb0], [blrT])
                                    Ep, bEp = f32g.next()
                                    En, bEn = f32g.next()
                                    Er, bEr = f32g.next()

                                stages.append((st0, False))
                                def do_decay(t):
                                    lc = t.col - g0
                                    nt = t.nt
                                    bi, bb = psum()
                                    mm(banks[bi][0:nt, 0:128], [(lrT[:, lc:lc + nt], wlr2_b[:, h * 128:(h + 1) * 128])], [blrT, bconst], [bb])
                                    zb, bzb = sm128.next()
                                    V(lambda e: e.tensor_tensor(
                                        out=zb[0:nt, :], in0=banks[bi][0:nt, 0:128], in1=blr_bc[0:nt, h * 128:(h + 1) * 128], op=ALU.add),
                                      [bb, bconst], [bzb])
                                    A(lambda e: e.activation(out=zb[0:nt, :], in_=zb[0:nt, :], func=AF.Exp, scale=-1.0), [bzb], [bzb])
                                    lsb, blsb = sm128.next()
                                    A(lambda e: e.activation(out=lsb[0:nt, :], in_=zb[0:nt, :], func=AF.Ln, bias=1.0), [bzb], [blsb])
                                    bi2, bb2 = psum()
                                    mm(banks[bi2][:, 0:272], [(lsb[0:nt, :], tri_sb[0:nt, t.kind, :])], [blsb, bconst], [bb2])
                                    A(lambda e: e.activation(out=Ep[:, lc:lc + nt], in_=banks[bi2][:, 0:nt], func=AF.Exp, scale=-1.0 / 16), [bb2], [bEp])
                                    A(lambda e: e.activation(out=En[:, lc:lc + nt], in_=banks[bi2][:, 0:nt], func=AF.Exp, scale=1.0 / 16), [bb2], [bEn])
                                    A(lambda e: e.activation(out=Er[:, lc:lc + nt], in_=banks[bi2][:, 128:128 + nt], func=AF.Exp, scale=-1.0 / 16), [bb2], [bEr])
                                    dsd, bdsd = (dsd_s if t.samp else dsd_r).next()
                                    A(lambda e: e.activation(out=dsd[:, :], in_=banks[bi2][:, 256:272], func=AF.Exp, scale=-1.0 / 16), [bb2], [bdsd])
                                    dsd_of[t.key] = (dsd, bdsd)
                                for t in grp:
                                    stages.append((lambda t=t: do_decay(t), False))

                                def stq():
                                    nonlocal qA, bqA, kA, bkA, qS, bqS, kW, bkW
                                    biq, bbq = psum()
                                    mm(banks[biq][:, 0:NG], [(wv[:, kc, 0:128], hT[:, kc, gs]) for kc in range(8)], [bwu, bhT], [bbq])
                                    bik, bbk = psum()
                                    mm(banks[bik][:, 0:NG], [(wv[:, kc, 128:256], hT[:, kc, gs]) for kc in range(8)], [bwu, bhT], [bbk])
                                    bfgx = bfg_s if grp[0].samp else bfg
                                    qd, bqd = bfgx.next()
                                    kd, bkd = bfgx.next()
                                    kwT, bkwT = bfgx.next()
                                    V(lambda e: e.scalar_tensor_tensor(
                                        out=qd[:, 0, 0:NG], in0=banks[biq][:, 0:NG], scalar=128.0 ** -0.5, in1=Ep[:, 0:NG],
                                        op0=ALU.mult, op1=ALU.mult), [bbq, bEp], [bqd])
                                    V(lambda e: e.tensor_tensor(out=kd[:, 0, 0:NG], in0=banks[bik][:, 0:NG], in1=En[:, 0:NG], op=ALU.mult), [bbk, bEn], [bkd])
                                    V(lambda e: e.tensor_tensor(out=kwT[:, 0, 0:NG], in0=banks[bik][:, 0:NG], in1=Er[:, 0:NG], op=ALU.mult), [bbk, bEr], [bkwT])
                                    qA, bqA, kA, bkA, qS, bqS, kW, bkW = qd, bqd, kd, bkd, qd, bqd, kwT, bkwT
                                stages.append((stq, True))
                            else:
                                def str_():
                                    nonlocal qA, bqA, kA, bkA, qS, bqS, kW, bkW
                                    cosg = rope_sb[:, 0, gs]
                                    sing = rope_sb[:, 1, gs]

                                    def do_rot(which):
                                        b0, bb0 = psum()
                                        mm(banks[b0][:, 0:NG], [(wv[:, kc, which * 256:which * 256 + 128], hT[:, kc, gs]) for kc in range(8)],
                                           [bwu, bhT], [bb0])
                                        b1, bb1 = psum()
                                        mm(banks[b1][:, 0:NG], [(wv[:, kc, which * 256 + 128:which * 256 + 256], hT[:, kc, gs]) for kc in range(8)],
                                           [bwu, bhT], [bb1])
                                        t1, bt1 = f32g.next()
                                        t2, bt2 = f32g.next()
                                        t3, bt3 = f32g.next()
                                        t4, bt4 = f32g.next()
                                        V(lambda e: e.tensor_tensor(out=t1[:, 0:NG], in0=banks[b0][:, 0:NG], in1=cosg, op=ALU.mult), [bb0, brope], [bt1])
                                        V(lambda e: e.tensor_tensor(out=t2[:, 0:NG], in0=banks[b1][:, 0:NG], in1=sing, op=ALU.mult), [bb1, brope], [bt2])
                                        V(lambda e: e.tensor_tensor(out=t3[:, 0:NG], in0=banks[b0][:, 0:NG], in1=sing, op=ALU.mult), [bb0, brope], [bt3])
                                        V(lambda e: e.tensor_tensor(out=t4[:, 0:NG], in0=banks[b1][:, 0:NG], in1=cosg, op=ALU.mult), [bb1, brope], [bt4])
                                        rot, brot = (bfg_s if grp[0].samp else bfg).next()
                                        G(lambda e: e.tensor_tensor(out=rot[:, 0, 0:NG], in0=t1[:, 0:NG], in1=t2[:, 0:NG], op=ALU.subtract), [bt1, bt2], [brot])
                                        G(lambda e: e.tensor_tensor(out=rot[:, 1, 0:NG], in0=t3[:, 0:NG], in1=t4[:, 0:NG], op=ALU.add), [bt3, bt4], [brot])
                                        return rot, brot
                                    qr, bqr = do_rot(0)
                                    kr, bkr = do_rot(1)
                                    qrd, bqrd = (bfg_s if grp[0].samp else bfg).next()
                                    for t in grp:
                                        V(lambda e, t=t, lc=t.col - g0: e.tensor_tensor(
                                            out=qrd[:, :, lc:lc + t.nt], in0=qr[:, :, lc:lc + t.nt],
                                            in1=mk[:, 1, t.kind:t.kind + 1, 0:t.nt].to_broadcast([128, 2, t.nt]), op=ALU.mult), [bqr, bmk], [bqrd])
                                    qA, bqA, kA, bkA, qS, bqS, kW, bkW = qr, bqr, kr, bkr, qrd, bqrd, kr, bkr
                                stages.append((str_, True))

                            def do_tile(t):
                                lc = t.col - g0
                                nt = t.nt
                                tsl = slice(t.col, t.col + nt)
                                vbf, bvbf = (vbf_s if t.samp else vbf_r).next()
                                th, bth = th_r.next()
                                zs, bzs = (zs_s if t.samp else zs_r).next()
                                if gla:
                                    biv, bbv = psum()
                                    mm(banks[biv][0:nt, 0:512], [(hT[:, kc, tsl], wv[:, kc, 256:768]) for kc in range(8)], [bhT, bwu], [bbv])
                                    vsrc = banks[biv][0:nt, 0:256]
                                    zsrc = banks[biv][0:nt, 256:512]
                                    bbz = bbv
                                else:
                                    biv, bbv = psum()
                                    mm(banks[biv][0:nt, 0:512], [(hT[:, kc, tsl], wv[:, kc, 512:1024]) for kc in range(8)], [bhT, bwu], [bbv])
                                    vsrc = banks[biv][0:nt, 0:512]
                                    biz, bbz = psum()
                                    mm(banks[biz][0:nt, 0:512], [(hT[:, kc, tsl], wv[:, kc, 1024:1536]) for kc in range(8)], [bhT, bwu], [bbz])
                                    zsrc = banks[biz][0:nt, 0:512]
                                A(lambda e: e.activation(out=vbf[0:nt, 0:dv], in_=vsrc, func=AF.Copy), [bbv], [bvbf])
                                if gla:
                                    A(lambda e: e.activation(out=th[0:nt, 0:dv], in_=zsrc, func=AF.Tanh, scale=0.5), [bbz], [bth])
                                    V(lambda e: e.scalar_tensor_tensor(
                                        out=zs[0:nt, 0:dv], in0=th[0:nt, 0:dv], scalar=1.0, in1=zsrc, op0=ALU.add, op1=ALU.mult), [bth, bbz], [bzs])
                                else:
                                    A(lambda e: e.activation(out=zs[0:nt, 0:dv], in_=zsrc, func=AF.Silu), [bbz], [bzs])
                                bia, bba = psum()
                                mm(banks[bia][0:nt, 0:nt], [(kA[:, c, lc:lc + nt], qA[:, c, lc:lc + nt]) for c in range(dkc)], [bkA, bqA], [bba])
                                am, bam = (am_s if t.samp else am_r).next()
                                if gla:
                                    msk = tri_sb[0:nt, t.kind, 0:nt]
                                    bmsk = bconst
                                else:
                                    msk = mk[0:nt, 0, t.kind, 0:nt]
                                    bmsk = bmk
                                V(lambda e: e.tensor_tensor(out=am[0:nt, 0:nt], in0=banks[bia][0:nt, 0:nt], in1=msk, op=ALU.mult), [bba, bmsk], [bam])
                                osl = {}

                                def issue_o():
                                    bio, bbo = psum_o()
                                    pairs = [(am[0:nt, 0:nt], vbf[0:nt, 0:dv])]
                                    Rl = [bam, bvbf]
                                    if (not t.samp) and us["valid"]:
                                        if us["sbf"] is None:
                                            init_sbf()
                                        sbfc = us["sbf"]
                                        pairs += [(qS[:, c, lc:lc + nt], sbfc[:, c, 0:dv]) for c in range(dkc)]
                                        Rl += [bqS, us["bsbf"]]
                                    mm(banks[bio][0:nt, 0:dv], pairs, Rl, [bbo])
                                    osl["bio"], osl["bbo"] = bio, bbo
                                bit, bbt = psum()
                                tr([(banks_bf[bit][0:nt, c * 128:(c + 1) * 128], kW[:, c, lc:lc + nt], ident_b[:, :]) for c in range(dkc)],
                                   [bkW, bconst], [bbt])
                                kw, bkw = (kw_s if t.samp else kw_r).next()
                                if gla:
                                    A(lambda e: e.activation(out=kw[0:nt, 0:128], in_=banks_bf[bit][0:nt, 0:128], func=AF.Copy), [bbt], [bkw])
                                else:
                                    A(lambda e: e.activation(
                                        out=kw[0:nt, 0:256], in_=banks_bf[bit][0:nt, 0:256], func=AF.Identity,
                                        scale=rdw_sb[0:nt, 2 * h + t.kind:2 * h + t.kind + 1]), [bbt, bconst], [bkw])
                                yield
                                if not t.samp:
                                    issue_o()
                                    o_src = banks[osl["bio"]][0:nt, 0:dv]
                                    bo_src = osl["bbo"]
                                    nsbf, bnsbf = sbf_r.next()
                                    valid = us["valid"]

                                    def do_c(c):
                                        bi, bb = psum()
                                        mm(banks[bi][:, 0:dv], [(kw[0:nt, c * 128:(c + 1) * 128], vbf[0:nt, 0:dv])], [bkw, bvbf], [bb])
                                        Sd = S_gla[:, h, :] if gla else S_ret[:, h, c, :]
                                        if not valid:
                                            V(lambda e: e.tensor_copy(out=Sd, in_=banks[bi][:, 0:dv]), [bb], [bS])
                                        elif gla:
                                            dsd, bdsd = dsd_of[t.key]
                                            V(lambda e: e.scalar_tensor_tensor(
                                                out=Sd, in0=Sd, scalar=dsd[:, 0:1], in1=banks[bi][:, 0:dv], op0=ALU.mult, op1=ALU.add),
                                              [bb, bdsd, bS], [bS])
                                        else:
                                            V(lambda e: e.scalar_tensor_tensor(
                                                out=Sd, in0=Sd, scalar=dchunk[h][0], in1=banks[bi][:, 0:dv], op0=ALU.mult, op1=ALU.add),
                                              [bb, bS], [bS])
                                        A(lambda e: e.activation(out=nsbf[:, c, 0:dv], in_=Sd, func=AF.Copy), [bS], [bnsbf])
                                    for c in range(dkc):
                                        do_c(c)
                                    us["sbf"], us["bsbf"] = nsbf, bnsbf
                                    us["valid"] = True
                                    if last_pass and t.key == 15:
                                        if gla:
                                            P.dma("sync", "stp", lambda e, inc: inc(e.dma_start(out=sgp[h, :, :], in_=S_gla[:, h, :])), R=[bS])
                                        else:
                                            P.dma("sync", "stp", lambda e, inc: inc(e.dma_start(
                                                out=srp[h, :, :].rearrange("(c p) v -> p c v", p=128), in_=S_ret[:, h, :, :])), R=[bS])
                                else:
                                    loads = {}
                                    lanes_of = {}
                                    bbuf[7] = Buf("pb7", prev=bbuf[7])
                                    b7 = bbuf[7]

                                    def issue_load(i):
                                        if gla and i % 4 != 0:
                                            loads[i] = loads[i - 1]
                                            lanes_of[i] = lanes_of[i - 1]
                                            return
                                        sin_, bsin = sin_r.next()
                                        li = lane_ctr["sin"] % 4
                                        lane_ctr["sin"] += 1
                                        lanes_of[i] = li
                                        if gla:
                                            P.dma("sync", f"sin{li}", lambda e, inc: inc(e.dma_start(
                                                out=sin_[:, :, :].rearrange("p c (s v) -> p (c s) v", v=256),
                                                in_=sg_in[i:i + 4, h, :, :].rearrange("s d v -> d s v"))), W=[bsin])
                                        else:
                                            P.dma("sync", f"sin{li}", lambda e, inc: inc(e.dma_start(
                                                out=sin_[:, :, :], in_=sr_in[i, h, :, :].rearrange("(c p) v -> p c v", p=128))), W=[bsin])
                                        loads[i] = (sin_, bsin)
                                    if gla:
                                        for i0 in range(NS):
                                            issue_load(i0)
                                    else:
                                        issue_load(0)
                                        issue_load(1)
                                        issue_load(2)

                                    def st_ap(sin_, i):
                                        if gla:
                                            return sin_[:, :, :].rearrange("p c (s v) -> p (c s) v", v=256)[:, i % 4, :]
                                        return None

                                    preps = {}

                                    def prep_seq(i):
                                        sin_, bsin = loads[i]
                                        qm, bqm = qm_r.next()
                                        V(lambda e: e.tensor_tensor(
                                            out=qm[:, 0:dkc, :], in0=qS[:, 0:dkc, lc:lc + 64],
                                            in1=cmask_sb[:, i:i + 1, :].to_broadcast([128, dkc, 64]), op=ALU.mult), [bqS, bconst], [bqm])
                                        sinb, bsinb = sinb_r.next()
                                        if gla:
                                            A(lambda e: e.activation(out=sinb[:, 0, 0:256], in_=st_ap(sin_, i), func=AF.Copy), [bsin], [bsinb])
                                        else:
                                            A(lambda e: e.activation(out=sinb[:, 0:dkc, 0:dv], in_=sin_[:, 0:dkc, 0:dv], func=AF.Copy), [bsin], [bsinb])
                                        km, bkm = km_r.next()
                                        V(lambda e: e.tensor_scalar(
                                            out=km[:, 0:dkc * 128], in0=kw[0:64, 0:dkc * 128], scalar1=rowm_sb[0:64, i:i + 1], scalar2=None,
                                            op0=ALU.mult), [bkw, bconst], [bkm])
                                        preps[i] = (qm, bqm, sinb, bsinb, km, bkm)
                                    prep_seq(0)

                                    def do_seq(i):
                                        sin_, bsin = loads[i]
                                        if i + 1 < NS:
                                            prep_seq(i + 1)
                                        qm, bqm, sinb, bsinb, km, bkm = preps.pop(i)
                                        mm_pairs = [(qm[:, c, :], sinb[:, c, 0:dv]) for c in range(dkc)]

                                        def fn_oi(e):
                                            for c, (l, r) in enumerate(mm_pairs):
                                                ins = e.matmul(out=banks[7][0:64, 0:dv], lhsT=l, rhs=r,
                                                               start=(i == 0 and c == 0), stop=(i == NS - 1 and c == dkc - 1))
                                            return ins
                                        P.op("tensor", fn_oi, R=[bqm, bsinb], W=[b7])
                                        def do_sc(c):
                                            bi, bb = psum()
                                            mm(banks[bi][:, 0:dv], [(km[:, c * 128:(c + 1) * 128], vbf[0:64, 0:dv])], [bkm, bvbf], [bb])
                                            if gla:
                                                dsd, bdsd = dsd_of[t.key]
                                                V(lambda e: e.scalar_tensor_tensor(
                                                    out=st_ap(sin_, i), in0=st_ap(sin_, i), scalar=dsd[:, i:i + 1], in1=banks[bi][:, 0:dv],
                                                    op0=ALU.mult, op1=ALU.add), [bb, bsin, bdsd], [bsin])
                                            else:
                                                V(lambda e: e.scalar_tensor_tensor(
                                                    out=sin_[:, c, 0:dv], in0=sin_[:, c, 0:dv], scalar=dchunk[h][1], in1=banks[bi][:, 0:dv],
                                                    op0=ALU.mult, op1=ALU.add), [bb, bsin], [bsin])
                                        for c in range(dkc):
                                            do_sc(c)
                                        lo = lanes_of[i]
                                        if gla:
                                            if i % 4 == 3:
                                                P.dma("sync", f"sout{lo}", lambda e, inc: inc(e.dma_start(
                                                    out=sgs[i - 3:i + 1, h, :, :].rearrange("s d v -> d s v"),
                                                    in_=sin_[:, :, :].rearrange("p c (s v) -> p (c s) v", v=256))), R=[bsin])
                                        else:
                                            P.dma("sync", f"sout{lo}", lambda e, inc: inc(e.dma_start(
                                                out=srs[i, h, :, :].rearrange("(c p) v -> p c v", p=128), in_=sin_[:, :, :])), R=[bsin])
                                        if (not gla) and i + 3 < NS:
                                            issue_load(i + 3)
                                    for i in range(NS):
                                        do_seq(i)
                                        if (i + 1) % seq_per_step[0] == 0 or i == NS - 1:
                                            yield
                                    A(lambda e: e.activation(out=oint[:, 0:dv], in_=banks[7][0:64, 0:dv], func=AF.Copy), [b7], [boint])
                                    issue_o()
                                    bio_s, bbo_s = osl["bio"], osl["bbo"]
                                    V(lambda e: e.tensor_tensor(out=oint[:, 0:dv], in0=banks[bio_s][0:64, 0:dv], in1=oint[:, 0:dv], op=ALU.add),
                                      [bbo_s, boint], [boint])
                                    o_src = oint[:, 0:dv]
                                    bo_src = boint
                                yield
                                stt_, bst = st_r.next()
                                on, bon = on_r.next()
                                if gla:
                                    junk, bjunk = t512.next()
                                    A(lambda e: e.activation(
                                        out=junk[0:nt, 0:dv], in_=o_src, func=AF.Square, accum_out=stt_[0:nt, 0:1]), [bo_src], [bjunk, bst])
                                    V(lambda e: e.tensor_scalar(
                                        out=stt_[0:nt, 1:2], in0=stt_[0:nt, 0:1], scalar1=1.0 / dv, scalar2=EPS, op0=ALU.mult, op1=ALU.add), [bst], [bst])
                                    G(lambda e: e.tensor_tensor(out=stt_[0:nt, 2:3], in0=stt_[0:nt, 1:2], in1=mhalf[0:nt, :], op=ALU.pow),
                                      [bst, bconst], [bst])
                                    yield
                                    V(lambda e: e.scalar_tensor_tensor(
                                        out=on[0:nt, 0:dv], in0=o_src, scalar=stt_[0:nt, 2:3], in1=zs[0:nt, 0:dv], op0=ALU.mult, op1=ALU.mult),
                                      [bo_src, bst, bzs], [bon])
                                else:
                                    V(lambda e: e.bn_stats(out=stt_[0:nt, 0:6], in_=o_src), [bo_src], [bst])
                                    V(lambda e: e.bn_aggr(out=stt_[0:nt, 6:8], in_=stt_[0:nt, 0:6]), [bst], [bst])
                                    V(lambda e: e.tensor_scalar(
                                        out=stt_[0:nt, 8:9], in0=stt_[0:nt, 7:8], scalar1=EPS, scalar2=None, op0=ALU.add), [bst], [bst])
                                    G(lambda e: e.tensor_tensor(out=stt_[0:nt, 9:10], in0=stt_[0:nt, 8:9], in1=mhalf[0:nt, :], op=ALU.pow),
                                      [bst, bconst], [bst])
                                    yield
                                    onm, bonm = t512.next()
                                    V(lambda e: e.tensor_scalar(
                                        out=onm[0:nt, :], in0=o_src, scalar1=stt_[0:nt, 6:7], scalar2=stt_[0:nt, 9:10],
                                        op0=ALU.subtract, op1=ALU.mult), [bo_src, bst], [bonm])
                                    G(lambda e: e.tensor_tensor(out=on[0:nt, :], in0=onm[0:nt, :], in1=zs[0:nt, :], op=ALU.mult), [bonm, bzs], [bon])
                                yield
                                bix, bbx = psum()
                                tr([(banks[bix][:, j * 128:j * 128 + nt], on[0:nt, j * 128:(j + 1) * 128], ident_f[0:nt, 0:nt]) for j in range(nch)],
                                   [bon, bconst], [bbx])
                                for j in range(nch):
                                    A(lambda e, j=j, oc=ochunk0 + j: e.activation(
                                        out=oT[:, oc, tsl], in_=banks[bix][:, j * 128:j * 128 + nt], func=AF.Identity, scale=gnT[:, oc:oc + 1]),
                                      [bbx, bconst], [boT[u]])
                            if staged:
                                return stages, do_tile
                            for f_, _late in stages:
                                f_()
                            return do_tile
                        return do_group
                    ps = {"s2": None, "s3": None, "s4": None}

                    def pstep(g, adv=None):
                        g3 = ps["s3"]
                        if g3 is not None:
                            next(g3)
                        if adv:
                            adv(0)
                        if g is not None:
                            next(g)
                        if adv:
                            adv(1)
                        if g3 is not None:
                            next(g3)
                        if adv:
                            adv(2)
                        if ps["s2"] is not None:
                            next(ps["s2"])
                        if adv:
                            adv(3)
                        if ps["s4"] is not None:
                            for _ in ps["s4"]:
                                pass
                        if adv:
                            adv(4)
                        ps["s4"] = g3
                        ps["s3"] = ps["s2"]
                        ps["s2"] = g
                    if not has_s:
                        upairs = [(u, gi) for u in range(8) for gi in range(len(groups))]
                        ufn = {0: do_unit(0)}
                        gfn = {(0, 0): ufn[0](groups[0])}
                        pend = {"st": [], "tile": None, "key": None}
                        for j, (u, gi) in enumerate(upairs):
                            grp = groups[gi]
                            if gi == 0 and u + 1 < 8:
                                ufn[u + 1] = do_unit(u + 1)
                            if j + 1 < len(upairs):
                                nk = upairs[j + 1]
                                st_list, tile_fn = ufn[nk[0]](groups[nk[1]], staged=True)
                                pend["st"], pend["tile"], pend["key"] = list(st_list), tile_fn, nk
                            n_early = sum(1 for _f, late in pend["st"] if not late)
                            per = -(-n_early // len(grp)) if n_early else 0
                            for i, t in enumerate(grp):
                                k_ = per
                                while k_ > 0 and pend["st"] and not pend["st"][0][1]:
                                    pend["st"].pop(0)[0]()
                                    k_ -= 1
                                if i == len(grp) - 1:
                                    while pend["st"]:
                                        pend["st"].pop(0)[0]()
                                    if pend["key"] is not None:
                                        gfn[pend["key"]] = pend["tile"]
                                pstep(gfn[(u, gi)](t))
                    else:
                        sgrp = [g for g in groups if g[0].samp][0]
                        pgrps = [g for g in groups if not g[0].samp]
                        ptl = [(gi, t) for gi, g in enumerate(pgrps) for t in g]
                        seq_per_step[0] = 1
                        quota = -(-NS // len(ptl))
                        sched = [quota // 5] * 5
                        for sl in [1, 3, 0, 2, 4][:quota % 5]:
                            sched[sl] += 1

                        def wrap(g):
                            yield
                            yield from g
                        ufn = {0: do_unit(0)}
                        for u in range(8):
                            if u + 1 < 8:
                                ufn[u + 1] = do_unit(u + 1)
                            gen_s = ufn[u](sgrp)(sgrp[0])
                            next(gen_s)
                            gf = {}
                            st_ = {"done": 0, "first": True}

                            def adv(slot, gen_s=gen_s, st_=st_):
                                sc = sched
                                if st_["first"]:
                                    sc = [0, 0, 0, quota - quota // 2, quota // 2]
                                for _ in range(sc[slot]):
                                    if st_["done"] < NS:
                                        next(gen_s)
                                        st_["done"] += 1
                            for idx, (gi, t) in enumerate(ptl):
                                if gi not in gf:
                                    gf[gi] = ufn[u](pgrps[gi])
                                pstep(gf[gi](t), adv)
                                st_["first"] = False
                            while st_["done"] < NS:
                                next(gen_s)
                                st_["done"] += 1
                            pstep(wrap(gen_s))
                    pstep(None)
                    pstep(None)
                    pstep(None)
                    P.barrier()

                with ExitStack() as stf:
                    mT = sb(f"mT{pi}", [128, 8, NP], BF16, stf)
                    bmT = Buf("mT")
                    gate_p = sb(f"gatep{pi}", [128, D], F32, stf)
                    gate_s = sb(f"gates{pi}", [64, D], F32, stf) if has_s else None
                    bgate = Buf("gate")
                    lng = sb(f"lng{pi}", [128, D], F32, stf)
                    lnb = sb(f"lnb{pi}", [128, D], F32, stf)
                    bln = Buf("ln")
                    P.dma("sync", "ln", lambda e, inc: inc(e.dma_start(out=lng[:, :], in_=ln_g.partition_broadcast(128))), W=[bln])
                    P.dma("sync", "ln", lambda e, inc: inc(e.dma_start(out=lnb[:, :], in_=ln_b.partition_broadcast(128))), W=[bln])
                    P.dma("sync", "ln", lambda e, inc: inc(e.dma_start(out=gate_p[:, :], in_=b_ada[2 * D:3 * D].partition_broadcast(128))), W=[bln])
                    if has_s:
                        P.dma("sync", "ln", lambda e, inc: inc(e.dma_start(out=gate_s[:, :], in_=b_ada[2 * D:3 * D].partition_broadcast(64))), W=[bln])
                    P.seal("ln", [bln])
                    V(lambda e: e.tensor_scalar(out=gate_p[:, :], in0=gate_p[:, :], scalar1=0.5, scalar2=None, op0=ALU.mult), [bln], [bln])
                    if has_s:
                        V(lambda e: e.tensor_scalar(out=gate_s[:, :], in0=gate_s[:, :], scalar1=0.5, scalar2=None, op0=ALU.mult), [bln], [bln])
                    f4 = Ring(nc, stf, f"f4{pi}", [128, 512], F32, 4)
                    xringf = Ring(nc, stf, f"xtf{pi}", [128, D], F32, 2)
                    ft = Ring(nc, stf, f"ft{pi}", [128, D], F32, 2)
                    yr = Ring(nc, stf, f"yr{pi}", [128, D], F32, 2)
                    stf_r = Ring(nc, stf, f"stf{pi}", [128, 24], F32, 2)
                    ylane = [0]

                    fl = {"issued": 0, "slots": {}}

                    def fl_gate(j):
                        def f(wsl):
                            wv = wslot[wsl][:, 0:4096].rearrange("p (k n) -> p k n", k=8)
                            P.dma("gpsimd", f"w{wsl}", lambda e, inc: inc(e.dma_start(
                                out=wv, in_=w_ada_v[:, :, 2 * D + j * 512:2 * D + (j + 1) * 512])), W=[bw[wsl]])
                            return wv
                        return f

                    def fl_pair(m):
                        def f(wsl):
                            wv = wslot[wsl][:, 0:40 * 256].rearrange("p (k n) -> p k n", n=256)

                            def wload(e, inc):
                                inc(e.dma_start(out=wv[:, 0:8, :], in_=w_in_v[:, :, MG + m * 128:MG + (m + 2) * 128]))
                                inc(e.dma_start(out=wv[:, 8:16, :], in_=w_in_v[:, :, MR + m * 128:MR + (m + 2) * 128]))
                                inc(e.dma_start(out=wv[:, 16:24, :], in_=wbg_v[:, :, m * 128:(m + 2) * 128]))
                                inc(e.dma_start(out=wv[:, 24:40, :], in_=wbr_v[:, :, m * 128:(m + 2) * 128]))
                            P.dma("gpsimd", f"w{wsl}", wload, W=[bw[wsl]], n=4)
                            return wv
                        return f

                    def fl_out():
                        def f(wsl):
                            wv = wslot[wsl][:, 0:8192].rearrange("p (k n) -> p k n", k=8)
                            P.dma("gpsimd", f"w{wsl}", lambda e, inc: inc(e.dma_start(out=wv, in_=w_out_v[:, :, :])), W=[bw[wsl]])
                            return wv
                        return f
                    fl_list = [fl_gate(0), fl_gate(1), fl_pair(0), fl_pair(2), fl_pair(4), fl_pair(6), fl_out()]

                    def fl_issue_upto(k):
                        while fl["issued"] <= k and fl["issued"] < len(fl_list):
                            j = fl["issued"]
                            wsl = next_w()
                            fl["slots"][j] = (wsl, fl_list[j](wsl))
                            fl["issued"] += 1

                    def fl_get(j):
                        fl_issue_upto(j + 1)
                        return fl["slots"][j]

                    def do_gate(j):
                        wsl, wv = fl_get(j)
                        bi, bb = psum7()
                        mm(banks[bi][:, 0:512], [(cT[:, kc, 64:192], wv[:, kc, :]) for kc in range(8)], [bcT, bw[wsl]], [bb])
                        V(lambda e: e.scalar_tensor_tensor(
                            out=gate_p[:, j * 512:(j + 1) * 512], in0=banks[bi][:, 0:512], scalar=0.5, in1=gate_p[:, j * 512:(j + 1) * 512],
                            op0=ALU.mult, op1=ALU.add), [bb, bln], [bgate])
                        if has_s:
                            bi2, bb2 = psum7()
                            mm(banks[bi2][0:64, 0:512], [(cT[:, kc, 0:64], wv[:, kc, :]) for kc in range(8)], [bcT, bw[wsl]], [bb2])
                            V(lambda e: e.scalar_tensor_tensor(
                                out=gate_s[:, j * 512:(j + 1) * 512], in0=banks[bi2][0:64, 0:512], scalar=0.5, in1=gate_s[:, j * 512:(j + 1) * 512],
                                op0=ALU.mult, op1=ALU.add), [bb2, bln], [bgate])
                    for j in range(2):
                        do_gate(j)

                    wcur = {}

                    def do_m(m):
                        mo = (m % 2) * 128
                        wsl, wv = fl_get(2 + m // 2)

                        def do_fg(grp):
                            g0 = grp[0].col
                            NG = sum(t.nt for t in grp)
                            gs = slice(g0, g0 + NG)
                            b_mg, bb_mg = psum7()
                            mm(banks[b_mg][:, 0:NG], [(wv[:, kc, mo:mo + 128], hT[:, kc, gs]) for kc in range(8)], [bw[wsl], bhT], [bb_mg])
                            b_pg, bb_pg = psum7()
                            mm(banks[b_pg][:, 0:NG], [(wv[:, 16 + kc, mo:mo + 128], oT[:, kc, gs]) for kc in range(8)], [bw[wsl]] + boT[0:4], [bb_pg])
                            b_mr, bb_mr = psum7()
                            mm(banks[b_mr][:, 0:NG], [(wv[:, 8 + kc, mo:mo + 128], hT[:, kc, gs]) for kc in range(8)], [bw[wsl], bhT], [bb_mr])
                            b_pr, bb_pr = psum7()
                            mm(banks[b_pr][:, 0:NG], [(wv[:, 24 + kc, mo:mo + 128], oT[:, 8 + kc, gs]) for kc in range(16)], [bw[wsl]] + boT[4:8], [bb_pr])
                            tg, btg = f4.next()
                            trr, btr = f4.next()
                            A(lambda e: e.activation(out=tg[:, 0:NG], in_=banks[b_mg][:, 0:NG], func=AF.Tanh, scale=0.5), [bb_mg], [btg])
                            A(lambda e: e.activation(out=trr[:, 0:NG], in_=banks[b_mr][:, 0:NG], func=AF.Tanh, scale=0.5), [bb_mr], [btr])
                            V(lambda e: e.scalar_tensor_tensor(
                                out=tg[:, 0:NG], in0=tg[:, 0:NG], scalar=1.0, in1=banks[b_pg][:, 0:NG], op0=ALU.add, op1=ALU.mult), [btg, bb_pg], [btg])
                            V(lambda e: e.scalar_tensor_tensor(
                                out=trr[:, 0:NG], in0=trr[:, 0:NG], scalar=1.0, in1=banks[b_pr][:, 0:NG], op0=ALU.add, op1=ALU.mult), [btr, bb_pr], [btr])
                            G(lambda e: e.tensor_tensor(out=mT[:, m, gs], in0=tg[:, 0:NG], in1=trr[:, 0:NG], op=ALU.add), [btg, btr], [bmT])
                        for grp in groups:
                            do_fg(grp)
                    for m in range(8):
                        do_m(m)
                    wslo, wvo = fl_get(6)

                    def do_ft(t):
                        nt = t.nt
                        tsl = slice(t.col, t.col + nt)
                        xt, bx = xringf.next()
                        xsrc = x_s[0:64, :] if t.samp else x_p[t.tok0:t.tok0 + 128, :]
                        xi = xcnt[0] % 2
                        xcnt[0] += 1
                        P.dma("sync", f"x{xi}", lambda e, inc: inc(e.dma_start(out=xt[0:nt, :], in_=xsrc)), W=[bx])
                        gt = gate_s if t.samp else gate_p
                        vt, bvt = ft.next()
                        for n2 in range(2):
                            bi, bb = psum7()
                            mm(banks[bi][0:nt, 0:512], [(mT[:, kc, tsl], wvo[:, kc, n2 * 512:(n2 + 1) * 512]) for kc in range(8)], [bmT, bw[wslo]], [bb])
                            V(lambda e, bi=bi, n2=n2: e.tensor_tensor(
                                out=vt[0:nt, n2 * 512:(n2 + 1) * 512], in0=banks[bi][0:nt, 0:512], in1=gt[0:nt, n2 * 512:(n2 + 1) * 512], op=ALU.mult),
                              [bb, bgate], [bvt])
                        V(lambda e: e.scalar_tensor_tensor(
                            out=vt[0:nt, :], in0=xt[0:nt, :], scalar=ALPHA, in1=vt[0:nt, :], op0=ALU.mult, op1=ALU.add), [bx, bvt], [bvt])
                        sf, bsf = stf_r.next()
                        V(lambda e: e.bn_stats(out=sf[0:nt, 0:6], in_=vt[0:nt, 0:512]), [bvt], [bsf])
                        V(lambda e: e.bn_stats(out=sf[0:nt, 6:12], in_=vt[0:nt, 512:1024]), [bvt], [bsf])
                        V(lambda e: e.bn_aggr(out=sf[0:nt, 12:14], in_=sf[0:nt, 0:12]), [bsf], [bsf])
                        V(lambda e: e.tensor_scalar(out=sf[0:nt, 14:15], in0=sf[0:nt, 13:14], scalar1=EPS, scalar2=None, op0=ALU.add), [bsf], [bsf])
                        G(lambda e: e.tensor_tensor(out=sf[0:nt, 15:16], in0=sf[0:nt, 14:15], in1=mhalf[0:nt, :], op=ALU.pow), [bsf, bconst], [bsf])
                        yield
                        V(lambda e: e.scalar_tensor_tensor(
                            out=sf[0:nt, 16:17], in0=sf[0:nt, 12:13], scalar=-1.0, in1=sf[0:nt, 15:16], op0=ALU.mult, op1=ALU.mult), [bsf], [bsf])
                        yt, byt = yr.next()
                        A(lambda e: e.activation(
                            out=yt[0:nt, :], in_=vt[0:nt, :], func=AF.Identity, scale=sf[0:nt, 15:16], bias=sf[0:nt, 16:17]), [bvt, bsf], [byt])
                        V(lambda e: e.tensor_tensor(out=yt[0:nt, :], in0=yt[0:nt, :], in1=lng[0:nt, :], op=ALU.mult), [byt, bln], [byt])
                        G(lambda e: e.tensor_tensor(out=yt[0:nt, :], in0=yt[0:nt, :], in1=lnb[0:nt, :], op=ALU.add), [byt, bln], [byt])
                        ydst = y_s[0:64, :] if t.samp else y_p[t.tok0:t.tok0 + 128, :]
                        yl = ylane[0] % 2
                        ylane[0] += 1
                        P.dma("sync", f"y{yl}", lambda e, inc: inc(e.dma_start(out=ydst, in_=yt[0:nt, :])), R=[byt])
                    prev_g = None
                    for t in tiles:
                        g_ = do_ft(t)
                        next(g_)
                        if prev_g is not None:
                            for _ in prev_g:
                                pass
                        prev_g = g_
                    for _ in prev_g:
                        pass
                    P.barrier()

        for pi, pkeys in enumerate(PASSES):
            do_pass(pi, pkeys)
        P.barrier()
        P.emit()
    return nc, dbg_outs


def _constants():
    half = 128
    inv_freq = (10000.0 ** (-np.arange(half, dtype=np.float32) / np.float32(half))).astype(np.float32)
    pos = np.concatenate([np.arange(TP, dtype=np.int32),
                          np.tile(16384 + np.arange(TS, dtype=np.int32), NS)]).astype(np.float32)
    ang = (pos[:, None] * inv_freq[None, :]).astype(np.float32)
    rope = np.stack([np.cos(ang).T, np.sin(ang).T], axis=1).astype(np.float32)
    lg = np.log1p(-np.exp2(-5.0 - np.arange(4, dtype=np.float32))).astype(np.float32)
    rmask = np.zeros((128, 4, 2, 128), np.float32)
    rdread = np.zeros((128, 4, 2, 128), np.float32)
    rdwrite = np.zeros((128, 8), np.float32)
    idx = np.arange(128)
    for h in range(4):
        for kind, L in ((0, 128), (1, 4)):
            n = 128 if kind == 0 else 64
            s = idx[:n, None]
            t = idx[None, :n]
            same = (s // L) == (t // L)
            diff = (t - s).astype(np.float32)
            m = np.where(same & (t >= s), np.exp(np.maximum(diff, 0.0) * lg[h]), 0.0).astype(np.float32) / np.float32(16.0)
            rmask[:n, h, kind, :n] = m
            rdread[:, h, kind, :n] = np.exp(((idx[:n] % L) + 1.0).astype(np.float32) * lg[h])[None, :]
            rdwrite[:n, 2 * h + kind] = np.exp((L - 1.0 - (idx[:n] % L)).astype(np.float32) * lg[h]) / np.float32(16.0)
    tri = np.zeros((128, 2, 272), np.float32)
    for kind, L in ((0, 128), (1, 4)):
        n = 128 if kind == 0 else 64
        s = idx[:n, None]
        t = idx[None, :n]
        same = (s // L) == (t // L)
        tri[:n, kind, 0:n] = (same & (s <= t)).astype(np.float32)
        tri[:n, kind, 128:128 + n] = (same & (s > t)).astype(np.float32)
        for i in range(16):
            tri[:n, kind, 256 + i] = ((idx[:n] // L) == i).astype(np.float32)
    colmask = np.zeros((128, 16, 64), np.float32)
    rowmask = np.zeros((128, 16), np.float32)
    for i in range(16):
        colmask[:, i, 4 * i:4 * i + 4] = 1.0
        rowmask[4 * i:4 * i + 4, i] = 1.0
    return dict(rope=rope, rmask=rmask, rdread=rdread, rdwrite=rdwrite, tri=tri, colmask=colmask,
                rowmask=rowmask, ident=np.eye(128, dtype=np.float32))


def _in_maps(inp):
    consts = _constants()
    f = lambda a: np.ascontiguousarray(np.asarray(a, dtype=np.float32))
    shared = dict(
        w_ada=f(inp["w_ada"][0]), b_ada=f(inp["b_ada"][0]), w_in=f(inp["w_in"][0]), w_lr2=f(inp["w_lr2"][0]),
        b_lr2=f(inp["b_lr2"][0]), gla_norm_g=f(inp["gla_norm_g"][0]), ret_norm_g=f(inp["ret_norm_g"][0]),
        w_branch_gla=f(inp["w_branch_gla"][0]), w_branch_ret=f(inp["w_branch_ret"][0]), w_out=f(inp["w_out"][0]),
        ln_g=f(inp["ln_g"][0]), ln_b=f(inp["ln_b"][0]), **consts)
    maps = []
    for c in range(NCORES):
        sl = slice(NS * c, NS * (c + 1))
        c_all = np.concatenate([np.repeat(np.asarray(inp["c_sample"][sl]), TS, axis=0),
                                np.repeat(np.asarray(inp["c_prompt"][c:c + 1]), 128, axis=0)], axis=0)
        m = dict(shared)
        m.update(x_p=f(inp["x_prompt"][c]), x_s=f(np.asarray(inp["x_sample"][sl]).reshape(NS * TS, D)), c_all=f(c_all),
                 sg_in=f(inp["state_gla"][0, sl]), sr_in=f(inp["state_ret"][0, sl]))
        maps.append(m)
    return maps


def kernel(**inputs):
    nc, _ = build_program()
    maps = _in_maps(inputs)
    res = run_bass_kernel_spmd(nc, maps, core_ids=list(range(NCORES)))
    r = res.results
    y_p = np.stack([r[c]["y_p"] for c in range(NCORES)], axis=0)
    y_s = np.concatenate([r[c]["y_s"].reshape(NS, TS, D) for c in range(NCORES)], axis=0)
    sgp = np.stack([r[c]["sgp"] for c in range(NCORES)], axis=0)[None]
    srp = np.stack([r[c]["srp"] for c in range(NCORES)], axis=0)[None]
    sgs = np.concatenate([r[c]["sgs"] for c in range(NCORES)], axis=0)[None]
    srs = np.concatenate([r[c]["srs"] for c in range(NCORES)], axis=0)[None]
    return (y_p.astype(np.float32), y_s.astype(np.float32), sgp.astype(np.float32), srp.astype(np.float32),
            sgs.astype(np.float32), srs.astype(np.float32))
```

```python
import math
from contextlib import ExitStack

import numpy as np
import concourse.bass as bass
import concourse.mybir as mybir
from concourse.bass_utils import run_bass_kernel_spmd

F32 = mybir.dt.float32
BF16 = mybir.dt.bfloat16
AF = mybir.ActivationFunctionType
ALU = mybir.AluOpType

NCORES = 8
D = 1024
TP = 2048
NS = 16
TS = 4
NTOK = TP + NS * TS
GQ, GK, GV, GZ, LR, RQ, RK, RV, RZ, MG, MR = 0, 512, 1024, 2048, 3072, 3088, 4112, 5136, 7184, 9232, 10256
DIN = 11280
ALPHA = 2.0 ** 0.25
EPS = 1e-5
PASSES = [list(range(0, 6)), list(range(6, 13)), list(range(13, 16)) + ["s"]]
WSLOT = 8 * 1536


class Buf:
    __slots__ = ("w", "r", "name", "dead")

    def __init__(self, name="", prev=None):
        self.w = None
        self.r = {}
        self.name = name
        self.dead = False
        if prev is not None:
            self.w = prev.w
            self.r = prev.r
            prev.dead = True


class Prog:
    CE = ("tensor", "vector", "scalar", "gpsimd")
    ENG = ("tensor", "vector", "scalar", "gpsimd", "sync")

    def __init__(self, nc, stack):
        self.nc = nc
        self.stack = stack
        self.q = {e: [] for e in self.ENG}
        self.sems = []
        self.cnt = []
        self.esem = {}
        for e in self.CE:
            self.esem[e] = self._newsem("s_" + e)
        self.seen = {e: {} for e in self.ENG}
        self.lanes = {}

    def _newsem(self, name):
        h = self.stack.enter_context(self.nc.semaphore(name))
        self.sems.append(h)
        self.cnt.append(0)
        return len(self.sems) - 1

    def lane(self, name):
        if name not in self.lanes:
            self.lanes[name] = self._newsem("l_" + name)
        return self.lanes[name]

    def _waits(self, eng, R, W):
        need = {}
        for b in list(R) + list(W):
            assert not b.dead, f"use of recycled buffer {b.name}"

        def add(s, v):
            if need.get(s, 0) < v:
                need[s] = v
        for b in R:
            if b.w is not None:
                add(*b.w)
        for b in W:
            if b.w is not None:
                add(*b.w)
            for s, v in b.r.items():
                add(s, v)
        out = []
        seen = self.seen[eng]
        own = self.esem.get(eng) if eng == "tensor" else None
        for s, v in need.items():
            if s == own:
                continue
            if seen.get(s, 0) < v:
                seen[s] = v
                out.append((s, v))
        return out

    def _commit(self, ev, R, W):
        s, v = ev
        for b in R:
            if b.r.get(s, 0) < v:
                b.r[s] = v
        for b in W:
            b.w = ev
            b.r = {}

    def op(self, eng, fn, R=(), W=()):
        waits = self._waits(eng, R, W)
        s = self.esem[eng]
        self.cnt[s] += 1
        ev = (s, self.cnt[s])
        self.q[eng].append((waits, fn, (s, 1), False))
        self._commit(ev, R, W)
        return ev

    def dma(self, eng, lane, fn, R=(), W=(), n=1):
        waits = self._waits(eng, R, W)
        s = self.lane(lane)
        self.cnt[s] += 16 * n
        ev = (s, self.cnt[s])
        self.q[eng].append((waits, fn, (s, 16), True))
        self._commit(ev, R, W)
        return ev

    def seal(self, lane, bufs):
        s = self.lane(lane)
        for b in bufs:
            b.w = (s, self.cnt[s])

    def barrier(self, skip=()):
        sk = {self.lanes[n] for n in skip if n in self.lanes}
        allev = [(s, c) for s, c in enumerate(self.cnt) if c > 0 and s not in sk]
        for e in self.ENG:
            seen = self.seen[e]
            waits = []
            for s, v in allev:
                if seen.get(s, 0) < v:
                    seen[s] = v
                    waits.append((s, v))
            if waits:
                self.q[e].append((waits, None, None, False))

    def emit(self):
        nc = self.nc
        sems = self.sems

        def replay(name, e):
            for waits, fn, inc, is_dma in self.q[name]:
                for s, v in waits:
                    e.wait_ge(sems[s], v)
                if fn is None:
                    continue
                if is_dma:
                    s, amt = inc
                    fn(e, lambda ins, _s=s, _a=amt: ins.then_inc(sems[_s], _a))
                else:
                    ins = fn(e)
                    ins.then_inc(sems[inc[0]], inc[1])

        with nc.Block() as block:
            @block.tensor
            def _(e):
                replay("tensor", e)

            @block.vector
            def _(e):
                replay("vector", e)

            @block.scalar
            def _(e):
                replay("scalar", e)

            @block.gpsimd
            def _(e):
                replay("gpsimd", e)

            @block.sync
            def _(e):
                replay("sync", e)


class Ring:
    def __init__(self, nc, stack, name, shape, dt, n):
        self.t = [stack.enter_context(nc.sbuf_tensor(f"{name}{i}", list(shape), dt)) for i in range(n)]
        self.b = [Buf(f"{name}{i}") for i in range(n)]
        self.i = 0
        self.n = n

    def next(self):
        i = self.i
        self.i = (i + 1) % self.n
        self.b[i] = Buf(self.b[i].name, prev=self.b[i])
        return self.t[i], self.b[i]


class TileInfo:
    def __init__(self, key, col):
        self.key = key
        self.samp = (key == "s")
        self.kind = 1 if self.samp else 0
        self.nt = 64 if self.samp else 128
        self.tok0 = TP if self.samp else key * 128
        self.col = col


def build_program(debug=None):
    nc = bass.Bass("TRN2", target_bir_lowering=False)
    din = lambda n, s: nc.dram_tensor(n, list(s), F32, kind="ExternalInput").ap()
    dout = lambda n, s: nc.dram_tensor(n, list(s), F32, kind="ExternalOutput").ap()
    x_p = din("x_p", [TP, D])
    x_s = din("x_s", [64, D])
    c_all = din("c_all", [192, D])
    sg_in = din("sg_in", [NS, 4, 128, 256])
    sr_in = din("sr_in", [NS, 4, 256, 512])
    w_ada = din("w_ada", [D, 3 * D])
    b_ada = din("b_ada", [3 * D])
    w_in = din("w_in", [D, DIN])
    w_lr2 = din("w_lr2", [16, 512])
    b_lr2 = din("b_lr2", [512])
    gng = din("gla_norm_g", [1024])
    rng_ = din("ret_norm_g", [2048])
    wbg = din("w_branch_gla", [1024, D])
    wbr = din("w_branch_ret", [2048, D])
    w_out = din("w_out", [D, D])
    ln_g = din("ln_g", [D])
    ln_b = din("ln_b", [D])
    rope = din("rope", [128, 2, NTOK])
    rmask = din("rmask", [128, 4, 2, 128])
    rdread = din("rdread", [128, 4, 2, 128])
    rdwrite = din("rdwrite", [128, 8])
    tri = din("tri", [128, 2, 272])
    colmask = din("colmask", [128, 16, 64])
    rowmask = din("rowmask", [128, 16])
    ident = din("ident", [128, 128])
    y_p = dout("y_p", [TP, D])
    y_s = dout("y_s", [64, D])
    sgp = dout("sgp", [4, 128, 256])
    srp = dout("srp", [4, 256, 512])
    sgs = dout("sgs", [NS, 4, 128, 256])
    srs = dout("srs", [NS, 4, 256, 512])
    dbg_outs = {}

    w_in_v = w_in.rearrange("(kc p) n -> p kc n", p=128)
    w_ada_v = w_ada.rearrange("(kc p) n -> p kc n", p=128)
    wbg_v = wbg.rearrange("(kc p) n -> p kc n", p=128)
    wbr_v = wbr.rearrange("(kc p) n -> p kc n", p=128)
    w_out_v = w_out.rearrange("(kc p) n -> p kc n", p=128)

    lg = [math.log1p(-2.0 ** (-5 - h)) for h in range(4)]
    dchunk = [[math.exp(128 * lg[h]), math.exp(4 * lg[h])] for h in range(4)]

    with ExitStack() as stack:
        P = Prog(nc, stack)

        def sb(name, shape, dt, st=stack):
            return st.enter_context(nc.sbuf_tensor(name, list(shape), dt))

        banks = [stack.enter_context(nc.psum_tensor(f"pb{i}", [128, 512], F32)) for i in range(8)]
        banks_bf = [b.bitcast(BF16) for b in banks]
        bbuf = [Buf(f"pb{i}") for i in range(8)]
        ring_i = [0]

        main_banks = [[0, 1, 2, 3, 4]]
        oring_i = [0]

        def psum():
            mb = main_banks[0]
            i = mb[ring_i[0] % len(mb)]
            ring_i[0] += 1
            bbuf[i] = Buf(bbuf[i].name, prev=bbuf[i])
            return i, bbuf[i]

        ring7_i = [0]

        def psum7():
            i = ring7_i[0]
            ring7_i[0] = (i + 1) % 8
            bbuf[i] = Buf(bbuf[i].name, prev=bbuf[i])
            return i, bbuf[i]

        def psum_o():
            i = 5 + oring_i[0]
            oring_i[0] = (oring_i[0] + 1) % 2
            bbuf[i] = Buf(bbuf[i].name, prev=bbuf[i])
            return i, bbuf[i]

        def dbg(name, ap, buf, shape, dt=F32):
            if debug is None or name not in debug:
                return
            d = nc.dram_tensor("dbg_" + name, list(shape), dt, kind="ExternalOutput").ap()
            dbg_outs[name] = d
            P.dma("sync", "dbg", lambda e, inc: inc(e.dma_start(out=d, in_=ap)), R=[buf])
            P.seal("dbg", [])

        def mm(out_ap, pairs, R, W):
            def fn(e):
                n = len(pairs)
                for i, (l, r) in enumerate(pairs):
                    ins = e.matmul(out=out_ap, lhsT=l, rhs=r, start=(i == 0), stop=(i == n - 1))
                return ins
            P.op("tensor", fn, R=R, W=W)

        def tr(specs, R, W):
            def fn(e):
                for o, i, idn in specs:
                    ins = e.transpose(out=o, in_=i, identity=idn)
                return ins
            P.op("tensor", fn, R=R, W=W)

        def V(fn, R, W):
            P.op("vector", fn, R=R, W=W)

        def A(fn, R, W):
            P.op("scalar", fn, R=R, W=W)

        def G(fn, R, W):
            P.op("gpsimd", fn, R=R, W=W)

        ident_f = sb("ident_f", [128, 128], F32)
        ident_b = sb("ident_b", [128, 128], BF16)
        tri_sb = sb("tri_sb", [128, 2, 272], F32)
        cmask_sb = sb("cmask_sb", [128, 16, 64], BF16)
        rowm_sb = sb("rowm_sb", [128, 16], F32)
        rdw_sb = sb("rdw_sb", [128, 8], F32)
        blr_bc = sb("blr_bc", [128, 512], F32)
        wlr2_b = sb("wlr2_b", [16, 512], BF16)
        gnT = sb("gnT", [128, 24], F32)
        badaT = sb("badaT", [128, 24], F32)
        mhalf = sb("mhalf", [128, 1], F32)
        adaT = sb("adaT", [128, 16, 65], F32)
        cT = sb("cT", [128, 8, 192], BF16)
        S_gla = sb("S_gla", [128, 4, 256], F32)
        S_ret = sb("S_ret", [128, 4, 2, 512], F32)
        bS_gla = [Buf(f"Sg{h}") for h in range(4)]
        bS_ret = [Buf(f"Sr{h}") for h in range(4)]
        wslot = [sb(f"wslot{i}", [128, WSLOT], BF16) for i in range(2)]
        bw = [Buf("w0"), Buf("w1")]
        wi = [0]

        def next_w():
            i = wi[0]
            wi[0] = 1 - i
            return i

        xcnt = [0]
        bconst = Buf("const")
        bada = Buf("ada")
        bcT = Buf("cT")

        def ld(dst, src, **kw):
            P.dma("sync", "c", lambda e, inc: inc(e.dma_start(out=dst, in_=src, **kw)), W=[bconst])

        ld(ident_f[:, :], ident[:, :])
        ld(tri_sb[:, :, :], tri[:, :, :])
        ld(rowm_sb[:, :], rowmask[:, :])
        ld(rdw_sb[:, :], rdwrite[:, :])
        ld(blr_bc[:, :], b_lr2.partition_broadcast(128))
        ld(gnT[:, 0:8], gng.rearrange("(m p) -> p m", p=128), allow_slow_non_contiguous=True)
        ld(gnT[:, 8:24], rng_.rearrange("(m p) -> p m", p=128), allow_slow_non_contiguous=True)
        ld(badaT[:, :], b_ada.rearrange("(m p) -> p m", p=128), allow_slow_non_contiguous=True)
        P.dma("gpsimd", "cw", lambda e, inc: inc(e.dma_start(out=ident_b[:, :], in_=ident[:, :])), W=[bconst])
        P.dma("gpsimd", "cw", lambda e, inc: inc(e.dma_start(out=wlr2_b[:, :], in_=w_lr2[:, :])), W=[bconst])
        P.dma("gpsimd", "cw", lambda e, inc: inc(e.dma_start(out=cmask_sb[:, :, :], in_=colmask[:, :, :])), W=[bconst])
        P.barrier()
        G(lambda e: e.memset(mhalf[:, :], -0.5), [], [bconst])

        with ExitStack() as st0:
            cA = sb("cA", [64, D], F32, st0)
            cB = sb("cB", [128, D], F32, st0)
            bc = Buf("c")
            P.dma("sync", "c2", lambda e, inc: inc(e.dma_start(out=cA[:, :], in_=c_all[0:64, :])), W=[bc])
            P.dma("sync", "c2", lambda e, inc: inc(e.dma_start(out=cB[:, :], in_=c_all[64:192, :])), W=[bc])
            P.seal("c2", [bc])
            for k2 in range(4):
                bi, bb = psum()
                specs = []
                for j in range(2):
                    kc = 2 * k2 + j
                    specs.append((banks[bi][:, j * 192:j * 192 + 64], cA[:, kc * 128:(kc + 1) * 128], ident_f[0:64, 0:64]))
                    specs.append((banks[bi][:, j * 192 + 64:j * 192 + 192], cB[:, kc * 128:(kc + 1) * 128], ident_f[:, :]))
                tr(specs, [bc, bconst], [bb])
                A(lambda e, bi=bi, k2=k2: e.activation(
                    out=cT[:, 2 * k2:2 * k2 + 2, :], in_=banks[bi][:, 0:384].rearrange("p (j n) -> p j n", j=2),
                    func=AF.Copy), [bb], [bcT])
            for j in range(4):
                wsl = next_w()
                wv = wslot[wsl][:, 0:4096].rearrange("p (k n) -> p k n", k=8)
                P.dma("gpsimd", f"w{wsl}", lambda e, inc, wv=wv, j=j: inc(e.dma_start(
                    out=wv, in_=w_ada_v[:, :, j * 512:(j + 1) * 512])), W=[bw[wsl]])
                for mm_ in range(4):
                    m = 4 * j + mm_
                    bi, bb = psum()
                    mm(banks[bi][:, 0:65], [(wv[:, kc, mm_ * 128:(mm_ + 1) * 128], cT[:, kc, 0:65]) for kc in range(8)],
                       [bw[wsl], bcT], [bb])
                    V(lambda e, bi=bi, m=m: e.tensor_scalar(
                        out=adaT[:, m, :], in0=banks[bi][:, 0:65], scalar1=badaT[:, m:m + 1],
                        scalar2=(1.0 if m >= 8 else 0.0), op0=ALU.add, op1=ALU.add), [bb, bconst], [bada])
            P.barrier()

        def do_pass(pi, pkeys):
            tiles = []
            col = 0
            for k in pkeys:
                t = TileInfo(k, col)
                tiles.append(t)
                col += t.nt
            NP = col
            groups = []
            cur = []
            for t in tiles:
                if t.samp:
                    if cur:
                        groups.append(cur)
                    groups.append([t])
                    cur = []
                else:
                    cur.append(t)
                    if len(cur) == 4:
                        groups.append(cur)
                        cur = []
            if cur:
                groups.append(cur)
            has_s = any(t.samp for t in tiles)
            main_banks[0] = [0, 1, 2, 3, 4] if has_s else [0, 1, 2, 3, 4, 7]
            ptiles = [t for t in tiles if not t.samp]
            first_is_zero = (ptiles[0].key == 0)
            last_pass = (ptiles[-1].key == 15)

            def issue_unit_w(u):
                gla = u < 4
                h = u % 4
                wsl = next_w()
                if gla:
                    ncol = 784
                    segs = [(0, GQ + h * 128, 128), (128, GK + h * 128, 128), (256, GV + h * 256, 256),
                            (512, GZ + h * 256, 256), (768, LR, 16)]
                else:
                    ncol = 1536
                    segs = [(0, RQ + h * 256, 256), (256, RK + h * 256, 256), (512, RV + h * 512, 512),
                            (1024, RZ + h * 512, 512)]
                wv = wslot[wsl][:, 0:8 * ncol].rearrange("p (k n) -> p k n", k=8)

                def wload(e, inc):
                    for d0, s0, n in segs:
                        inc(e.dma_start(out=wv[:, :, d0:d0 + n], in_=w_in_v[:, :, s0:s0 + n]))
                P.dma("gpsimd", f"w{wsl}", wload, W=[bw[wsl]], n=len(segs))
                return wsl, wv
            pre_w = {0: issue_unit_w(0), 1: issue_unit_w(1)}

            with ExitStack() as stp:
                hT = sb(f"hT{pi}", [128, 8, NP], BF16, stp)
                oT = sb(f"oT{pi}", [128, 24, NP], BF16, stp)
                bhT = Buf("hT")
                boT = [Buf(f"oT{u}") for u in range(8)]

                with ExitStack() as stq:
                    tmpr = Ring(nc, stq, f"htmp{pi}", [128, 64], F32, 2) if has_s else None
                    xring = Ring(nc, stq, f"xtq{pi}", [128, D], F32, 4)
                    xqc = [0]

                    def do_pt(t):
                        xt, bx = xring.next()
                        xsrc = x_s[0:64, :] if t.samp else x_p[t.tok0:t.tok0 + 128, :]
                        xi = xqc[0] % 4
                        xqc[0] += 1
                        nt = t.nt
                        P.dma("sync", f"xq{xi}", lambda e, inc: inc(e.dma_start(out=xt[0:nt, :], in_=xsrc)), W=[bx])
                        for half in range(2):
                            bi, bb = psum()
                            tr([(banks[bi][:, j * 128:j * 128 + nt], xt[0:nt, (half * 4 + j) * 128:(half * 4 + j + 1) * 128],
                                 ident_f[0:nt, 0:nt]) for j in range(4)], [bx, bconst], [bb])
                            for j in range(4):
                                kc = half * 4 + j
                                src = banks[bi][:, j * 128:j * 128 + nt]
                                dst = hT[:, kc, t.col:t.col + nt]
                                if not t.samp:
                                    if j % 2 == 0:
                                        A(lambda e, src=src, dst=dst, kc=kc: e.activation(
                                            out=dst, in_=src, func=AF.Identity, scale=adaT[:, 8 + kc, 64:65],
                                            bias=adaT[:, kc, 64:65]), [bb, bada], [bhT])
                                    else:
                                        V(lambda e, src=src, dst=dst, kc=kc: e.tensor_scalar(
                                            out=dst, in0=src, scalar1=adaT[:, 8 + kc, 64:65], scalar2=adaT[:, kc, 64:65],
                                            op0=ALU.mult, op1=ALU.add), [bb, bada], [bhT])
                                else:
                                    tm, btm = tmpr.next()
                                    V(lambda e, src=src, tm=tm, kc=kc: e.tensor_tensor(
                                        out=tm[:, :], in0=src, in1=adaT[:, 8 + kc, 0:64], op=ALU.mult), [bb, bada], [btm])
                                    V(lambda e, dst=dst, tm=tm, kc=kc: e.tensor_tensor(
                                        out=dst, in0=tm[:, :], in1=adaT[:, kc, 0:64], op=ALU.add), [btm, bada], [bhT])
                    for t in tiles:
                        do_pt(t)
                    P.barrier(skip=("w0", "w1"))

                with ExitStack() as stu:
                    rope_sb = sb(f"rope{pi}", [128, 2, NP], F32, stu)
                    brope = Buf("rope")
                    for cs in range(2):
                        for t in tiles:
                            P.dma("sync", "rope", lambda e, inc, cs=cs, t=t: inc(e.dma_start(
                                out=rope_sb[:, cs, t.col:t.col + t.nt], in_=rope[:, cs, t.tok0:t.tok0 + t.nt])), W=[brope])
                    P.seal("rope", [brope])
                    f32g = Ring(nc, stu, f"f32g{pi}", [128, 512], F32, 4)
                    bfg = Ring(nc, stu, f"bfg{pi}", [128, 2, 512], BF16, 6)
                    lrT_r = Ring(nc, stu, f"lrT{pi}", [16, 512], BF16, 2)
                    sm128 = Ring(nc, stu, f"sm{pi}", [128, 128], F32, 4)
                    vbf_r = Ring(nc, stu, f"vbf{pi}", [128, 512], BF16, 2)
                    th_r = Ring(nc, stu, f"th{pi}", [128, 512], F32, 1)
                    zs_r = Ring(nc, stu, f"zs{pi}", [128, 512], F32, 3)
                    am_r = Ring(nc, stu, f"am{pi}", [128, 128], BF16, 2)
                    kw_r = Ring(nc, stu, f"kw{pi}", [128, 256], BF16, 2)
                    sbf_r = Ring(nc, stu, f"sbf{pi}", [128, 2, 512], BF16, 3)
                    on_r = Ring(nc, stu, f"on{pi}", [128, 512], F32, 3)
                    t512 = Ring(nc, stu, f"t512{pi}", [128, 512], F32, 2)
                    st_r = Ring(nc, stu, f"st{pi}", [128, 16], F32, 4)
                    dsd_r = Ring(nc, stu, f"dsd{pi}", [128, 16], F32, 8)
                    mk_r = Ring(nc, stu, f"mk{pi}", [128, 2, 2, 128], F32, 2)
                    if has_s:
                        sin_r = Ring(nc, stu, f"sin{pi}", [128, 2, 512], F32, 4)
                        qm_r = Ring(nc, stu, f"qm{pi}", [128, 2, 64], BF16, 2)
                        sinb_r = Ring(nc, stu, f"sinb{pi}", [128, 2, 512], BF16, 2)
                        km_r = Ring(nc, stu, f"km{pi}", [64, 256], BF16, 2)
                        oint = sb(f"oint{pi}", [64, 512], F32, stu)
                        boint = Buf("oint")
                    lane_ctr = {"sin": 0, "sout": 0, "mk": 0}
                    if has_s:
                        bfg_s = Ring(nc, stu, f"bfgs{pi}", [128, 2, 64], BF16, 6)
                        dsd_s = Ring(nc, stu, f"dsds{pi}", [128, 16], F32, 2)
                        vbf_s = Ring(nc, stu, f"vbfs{pi}", [64, 512], BF16, 2)
                        zs_s = Ring(nc, stu, f"zss{pi}", [64, 512], F32, 2)
                        am_s = Ring(nc, stu, f"ams{pi}", [64, 64], BF16, 2)
                        kw_s = Ring(nc, stu, f"kws{pi}", [64, 256], BF16, 2)
                    seq_per_step = [NS]

                    def do_unit(u):
                        gla = u < 4
                        h = u % 4
                        dkc = 1 if gla else 2
                        dv = 256 if gla else 512
                        ochunk0 = (h * 2) if gla else (8 + h * 4)
                        nch = dv // 128
                        wsl, wv = pre_w.pop(u) if u in pre_w else issue_unit_w(u)
                        bwu = bw[wsl]
                        mk = bmk = None
                        if not gla:
                            mk, bmk = mk_r.next()
                            ml = lane_ctr["mk"] % 2
                            lane_ctr["mk"] += 1

                            def mkload(e, inc):
                                inc(e.dma_start(out=mk[:, 0, :, :], in_=rmask[:, h, :, :]))
                                inc(e.dma_start(out=mk[:, 1, :, :], in_=rdread[:, h, :, :]))
                            P.dma("sync", f"mk{ml}", mkload, W=[bmk], n=2)
                        bS = bS_gla[h] if gla else bS_ret[h]
                        us = {"valid": not first_is_zero, "sbf": None, "bsbf": None}

                        def init_sbf():
                            sbf0, bsbf0 = sbf_r.next()
                            us["sbf"], us["bsbf"] = sbf0, bsbf0
                            if gla:
                                A(lambda e: e.activation(out=sbf0[:, 0, 0:256], in_=S_gla[:, h, :], func=AF.Copy), [bS], [bsbf0])
                            else:
                                A(lambda e: e.activation(out=sbf0[:, :, :], in_=S_ret[:, h, :, :], func=AF.Copy), [bS], [bsbf0])

                        def do_group(grp):
                            g0 = grp[0].col
                            NG = sum(t.nt for t in grp)
                            gs = slice(g0, g0 + NG)
                            dsd_of = {}
                            if gla:
                                bi0, bb0 = psum()
                                mm(banks[bi0][0:16, 0:NG], [(wv[:, kc, 768:784], hT[:, kc, gs]) for kc in range(8)], [bwu, bhT], [bb0])
                                lrT, blrT = lrT_r.next()
                                A(lambda e: e.activation(out=lrT[:, 0:NG], in_=banks[bi0][0:16, 0:NG], func=AF.Copy), [bb0], [blrT])
                                Ep, bEp = f32g.next()
                                En, bEn = f32g.next()
                                Er, bEr = f32g.next()

                                def do_decay(t):
                                    lc = t.col - g0
                                    nt = t.nt
                                    bi, bb = psum()
                                    mm(banks[bi][0:nt, 0:128], [(lrT[:, lc:lc + nt], wlr2_b[:, h * 128:(h + 1) * 128])], [blrT, bconst], [bb])
                                    zb, bzb = sm128.next()
                                    V(lambda e: e.tensor_tensor(
                                        out=zb[0:nt, :], in0=banks[bi][0:nt, 0:128], in1=blr_bc[0:nt, h * 128:(h + 1) * 128], op=ALU.add),
                                      [bb, bconst], [bzb])
                                    A(lambda e: e.activation(out=zb[0:nt, :], in_=zb[0:nt, :], func=AF.Exp, scale=-1.0), [bzb], [bzb])
                                    lsb, blsb = sm128.next()
                                    A(lambda e: e.activation(out=lsb[0:nt, :], in_=zb[0:nt, :], func=AF.Ln, bias=1.0), [bzb], [blsb])
                                    bi2, bb2 = psum()
                                    mm(banks[bi2][:, 0:272], [(lsb[0:nt, :], tri_sb[0:nt, t.kind, :])], [blsb, bconst], [bb2])
                                    A(lambda e: e.activation(out=Ep[:, lc:lc + nt], in_=banks[bi2][:, 0:nt], func=AF.Exp, scale=-1.0 / 16), [bb2], [bEp])
                                    A(lambda e: e.activation(out=En[:, lc:lc + nt], in_=banks[bi2][:, 0:nt], func=AF.Exp, scale=1.0 / 16), [bb2], [bEn])
                                    A(lambda e: e.activation(out=Er[:, lc:lc + nt], in_=banks[bi2][:, 128:128 + nt], func=AF.Exp, scale=-1.0 / 16), [bb2], [bEr])
                                    dsd, bdsd = (dsd_s if t.samp else dsd_r).next()
                                    A(lambda e: e.activation(out=dsd[:, :], in_=banks[bi2][:, 256:272], func=AF.Exp, scale=-1.0 / 16), [bb2], [bdsd])
                                    dsd_of[t.key] = (dsd, bdsd)
                                for t in grp:
                                    do_decay(t)
                                biq, bbq = psum()
                                mm(banks[biq][:, 0:NG], [(wv[:, kc, 0:128], hT[:, kc, gs]) for kc in range(8)], [bwu, bhT], [bbq])
                                bik, bbk = psum()
                                mm(banks[bik][:, 0:NG], [(wv[:, kc, 128:256], hT[:, kc, gs]) for kc in range(8)], [bwu, bhT], [bbk])
                                bfgx = bfg_s if grp[0].samp else bfg
                                qd, bqd = bfgx.next()
                                kd, bkd = bfgx.next()
                                kwT, bkwT = bfgx.next()
                                V(lambda e: e.scalar_tensor_tensor(
                                    out=qd[:, 0, 0:NG], in0=banks[biq][:, 0:NG], scalar=128.0 ** -0.5, in1=Ep[:, 0:NG],
                                    op0=ALU.mult, op1=ALU.mult), [bbq, bEp], [bqd])
                                V(lambda e: e.tensor_tensor(out=kd[:, 0, 0:NG], in0=banks[bik][:, 0:NG], in1=En[:, 0:NG], op=ALU.mult), [bbk, bEn], [bkd])
                                V(lambda e: e.tensor_tensor(out=kwT[:, 0, 0:NG], in0=banks[bik][:, 0:NG], in1=Er[:, 0:NG], op=ALU.mult), [bbk, bEr], [bkwT])
                                qA, bqA, kA, bkA, qS, bqS, kW, bkW = qd, bqd, kd, bkd, qd, bqd, kwT, bkwT
                            else:
                                cosg = rope_sb[:, 0, gs]
                                sing = rope_sb[:, 1, gs]

                                def do_rot(which):
                                    b0, bb0 = psum()
                                    mm(banks[b0][:, 0:NG], [(wv[:, kc, which * 256:which * 256 + 128], hT[:, kc, gs]) for kc in range(8)],
                                       [bwu, bhT], [bb0])
                                    b1, bb1 = psum()
                                    mm(banks[b1][:, 0:NG], [(wv[:, kc, which * 256 + 128:which * 256 + 256], hT[:, kc, gs]) for kc in range(8)],
                                       [bwu, bhT], [bb1])
                                    t1, bt1 = f32g.next()
                                    t2, bt2 = f32g.next()
                                    t3, bt3 = f32g.next()
                                    t4, bt4 = f32g.next()
                                    V(lambda e: e.tensor_tensor(out=t1[:, 0:NG], in0=banks[b0][:, 0:NG], in1=cosg, op=ALU.mult), [bb0, brope], [bt1])
                                    V(lambda e: e.tensor_tensor(out=t2[:, 0:NG], in0=banks[b1][:, 0:NG], in1=sing, op=ALU.mult), [bb1, brope], [bt2])
                                    V(lambda e: e.tensor_tensor(out=t3[:, 0:NG], in0=banks[b0][:, 0:NG], in1=sing, op=ALU.mult), [bb0, brope], [bt3])
                                    V(lambda e: e.tensor_tensor(out=t4[:, 0:NG], in0=banks[b1][:, 0:NG], in1=cosg, op=ALU.mult), [bb1, brope], [bt4])
                                    rot, brot = (bfg_s if grp[0].samp else bfg).next()
                                    G(lambda e: e.tensor_tensor(out=rot[:, 0, 0:NG], in0=t1[:, 0:NG], in1=t2[:, 0:NG], op=ALU.subtract), [bt1, bt2], [brot])
                                    G(lambda e: e.tensor_tensor(out=rot[:, 1, 0:NG], in0=t3[:, 0:NG], in1=t4[:, 0:NG], op=ALU.add), [bt3, bt4], [brot])
                                    return rot, brot
                                qr, bqr = do_rot(0)
                                kr, bkr = do_rot(1)
                                qrd, bqrd = (bfg_s if grp[0].samp else bfg).next()
                                for t in grp:
                                    V(lambda e, t=t, lc=t.col - g0: e.tensor_tensor(
                                        out=qrd[:, :, lc:lc + t.nt], in0=qr[:, :, lc:lc + t.nt],
                                        in1=mk[:, 1, t.kind:t.kind + 1, 0:t.nt].to_broadcast([128, 2, t.nt]), op=ALU.mult), [bqr, bmk], [bqrd])
                                qA, bqA, kA, bkA, qS, bqS, kW, bkW = qr, bqr, kr, bkr, qrd, bqrd, kr, bkr

                            def do_tile(t):
                                lc = t.col - g0
                                nt = t.nt
                                tsl = slice(t.col, t.col + nt)
                                vbf, bvbf = (vbf_s if t.samp else vbf_r).next()
                                th, bth = th_r.next()
                                zs, bzs = (zs_s if t.samp else zs_r).next()
                                if gla:
                                    biv, bbv = psum()
                                    mm(banks[biv][0:nt, 0:512], [(hT[:, kc, tsl], wv[:, kc, 256:768]) for kc in range(8)], [bhT, bwu], [bbv])
                                    vsrc = banks[biv][0:nt, 0:256]
                                    zsrc = banks[biv][0:nt, 256:512]
                                    bbz = bbv
                                else:
                                    biv, bbv = psum()
                                    mm(banks[biv][0:nt, 0:512], [(hT[:, kc, tsl], wv[:, kc, 512:1024]) for kc in range(8)], [bhT, bwu], [bbv])
                                    vsrc = banks[biv][0:nt, 0:512]
                                    biz, bbz = psum()
                                    mm(banks[biz][0:nt, 0:512], [(hT[:, kc, tsl], wv[:, kc, 1024:1536]) for kc in range(8)], [bhT, bwu], [bbz])
                                    zsrc = banks[biz][0:nt, 0:512]
                                A(lambda e: e.activation(out=vbf[0:nt, 0:dv], in_=vsrc, func=AF.Copy), [bbv], [bvbf])
                                if gla:
                                    A(lambda e: e.activation(out=th[0:nt, 0:dv], in_=zsrc, func=AF.Exp, scale=-1.0), [bbz], [bth])
                                    V(lambda e: e.tensor_scalar(out=th[0:nt, 0:dv], in0=th[0:nt, 0:dv], scalar1=1.0, scalar2=None, op0=ALU.add), [bth], [bth])
                                    V(lambda e: e.reciprocal(out=th[0:nt, 0:dv], in_=th[0:nt, 0:dv]), [bth], [bth])
                                    V(lambda e: e.tensor_tensor(out=zs[0:nt, 0:dv], in0=zsrc, in1=th[0:nt, 0:dv], op=ALU.mult), [bth, bbz], [bzs])
                                else:
                                    A(lambda e: e.activation(out=zs[0:nt, 0:dv], in_=zsrc, func=AF.Silu), [bbz], [bzs])
                                bia, bba = psum()
                                mm(banks[bia][0:nt, 0:nt], [(kA[:, c, lc:lc + nt], qA[:, c, lc:lc + nt]) for c in range(dkc)], [bkA, bqA], [bba])
                                am, bam = (am_s if t.samp else am_r).next()
                                if gla:
                                    msk = tri_sb[0:nt, t.kind, 0:nt]
                                    bmsk = bconst
                                else:
                                    msk = mk[0:nt, 0, t.kind, 0:nt]
                                    bmsk = bmk
                                V(lambda e: e.tensor_tensor(out=am[0:nt, 0:nt], in0=banks[bia][0:nt, 0:nt], in1=msk, op=ALU.mult), [bba, bmsk], [bam])
                                osl = {}

                                def issue_o():
                                    bio, bbo = psum_o()
                                    pairs = [(am[0:nt, 0:nt], vbf[0:nt, 0:dv])]
                                    Rl = [bam, bvbf]
                                    if (not t.samp) and us["valid"]:
                                        if us["sbf"] is None:
                                            init_sbf()
                                        sbfc = us["sbf"]
                                        pairs += [(qS[:, c, lc:lc + nt], sbfc[:, c, 0:dv]) for c in range(dkc)]
                                        Rl += [bqS, us["bsbf"]]
                                    mm(banks[bio][0:nt, 0:dv], pairs, Rl, [bbo])
                                    osl["bio"], osl["bbo"] = bio, bbo
                                bit, bbt = psum()
                                tr([(banks_bf[bit][0:nt, c * 128:(c + 1) * 128], kW[:, c, lc:lc + nt], ident_b[:, :]) for c in range(dkc)],
                                   [bkW, bconst], [bbt])
                                kw, bkw = (kw_s if t.samp else kw_r).next()
                                if gla:
                                    A(lambda e: e.activation(out=kw[0:nt, 0:128], in_=banks_bf[bit][0:nt, 0:128], func=AF.Copy), [bbt], [bkw])
                                else:
                                    A(lambda e: e.activation(
                                        out=kw[0:nt, 0:256], in_=banks_bf[bit][0:nt, 0:256], func=AF.Identity,
                                        scale=rdw_sb[0:nt, 2 * h + t.kind:2 * h + t.kind + 1]), [bbt, bconst], [bkw])
                                yield
                                if not t.samp:
                                    issue_o()
                                    o_src = banks[osl["bio"]][0:nt, 0:dv]
                                    bo_src = osl["bbo"]
                                    nsbf, bnsbf = sbf_r.next()
                                    valid = us["valid"]

                                    def do_c(c):
                                        bi, bb = psum()
                                        mm(banks[bi][:, 0:dv], [(kw[0:nt, c * 128:(c + 1) * 128], vbf[0:nt, 0:dv])], [bkw, bvbf], [bb])
                                        Sd = S_gla[:, h, :] if gla else S_ret[:, h, c, :]
                                        if not valid:
                                            V(lambda e: e.tensor_copy(out=Sd, in_=banks[bi][:, 0:dv]), [bb], [bS])
                                        elif gla:
                                            dsd, bdsd = dsd_of[t.key]
                                            V(lambda e: e.scalar_tensor_tensor(
                                                out=Sd, in0=Sd, scalar=dsd[:, 0:1], in1=banks[bi][:, 0:dv], op0=ALU.mult, op1=ALU.add),
                                              [bb, bdsd, bS], [bS])
                                        else:
                                            V(lambda e: e.scalar_tensor_tensor(
                                                out=Sd, in0=Sd, scalar=dchunk[h][0], in1=banks[bi][:, 0:dv], op0=ALU.mult, op1=ALU.add),
                                              [bb, bS], [bS])
                                        A(lambda e: e.activation(out=nsbf[:, c, 0:dv], in_=Sd, func=AF.Copy), [bS], [bnsbf])
                                    for c in range(dkc):
                                        do_c(c)
                                    us["sbf"], us["bsbf"] = nsbf, bnsbf
                                    us["valid"] = True
                                    if last_pass and t.key == 15:
                                        if gla:
                                            P.dma("sync", "stp", lambda e, inc: inc(e.dma_start(out=sgp[h, :, :], in_=S_gla[:, h, :])), R=[bS])
                                        else:
                                            P.dma("sync", "stp", lambda e, inc: inc(e.dma_start(
                                                out=srp[h, :, :].rearrange("(c p) v -> p c v", p=128), in_=S_ret[:, h, :, :])), R=[bS])
                                else:
                                    loads = {}
                                    lanes_of = {}
                                    bbuf[7] = Buf("pb7", prev=bbuf[7])
                                    b7 = bbuf[7]

                                    def issue_load(i):
                                        if gla and i % 4 != 0:
                                            loads[i] = loads[i - 1]
                                            lanes_of[i] = lanes_of[i - 1]
                                            return
                                        sin_, bsin = sin_r.next()
                                        li = lane_ctr["sin"] % 4
                                        lane_ctr["sin"] += 1
                                        lanes_of[i] = li
                                        if gla:
                                            P.dma("sync", f"sin{li}", lambda e, inc: inc(e.dma_start(
                                                out=sin_[:, :, :].rearrange("p c (s v) -> p (c s) v", v=256),
                                                in_=sg_in[i:i + 4, h, :, :].rearrange("s d v -> d s v"))), W=[bsin])
                                        else:
                                            P.dma("sync", f"sin{li}", lambda e, inc: inc(e.dma_start(
                                                out=sin_[:, :, :], in_=sr_in[i, h, :, :].rearrange("(c p) v -> p c v", p=128))), W=[bsin])
                                        loads[i] = (sin_, bsin)
                                    if gla:
                                        for i0 in range(NS):
                                            issue_load(i0)
                                    else:
                                        issue_load(0)
                                        issue_load(1)
                                        issue_load(2)

                                    def st_ap(sin_, i):
                                        if gla:
                                            return sin_[:, :, :].rearrange("p c (s v) -> p (c s) v", v=256)[:, i % 4, :]
                                        return None

                                    preps = {}

                                    def prep_seq(i):
                                        sin_, bsin = loads[i]
                                        qm, bqm = qm_r.next()
                                        V(lambda e: e.tensor_tensor(
                                            out=qm[:, 0:dkc, :], in0=qS[:, 0:dkc, lc:lc + 64],
                                            in1=cmask_sb[:, i:i + 1, :].to_broadcast([128, dkc, 64]), op=ALU.mult), [bqS, bconst], [bqm])
                                        sinb, bsinb = sinb_r.next()
                                        if gla:
                                            A(lambda e: e.activation(out=sinb[:, 0, 0:256], in_=st_ap(sin_, i), func=AF.Copy), [bsin], [bsinb])
                                        else:
                                            A(lambda e: e.activation(out=sinb[:, 0:dkc, 0:dv], in_=sin_[:, 0:dkc, 0:dv], func=AF.Copy), [bsin], [bsinb])
                                        km, bkm = km_r.next()
                                        V(lambda e: e.tensor_scalar(
                                            out=km[:, 0:dkc * 128], in0=kw[0:64, 0:dkc * 128], scalar1=rowm_sb[0:64, i:i + 1], scalar2=None,
                                            op0=ALU.mult), [bkw, bconst], [bkm])
                                        preps[i] = (qm, bqm, sinb, bsinb, km, bkm)
                                    prep_seq(0)

                                    def do_seq(i):
                                        sin_, bsin = loads[i]
                                        if i + 1 < NS:
                                            prep_seq(i + 1)
                                        qm, bqm, sinb, bsinb, km, bkm = preps.pop(i)
                                        mm_pairs = [(qm[:, c, :], sinb[:, c, 0:dv]) for c in range(dkc)]

                                        def fn_oi(e):
                                            for c, (l, r) in enumerate(mm_pairs):
                                                ins = e.matmul(out=banks[7][0:64, 0:dv], lhsT=l, rhs=r,
                                                               start=(i == 0 and c == 0), stop=(i == NS - 1 and c == dkc - 1))
                                            return ins
                                        P.op("tensor", fn_oi, R=[bqm, bsinb], W=[b7])
                                        def do_sc(c):
                                            bi, bb = psum()
                                            mm(banks[bi][:, 0:dv], [(km[:, c * 128:(c + 1) * 128], vbf[0:64, 0:dv])], [bkm, bvbf], [bb])
                                            if gla:
                                                dsd, bdsd = dsd_of[t.key]
                                                V(lambda e: e.scalar_tensor_tensor(
                                                    out=st_ap(sin_, i), in0=st_ap(sin_, i), scalar=dsd[:, i:i + 1], in1=banks[bi][:, 0:dv],
                                                    op0=ALU.mult, op1=ALU.add), [bb, bsin, bdsd], [bsin])
                                            else:
                                                V(lambda e: e.scalar_tensor_tensor(
                                                    out=sin_[:, c, 0:dv], in0=sin_[:, c, 0:dv], scalar=dchunk[h][1], in1=banks[bi][:, 0:dv],
                                                    op0=ALU.mult, op1=ALU.add), [bb, bsin], [bsin])
                                        for c in range(dkc):
                                            do_sc(c)
                                        lo = lanes_of[i]
                                        if gla:
                                            if i % 4 == 3:
                                                P.dma("sync", f"sout{lo}", lambda e, inc: inc(e.dma_start(
                                                    out=sgs[i - 3:i + 1, h, :, :].rearrange("s d v -> d s v"),
                                                    in_=sin_[:, :, :].rearrange("p c (s v) -> p (c s) v", v=256))), R=[bsin])
                                        else:
                                            P.dma("sync", f"sout{lo}", lambda e, inc: inc(e.dma_start(
                                                out=srs[i, h, :, :].rearrange("(c p) v -> p c v", p=128), in_=sin_[:, :, :])), R=[bsin])
                                        if (not gla) and i + 3 < NS:
                                            issue_load(i + 3)
                                    for i in range(NS):
                                        do_seq(i)
                                        if (i + 1) % seq_per_step[0] == 0 or i == NS - 1:
                                            yield
                                    A(lambda e: e.activation(out=oint[:, 0:dv], in_=banks[7][0:64, 0:dv], func=AF.Copy), [b7], [boint])
                                    issue_o()
                                    bio_s, bbo_s = osl["bio"], osl["bbo"]
                                    V(lambda e: e.tensor_tensor(out=oint[:, 0:dv], in0=banks[bio_s][0:64, 0:dv], in1=oint[:, 0:dv], op=ALU.add),
                                      [bbo_s, boint], [boint])
                                    o_src = oint[:, 0:dv]
                                    bo_src = boint
                                yield
                                stt_, bst = st_r.next()
                                on, bon = on_r.next()
                                if gla:
                                    junk, bjunk = t512.next()
                                    A(lambda e: e.activation(
                                        out=junk[0:nt, 0:dv], in_=o_src, func=AF.Square, accum_out=stt_[0:nt, 0:1]), [bo_src], [bjunk, bst])
                                    V(lambda e: e.tensor_scalar(
                                        out=stt_[0:nt, 1:2], in0=stt_[0:nt, 0:1], scalar1=1.0 / dv, scalar2=EPS, op0=ALU.mult, op1=ALU.add), [bst], [bst])
                                    G(lambda e: e.tensor_tensor(out=stt_[0:nt, 2:3], in0=stt_[0:nt, 1:2], in1=mhalf[0:nt, :], op=ALU.pow),
                                      [bst, bconst], [bst])
                                    yield
                                    V(lambda e: e.scalar_tensor_tensor(
                                        out=on[0:nt, 0:dv], in0=o_src, scalar=stt_[0:nt, 2:3], in1=zs[0:nt, 0:dv], op0=ALU.mult, op1=ALU.mult),
                                      [bo_src, bst, bzs], [bon])
                                else:
                                    V(lambda e: e.bn_stats(out=stt_[0:nt, 0:6], in_=o_src), [bo_src], [bst])
                                    V(lambda e: e.bn_aggr(out=stt_[0:nt, 6:8], in_=stt_[0:nt, 0:6]), [bst], [bst])
                                    V(lambda e: e.tensor_scalar(
                                        out=stt_[0:nt, 8:9], in0=stt_[0:nt, 7:8], scalar1=EPS, scalar2=None, op0=ALU.add), [bst], [bst])
                                    G(lambda e: e.tensor_tensor(out=stt_[0:nt, 9:10], in0=stt_[0:nt, 8:9], in1=mhalf[0:nt, :], op=ALU.pow),
                                      [bst, bconst], [bst])
                                    yield
                                    onm, bonm = t512.next()
                                    V(lambda e: e.tensor_scalar(
                                        out=onm[0:nt, :], in0=o_src, scalar1=stt_[0:nt, 6:7], scalar2=stt_[0:nt, 9:10],
                                        op0=ALU.subtract, op1=ALU.mult), [bo_src, bst], [bonm])
                                    G(lambda e: e.tensor_tensor(out=on[0:nt, :], in0=onm[0:nt, :], in1=zs[0:nt, :], op=ALU.mult), [bonm, bzs], [bon])
                                yield
                                bix, bbx = psum()
                                tr([(banks[bix][:, j * 128:j * 128 + nt], on[0:nt, j * 128:(j + 1) * 128], ident_f[0:nt, 0:nt]) for j in range(nch)],
                                   [bon, bconst], [bbx])
                                for j in range(nch):
                                    A(lambda e, j=j, oc=ochunk0 + j: e.activation(
                                        out=oT[:, oc, tsl], in_=banks[bix][:, j * 128:j * 128 + nt], func=AF.Identity, scale=gnT[:, oc:oc + 1]),
                                      [bbx, bconst], [boT[u]])
                            return do_tile
                        return do_group
                    ps = {"s2": None, "s3": None, "s4": None}

                    def pstep(g, adv=None):
                        g3 = ps["s3"]
                        if g3 is not None:
                            next(g3)
                        if adv:
                            adv(0)
                        if g is not None:
                            next(g)
                        if adv:
                            adv(1)
                        if g3 is not None:
                            next(g3)
                        if adv:
                            adv(2)
                        if ps["s2"] is not None:
                            next(ps["s2"])
                        if adv:
                            adv(3)
                        if ps["s4"] is not None:
                            for _ in ps["s4"]:
                                pass
                        if adv:
                            adv(4)
                        ps["s4"] = g3
                        ps["s3"] = ps["s2"]
                        ps["s2"] = g
                    if not has_s:
                        upairs = [(u, gi) for u in range(8) for gi in range(len(groups))]
                        ufn = {0: do_unit(0)}
                        gfn = {(0, 0): ufn[0](groups[0])}
                        for j, (u, gi) in enumerate(upairs):
                            grp = groups[gi]
                            if gi == 0 and u + 1 < 8:
                                ufn[u + 1] = do_unit(u + 1)
                            for i, t in enumerate(grp):
                                last = (i == len(grp) - 1 and j + 1 < len(upairs))
                                if last:
                                    gfn[upairs[j + 1]] = ufn[upairs[j + 1][0]](groups[upairs[j + 1][1]])
                                pstep(gfn[(u, gi)](t))
                    else:
                        sgrp = [g for g in groups if g[0].samp][0]
                        pgrps = [g for g in groups if not g[0].samp]
                        ptl = [(gi, t) for gi, g in enumerate(pgrps) for t in g]
                        seq_per_step[0] = 1
                        quota = -(-NS // len(ptl))
                        sched = [quota // 5] * 5
                        for sl in [1, 3, 0, 2, 4][:quota % 5]:
                            sched[sl] += 1

                        def wrap(g):
                            yield
                            yield from g
                        ufn = {0: do_unit(0)}
                        for u in range(8):
                            if u + 1 < 8:
                                ufn[u + 1] = do_unit(u + 1)
                            gen_s = ufn[u](sgrp)(sgrp[0])
                            next(gen_s)
                            gf = {}
                            st_ = {"done": 0, "first": True}

                            def adv(slot, gen_s=gen_s, st_=st_):
                                sc = sched
                                if st_["first"]:
                                    sc = [0, 0, 0, quota - quota // 2, quota // 2]
                                for _ in range(sc[slot]):
                                    if st_["done"] < NS:
                                        next(gen_s)
                                        st_["done"] += 1
                            for idx, (gi, t) in enumerate(ptl):
                                if gi not in gf:
                                    gf[gi] = ufn[u](pgrps[gi])
                                pstep(gf[gi](t), adv)
                                st_["first"] = False
                            while st_["done"] < NS:
                                next(gen_s)
                                st_["done"] += 1
                            pstep(wrap(gen_s))
                    pstep(None)
                    pstep(None)
                    pstep(None)
                    P.barrier()

                with ExitStack() as stf:
                    mT = sb(f"mT{pi}", [128, 8, NP], BF16, stf)
                    bmT = Buf("mT")
                    gate_p = sb(f"gatep{pi}", [128, D], F32, stf)
                    gate_s = sb(f"gates{pi}", [64, D], F32, stf) if has_s else None
                    bgate = Buf("gate")
                    lng = sb(f"lng{pi}", [128, D], F32, stf)
                    lnb = sb(f"lnb{pi}", [128, D], F32, stf)
                    bln = Buf("ln")
                    P.dma("sync", "ln", lambda e, inc: inc(e.dma_start(out=lng[:, :], in_=ln_g.partition_broadcast(128))), W=[bln])
                    P.dma("sync", "ln", lambda e, inc: inc(e.dma_start(out=lnb[:, :], in_=ln_b.partition_broadcast(128))), W=[bln])
                    P.dma("sync", "ln", lambda e, inc: inc(e.dma_start(out=gate_p[:, :], in_=b_ada[2 * D:3 * D].partition_broadcast(128))), W=[bln])
                    if has_s:
                        P.dma("sync", "ln", lambda e, inc: inc(e.dma_start(out=gate_s[:, :], in_=b_ada[2 * D:3 * D].partition_broadcast(64))), W=[bln])
                    P.seal("ln", [bln])
                    V(lambda e: e.tensor_scalar(out=gate_p[:, :], in0=gate_p[:, :], scalar1=0.5, scalar2=None, op0=ALU.mult), [bln], [bln])
                    if has_s:
                        V(lambda e: e.tensor_scalar(out=gate_s[:, :], in0=gate_s[:, :], scalar1=0.5, scalar2=None, op0=ALU.mult), [bln], [bln])
                    f4 = Ring(nc, stf, f"f4{pi}", [128, 512], F32, 4)
                    xringf = Ring(nc, stf, f"xtf{pi}", [128, D], F32, 2)
                    ft = Ring(nc, stf, f"ft{pi}", [128, D], F32, 2)
                    yr = Ring(nc, stf, f"yr{pi}", [128, D], F32, 2)
                    stf_r = Ring(nc, stf, f"stf{pi}", [128, 24], F32, 2)
                    ylane = [0]

                    fl = {"issued": 0, "slots": {}}

                    def fl_gate(j):
                        def f(wsl):
                            wv = wslot[wsl][:, 0:4096].rearrange("p (k n) -> p k n", k=8)
                            P.dma("gpsimd", f"w{wsl}", lambda e, inc: inc(e.dma_start(
                                out=wv, in_=w_ada_v[:, :, 2 * D + j * 512:2 * D + (j + 1) * 512])), W=[bw[wsl]])
                            return wv
                        return f

                    def fl_pair(m):
                        def f(wsl):
                            wv = wslot[wsl][:, 0:40 * 256].rearrange("p (k n) -> p k n", n=256)

                            def wload(e, inc):
                                inc(e.dma_start(out=wv[:, 0:8, :], in_=w_in_v[:, :, MG + m * 128:MG + (m + 2) * 128]))
                                inc(e.dma_start(out=wv[:, 8:16, :], in_=w_in_v[:, :, MR + m * 128:MR + (m + 2) * 128]))
                                inc(e.dma_start(out=wv[:, 16:24, :], in_=wbg_v[:, :, m * 128:(m + 2) * 128]))
                                inc(e.dma_start(out=wv[:, 24:40, :], in_=wbr_v[:, :, m * 128:(m + 2) * 128]))
                            P.dma("gpsimd", f"w{wsl}", wload, W=[bw[wsl]], n=4)
                            return wv
                        return f

                    def fl_out():
                        def f(wsl):
                            wv = wslot[wsl][:, 0:8192].rearrange("p (k n) -> p k n", k=8)
                            P.dma("gpsimd", f"w{wsl}", lambda e, inc: inc(e.dma_start(out=wv, in_=w_out_v[:, :, :])), W=[bw[wsl]])
                            return wv
                        return f
                    fl_list = [fl_gate(0), fl_gate(1), fl_pair(0), fl_pair(2), fl_pair(4), fl_pair(6), fl_out()]

                    def fl_issue_upto(k):
                        while fl["issued"] <= k and fl["issued"] < len(fl_list):
                            j = fl["issued"]
                            wsl = next_w()
                            fl["slots"][j] = (wsl, fl_list[j](wsl))
                            fl["issued"] += 1

                    def fl_get(j):
                        fl_issue_upto(j + 1)
                        return fl["slots"][j]

                    def do_gate(j):
                        wsl, wv = fl_get(j)
                        bi, bb = psum7()
                        mm(banks[bi][:, 0:512], [(cT[:, kc, 64:192], wv[:, kc, :]) for kc in range(8)], [bcT, bw[wsl]], [bb])
                        V(lambda e: e.scalar_tensor_tensor(
                            out=gate_p[:, j * 512:(j + 1) * 512], in0=banks[bi][:, 0:512], scalar=0.5, in1=gate_p[:, j * 512:(j + 1) * 512],
                            op0=ALU.mult, op1=ALU.add), [bb, bln], [bgate])
                        if has_s:
                            bi2, bb2 = psum7()
                            mm(banks[bi2][0:64, 0:512], [(cT[:, kc, 0:64], wv[:, kc, :]) for kc in range(8)], [bcT, bw[wsl]], [bb2])
                            V(lambda e: e.scalar_tensor_tensor(
                                out=gate_s[:, j * 512:(j + 1) * 512], in0=banks[bi2][0:64, 0:512], scalar=0.5, in1=gate_s[:, j * 512:(j + 1) * 512],
                                op0=ALU.mult, op1=ALU.add), [bb2, bln], [bgate])
                    for j in range(2):
                        do_gate(j)

                    wcur = {}

                    def do_m(m):
                        mo = (m % 2) * 128
                        wsl, wv = fl_get(2 + m // 2)

                        def do_fg(grp):
                            g0 = grp[0].col
                            NG = sum(t.nt for t in grp)
                            gs = slice(g0, g0 + NG)
                            b_mg, bb_mg = psum7()
                            mm(banks[b_mg][:, 0:NG], [(wv[:, kc, mo:mo + 128], hT[:, kc, gs]) for kc in range(8)], [bw[wsl], bhT], [bb_mg])
                            b_pg, bb_pg = psum7()
                            mm(banks[b_pg][:, 0:NG], [(wv[:, 16 + kc, mo:mo + 128], oT[:, kc, gs]) for kc in range(8)], [bw[wsl]] + boT[0:4], [bb_pg])
                            b_mr, bb_mr = psum7()
                            mm(banks[b_mr][:, 0:NG], [(wv[:, 8 + kc, mo:mo + 128], hT[:, kc, gs]) for kc in range(8)], [bw[wsl], bhT], [bb_mr])
                            b_pr, bb_pr = psum7()
                            mm(banks[b_pr][:, 0:NG], [(wv[:, 24 + kc, mo:mo + 128], oT[:, 8 + kc, gs]) for kc in range(16)], [bw[wsl]] + boT[4:8], [bb_pr])
                            tg, btg = f4.next()
                            trr, btr = f4.next()
                            A(lambda e: e.activation(out=tg[:, 0:NG], in_=banks[b_mg][:, 0:NG], func=AF.Tanh, scale=0.5), [bb_mg], [btg])
                            A(lambda e: e.activation(out=trr[:, 0:NG], in_=banks[b_mr][:, 0:NG], func=AF.Tanh, scale=0.5), [bb_mr], [btr])
                            V(lambda e: e.scalar_tensor_tensor(
                                out=tg[:, 0:NG], in0=tg[:, 0:NG], scalar=1.0, in1=banks[b_pg][:, 0:NG], op0=ALU.add, op1=ALU.mult), [btg, bb_pg], [btg])
                            V(lambda e: e.scalar_tensor_tensor(
                                out=trr[:, 0:NG], in0=trr[:, 0:NG], scalar=1.0, in1=banks[b_pr][:, 0:NG], op0=ALU.add, op1=ALU.mult), [btr, bb_pr], [btr])
                            G(lambda e: e.tensor_tensor(out=mT[:, m, gs], in0=tg[:, 0:NG], in1=trr[:, 0:NG], op=ALU.add), [btg, btr], [bmT])
                        for grp in groups:
                            do_fg(grp)
                    for m in range(8):
                        do_m(m)
                    wslo, wvo = fl_get(6)

                    def do_ft(t):
                        nt = t.nt
                        tsl = slice(t.col, t.col + nt)
                        xt, bx = xringf.next()
                        xsrc = x_s[0:64, :] if t.samp else x_p[t.tok0:t.tok0 + 128, :]
                        xi = xcnt[0] % 2
                        xcnt[0] += 1
                        P.dma("sync", f"x{xi}", lambda e, inc: inc(e.dma_start(out=xt[0:nt, :], in_=xsrc)), W=[bx])
                        gt = gate_s if t.samp else gate_p
                        vt, bvt = ft.next()
                        for n2 in range(2):
                            bi, bb = psum7()
                            mm(banks[bi][0:nt, 0:512], [(mT[:, kc, tsl], wvo[:, kc, n2 * 512:(n2 + 1) * 512]) for kc in range(8)], [bmT, bw[wslo]], [bb])
                            V(lambda e, bi=bi, n2=n2: e.tensor_tensor(
                                out=vt[0:nt, n2 * 512:(n2 + 1) * 512], in0=banks[bi][0:nt, 0:512], in1=gt[0:nt, n2 * 512:(n2 + 1) * 512], op=ALU.mult),
                              [bb, bgate], [bvt])
                        V(lambda e: e.scalar_tensor_tensor(
                            out=vt[0:nt, :], in0=xt[0:nt, :], scalar=ALPHA, in1=vt[0:nt, :], op0=ALU.mult, op1=ALU.add), [bx, bvt], [bvt])
                        sf, bsf = stf_r.next()
                        V(lambda e: e.bn_stats(out=sf[0:nt, 0:6], in_=vt[0:nt, 0:512]), [bvt], [bsf])
                        V(lambda e: e.bn_stats(out=sf[0:nt, 6:12], in_=vt[0:nt, 512:1024]), [bvt], [bsf])
                        V(lambda e: e.bn_aggr(out=sf[0:nt, 12:14], in_=sf[0:nt, 0:12]), [bsf], [bsf])
                        V(lambda e: e.tensor_scalar(out=sf[0:nt, 14:15], in0=sf[0:nt, 13:14], scalar1=EPS, scalar2=None, op0=ALU.add), [bsf], [bsf])
                        G(lambda e: e.tensor_tensor(out=sf[0:nt, 15:16], in0=sf[0:nt, 14:15], in1=mhalf[0:nt, :], op=ALU.pow), [bsf, bconst], [bsf])
                        yield
                        V(lambda e: e.scalar_tensor_tensor(
                            out=sf[0:nt, 16:17], in0=sf[0:nt, 12:13], scalar=-1.0, in1=sf[0:nt, 15:16], op0=ALU.mult, op1=ALU.mult), [bsf], [bsf])
                        yt, byt = yr.next()
                        A(lambda e: e.activation(
                            out=yt[0:nt, :], in_=vt[0:nt, :], func=AF.Identity, scale=sf[0:nt, 15:16], bias=sf[0:nt, 16:17]), [bvt, bsf], [byt])
                        V(lambda e: e.tensor_tensor(out=yt[0:nt, :], in0=yt[0:nt, :], in1=lng[0:nt, :], op=ALU.mult), [byt, bln], [byt])
                        G(lambda e: e.tensor_tensor(out=yt[0:nt, :], in0=yt[0:nt, :], in1=lnb[0:nt, :], op=ALU.add), [byt, bln], [byt])
                        ydst = y_s[0:64, :] if t.samp else y_p[t.tok0:t.tok0 + 128, :]
                        yl = ylane[0] % 2
                        ylane[0] += 1
                        P.dma("sync", f"y{yl}", lambda e, inc: inc(e.dma_start(out=ydst, in_=yt[0:nt, :])), R=[byt])
                    prev_g = None
                    for t in tiles:
                        g_ = do_ft(t)
                        next(g_)
                        if prev_g is not None:
                            for _ in prev_g:
                                pass
                        prev_g = g_
                    for _ in prev_g:
                        pass
                    P.barrier()

        for pi, pkeys in enumerate(PASSES):
            do_pass(pi, pkeys)
        P.barrier()
        P.emit()
    return nc, dbg_outs


def _constants():
    half = 128
    inv_freq = (10000.0 ** (-np.arange(half, dtype=np.float32) / np.float32(half))).astype(np.float32)
    pos = np.concatenate([np.arange(TP, dtype=np.int32),
                          np.tile(16384 + np.arange(TS, dtype=np.int32), NS)]).astype(np.float32)
    ang = (pos[:, None] * inv_freq[None, :]).astype(np.float32)
    rope = np.stack([np.cos(ang).T, np.sin(ang).T], axis=1).astype(np.float32)
    lg = np.log1p(-np.exp2(-5.0 - np.arange(4, dtype=np.float32))).astype(np.float32)
    rmask = np.zeros((128, 4, 2, 128), np.float32)
    rdread = np.zeros((128, 4, 2, 128), np.float32)
    rdwrite = np.zeros((128, 8), np.float32)
    idx = np.arange(128)
    for h in range(4):
        for kind, L in ((0, 128), (1, 4)):
            n = 128 if kind == 0 else 64
            s = idx[:n, None]
            t = idx[None, :n]
            same = (s // L) == (t // L)
            diff = (t - s).astype(np.float32)
            m = np.where(same & (t >= s), np.exp(np.maximum(diff, 0.0) * lg[h]), 0.0).astype(np.float32) / np.float32(16.0)
            rmask[:n, h, kind, :n] = m
            rdread[:, h, kind, :n] = np.exp(((idx[:n] % L) + 1.0).astype(np.float32) * lg[h])[None, :]
            rdwrite[:n, 2 * h + kind] = np.exp((L - 1.0 - (idx[:n] % L)).astype(np.float32) * lg[h]) / np.float32(16.0)
    tri = np.zeros((128, 2, 272), np.float32)
    for kind, L in ((0, 128), (1, 4)):
        n = 128 if kind == 0 else 64
        s = idx[:n, None]
        t = idx[None, :n]
        same = (s // L) == (t // L)
        tri[:n, kind, 0:n] = (same & (s <= t)).astype(np.float32)
        tri[:n, kind, 128:128 + n] = (same & (s > t)).astype(np.float32)
        for i in range(16):
            tri[:n, kind, 256 + i] = ((idx[:n] // L) == i).astype(np.float32)
    colmask = np.zeros((128, 16, 64), np.float32)
    rowmask = np.zeros((128, 16), np.float32)
    for i in range(16):
        colmask[:, i, 4 * i:4 * i + 4] = 1.0
        rowmask[4 * i:4 * i + 4, i] = 1.0
    return dict(rope=rope, rmask=rmask, rdread=rdread, rdwrite=rdwrite, tri=tri, colmask=colmask,
                rowmask=rowmask, ident=np.eye(128, dtype=np.float32))


def _in_maps(inp):
    consts = _constants()
    f = lambda a: np.ascontiguousarray(np.asarray(a, dtype=np.float32))
    shared = dict(
        w_ada=f(inp["w_ada"][0]), b_ada=f(inp["b_ada"][0]), w_in=f(inp["w_in"][0]), w_lr2=f(inp["w_lr2"][0]),
        b_lr2=f(inp["b_lr2"][0]), gla_norm_g=f(inp["gla_norm_g"][0]), ret_norm_g=f(inp["ret_norm_g"][0]),
        w_branch_gla=f(inp["w_branch_gla"][0]), w_branch_ret=f(inp["w_branch_ret"][0]), w_out=f(inp["w_out"][0]),
        ln_g=f(inp["ln_g"][0]), ln_b=f(inp["ln_b"][0]), **consts)
    maps = []
    for c in range(NCORES):
        sl = slice(NS * c, NS * (c + 1))
        c_all = np.concatenate([np.repeat(np.asarray(inp["c_sample"][sl]), TS, axis=0),
                                np.repeat(np.asarray(inp["c_prompt"][c:c + 1]), 128, axis=0)], axis=0)
        m = dict(shared)
        m.update(x_p=f(inp["x_prompt"][c]), x_s=f(np.asarray(inp["x_sample"][sl]).reshape(NS * TS, D)), c_all=f(c_all),
                 sg_in=f(inp["state_gla"][0, sl]), sr_in=f(inp["state_ret"][0, sl]))
        maps.append(m)
    return maps


def kernel(**inputs):
    nc, _ = build_program()
    maps = _in_maps(inputs)
    res = run_bass_kernel_spmd(nc, maps, core_ids=list(range(NCORES)))
    r = res.results
    y_p = np.stack([r[c]["y_p"] for c in range(NCORES)], axis=0)
    y_s = np.concatenate([r[c]["y_s"].reshape(NS, TS, D) for c in range(NCORES)], axis=0)
    sgp = np.stack([r[c]["sgp"] for c in range(NCORES)], axis=0)[None]
    srp = np.stack([r[c]["srp"] for c in range(NCORES)], axis=0)[None]
    sgs = np.concatenate([r[c]["sgs"] for c in range(NCORES)], axis=0)[None]
    srs = np.concatenate([r[c]["srs"] for c in range(NCORES)], axis=0)[None]
    return (y_p.astype(np.float32), y_s.astype(np.float32), sgp.astype(np.float32), srp.astype(np.float32),
            sgs.astype(np.float32), srs.astype(np.float32))
```

```python
import math
from contextlib import ExitStack

import numpy as np
import concourse.bass as bass
import concourse.mybir as mybir
from concourse.bass_utils import run_bass_kernel_spmd

F32 = mybir.dt.float32
BF16 = mybir.dt.bfloat16
AF = mybir.ActivationFunctionType
ALU = mybir.AluOpType

NCORES = 8
D = 1024
TP = 2048
NS = 16
TS = 4
NTOK = TP + NS * TS
GQ, GK, GV, GZ, LR, RQ, RK, RV, RZ, MG, MR = 0, 512, 1024, 2048, 3072, 3088, 4112, 5136, 7184, 9232, 10256
DIN = 11280
ALPHA = 2.0 ** 0.25
EPS = 1e-5
PASSES = [list(range(0, 6)), list(range(6, 13)), list(range(13, 16)) + ["s"]]
WSLOT = 8 * 1536


class Buf:
    __slots__ = ("w", "r", "name", "dead")

    def __init__(self, name="", prev=None):
        self.w = None
        self.r = {}
        self.name = name
        self.dead = False
        if prev is not None:
            self.w = prev.w
            self.r = prev.r
            prev.dead = True


class Prog:
    CE = ("tensor", "vector", "scalar", "gpsimd")
    ENG = ("tensor", "vector", "scalar", "gpsimd", "sync")

    def __init__(self, nc, stack):
        self.nc = nc
        self.stack = stack
        self.q = {e: [] for e in self.ENG}
        self.sems = []
        self.cnt = []
        self.esem = {}
        for e in self.CE:
            self.esem[e] = self._newsem("s_" + e)
        self.seen = {e: {} for e in self.ENG}
        self.lanes = {}

    def _newsem(self, name):
        h = self.stack.enter_context(self.nc.semaphore(name))
        self.sems.append(h)
        self.cnt.append(0)
        return len(self.sems) - 1

    def lane(self, name):
        if name not in self.lanes:
            self.lanes[name] = self._newsem("l_" + name)
        return self.lanes[name]

    def _waits(self, eng, R, W):
        need = {}
        for b in list(R) + list(W):
            assert not b.dead, f"use of recycled buffer {b.name}"

        def add(s, v):
            if need.get(s, 0) < v:
                need[s] = v
        for b in R:
            if b.w is not None:
                add(*b.w)
        for b in W:
            if b.w is not None:
                add(*b.w)
            for s, v in b.r.items():
                add(s, v)
        out = []
        seen = self.seen[eng]
        own = self.esem.get(eng) if eng == "tensor" else None
        for s, v in need.items():
            if s == own:
                continue
            if seen.get(s, 0) < v:
                seen[s] = v
                out.append((s, v))
        return out

    def _commit(self, ev, R, W):
        s, v = ev
        for b in R:
            if b.r.get(s, 0) < v:
                b.r[s] = v
        for b in W:
            b.w = ev
            b.r = {}

    def op(self, eng, fn, R=(), W=()):
        waits = self._waits(eng, R, W)
        s = self.esem[eng]
        self.cnt[s] += 1
        ev = (s, self.cnt[s])
        self.q[eng].append((waits, fn, (s, 1), False))
        self._commit(ev, R, W)
        return ev

    def dma(self, eng, lane, fn, R=(), W=(), n=1):
        waits = self._waits(eng, R, W)
        s = self.lane(lane)
        self.cnt[s] += 16 * n
        ev = (s, self.cnt[s])
        self.q[eng].append((waits, fn, (s, 16), True))
        self._commit(ev, R, W)
        return ev

    def seal(self, lane, bufs):
        s = self.lane(lane)
        for b in bufs:
            b.w = (s, self.cnt[s])

    def barrier(self, skip=()):
        sk = {self.lanes[n] for n in skip if n in self.lanes}
        allev = [(s, c) for s, c in enumerate(self.cnt) if c > 0 and s not in sk]
        for e in self.ENG:
            seen = self.seen[e]
            waits = []
            for s, v in allev:
                if seen.get(s, 0) < v:
                    seen[s] = v
                    waits.append((s, v))
            if waits:
                self.q[e].append((waits, None, None, False))

    def emit(self):
        nc = self.nc
        sems = self.sems

        def replay(name, e):
            for waits, fn, inc, is_dma in self.q[name]:
                for s, v in waits:
                    e.wait_ge(sems[s], v)
                if fn is None:
                    continue
                if is_dma:
                    s, amt = inc
                    fn(e, lambda ins, _s=s, _a=amt: ins.then_inc(sems[_s], _a))
                else:
                    ins = fn(e)
                    ins.then_inc(sems[inc[0]], inc[1])

        with nc.Block() as block:
            @block.tensor
            def _(e):
                replay("tensor", e)

            @block.vector
            def _(e):
                replay("vector", e)

            @block.scalar
            def _(e):
                replay("scalar", e)

            @block.gpsimd
            def _(e):
                replay("gpsimd", e)

            @block.sync
            def _(e):
                replay("sync", e)


class Ring:
    def __init__(self, nc, stack, name, shape, dt, n):
        self.t = [stack.enter_context(nc.sbuf_tensor(f"{name}{i}", list(shape), dt)) for i in range(n)]
        self.b = [Buf(f"{name}{i}") for i in range(n)]
        self.i = 0
        self.n = n

    def next(self):
        i = self.i
        self.i = (i + 1) % self.n
        self.b[i] = Buf(self.b[i].name, prev=self.b[i])
        return self.t[i], self.b[i]


class TileInfo:
    def __init__(self, key, col):
        self.key = key
        self.samp = (key == "s")
        self.kind = 1 if self.samp else 0
        self.nt = 64 if self.samp else 128
        self.tok0 = TP if self.samp else key * 128
        self.col = col


def build_program(debug=None):
    nc = bass.Bass("TRN2", target_bir_lowering=False)
    din = lambda n, s: nc.dram_tensor(n, list(s), F32, kind="ExternalInput").ap()
    dout = lambda n, s: nc.dram_tensor(n, list(s), F32, kind="ExternalOutput").ap()
    x_p = din("x_p", [TP, D])
    x_s = din("x_s", [64, D])
    c_all = din("c_all", [192, D])
    sg_in = din("sg_in", [NS, 4, 128, 256])
    sr_in = din("sr_in", [NS, 4, 256, 512])
    w_ada = din("w_ada", [D, 3 * D])
    b_ada = din("b_ada", [3 * D])
    w_in = din("w_in", [D, DIN])
    w_lr2 = din("w_lr2", [16, 512])
    b_lr2 = din("b_lr2", [512])
    gng = din("gla_norm_g", [1024])
    rng_ = din("ret_norm_g", [2048])
    wbg = din("w_branch_gla", [1024, D])
    wbr = din("w_branch_ret", [2048, D])
    w_out = din("w_out", [D, D])
    ln_g = din("ln_g", [D])
    ln_b = din("ln_b", [D])
    rope = din("rope", [128, 2, NTOK])
    rmask = din("rmask", [128, 4, 2, 128])
    rdread = din("rdread", [128, 4, 2, 128])
    rdwrite = din("rdwrite", [128, 8])
    tri = din("tri", [128, 2, 272])
    colmask = din("colmask", [128, 16, 64])
    rowmask = din("rowmask", [128, 16])
    ident = din("ident", [128, 128])
    y_p = dout("y_p", [TP, D])
    y_s = dout("y_s", [64, D])
    sgp = dout("sgp", [4, 128, 256])
    srp = dout("srp", [4, 256, 512])
    sgs = dout("sgs", [NS, 4, 128, 256])
    srs = dout("srs", [NS, 4, 256, 512])
    dbg_outs = {}
    carry = {}

    w_in_v = w_in.rearrange("(kc p) n -> p kc n", p=128)
    w_ada_v = w_ada.rearrange("(kc p) n -> p kc n", p=128)
    wbg_v = wbg.rearrange("(kc p) n -> p kc n", p=128)
    wbr_v = wbr.rearrange("(kc p) n -> p kc n", p=128)
    w_out_v = w_out.rearrange("(kc p) n -> p kc n", p=128)

    lg = [math.log1p(-2.0 ** (-5 - h)) for h in range(4)]
    dchunk = [[math.exp(128 * lg[h]), math.exp(4 * lg[h])] for h in range(4)]

    with ExitStack() as stack:
        P = Prog(nc, stack)

        def sb(name, shape, dt, st=stack):
            return st.enter_context(nc.sbuf_tensor(name, list(shape), dt))

        banks = [stack.enter_context(nc.psum_tensor(f"pb{i}", [128, 512], F32)) for i in range(8)]
        banks_bf = [b.bitcast(BF16) for b in banks]
        bbuf = [Buf(f"pb{i}") for i in range(8)]
        ring_i = [0]

        main_banks = [[0, 1, 2, 3, 4]]
        oring_i = [0]

        def psum():
            mb = main_banks[0]
            i = mb[ring_i[0] % len(mb)]
            ring_i[0] += 1
            bbuf[i] = Buf(bbuf[i].name, prev=bbuf[i])
            return i, bbuf[i]

        ring7_i = [0]

        def psum7():
            i = ring7_i[0]
            ring7_i[0] = (i + 1) % 8
            bbuf[i] = Buf(bbuf[i].name, prev=bbuf[i])
            return i, bbuf[i]

        def psum_o():
            i = 5 + oring_i[0]
            oring_i[0] = (oring_i[0] + 1) % 2
            bbuf[i] = Buf(bbuf[i].name, prev=bbuf[i])
            return i, bbuf[i]

        def dbg(name, ap, buf, shape, dt=F32):
            if debug is None or name not in debug:
                return
            d = nc.dram_tensor("dbg_" + name, list(shape), dt, kind="ExternalOutput").ap()
            dbg_outs[name] = d
            P.dma("sync", "dbg", lambda e, inc: inc(e.dma_start(out=d, in_=ap)), R=[buf])
            P.seal("dbg", [])

        def mm(out_ap, pairs, R, W):
            def fn(e):
                n = len(pairs)
                for i, (l, r) in enumerate(pairs):
                    ins = e.matmul(out=out_ap, lhsT=l, rhs=r, start=(i == 0), stop=(i == n - 1))
                return ins
            P.op("tensor", fn, R=R, W=W)

        def tr(specs, R, W):
            def fn(e):
                for o, i, idn in specs:
                    ins = e.transpose(out=o, in_=i, identity=idn)
                return ins
            P.op("tensor", fn, R=R, W=W)

        def V(fn, R, W):
            P.op("vector", fn, R=R, W=W)

        def A(fn, R, W):
            P.op("scalar", fn, R=R, W=W)

        def G(fn, R, W):
            P.op("gpsimd", fn, R=R, W=W)

        ident_f = sb("ident_f", [128, 128], F32)
        ident_b = sb("ident_b", [128, 128], BF16)
        tri_sb = sb("tri_sb", [128, 2, 272], F32)
        cmask_sb = sb("cmask_sb", [128, 16, 64], BF16)
        rowm_sb = sb("rowm_sb", [128, 16], F32)
        rdw_sb = sb("rdw_sb", [128, 8], F32)
        blr_bc = sb("blr_bc", [128, 512], F32)
        wlr2_b = sb("wlr2_b", [16, 512], BF16)
        gnT = sb("gnT", [128, 24], F32)
        badaT = sb("badaT", [128, 24], F32)
        mhalf = sb("mhalf", [128, 1], F32)
        adaT = sb("adaT", [128, 16, 65], F32)
        cT = sb("cT", [128, 8, 192], BF16)
        S_gla = sb("S_gla", [128, 4, 256], F32)
        S_ret = sb("S_ret", [128, 4, 2, 512], F32)
        bS_gla = [Buf(f"Sg{h}") for h in range(4)]
        bS_ret = [Buf(f"Sr{h}") for h in range(4)]
        wslot = [sb(f"wslot{i}", [128, WSLOT], BF16) for i in range(2)]
        bw = [Buf("w0"), Buf("w1")]
        wi = [0]

        def next_w():
            i = wi[0]
            wi[0] = 1 - i
            return i

        xcnt = [0]
        bconst = Buf("const")
        bada = Buf("ada")
        bcT = Buf("cT")

        def ld(dst, src, **kw):
            P.dma("sync", "c", lambda e, inc: inc(e.dma_start(out=dst, in_=src, **kw)), W=[bconst])

        ld(ident_f[:, :], ident[:, :])
        ld(tri_sb[:, :, :], tri[:, :, :])
        ld(rowm_sb[:, :], rowmask[:, :])
        ld(rdw_sb[:, :], rdwrite[:, :])
        ld(blr_bc[:, :], b_lr2.partition_broadcast(128))
        ld(gnT[:, 0:8], gng.rearrange("(m p) -> p m", p=128), allow_slow_non_contiguous=True)
        ld(gnT[:, 8:24], rng_.rearrange("(m p) -> p m", p=128), allow_slow_non_contiguous=True)
        ld(badaT[:, :], b_ada.rearrange("(m p) -> p m", p=128), allow_slow_non_contiguous=True)
        P.dma("gpsimd", "cw", lambda e, inc: inc(e.dma_start(out=ident_b[:, :], in_=ident[:, :])), W=[bconst])
        P.dma("gpsimd", "cw", lambda e, inc: inc(e.dma_start(out=wlr2_b[:, :], in_=w_lr2[:, :])), W=[bconst])
        P.dma("gpsimd", "cw", lambda e, inc: inc(e.dma_start(out=cmask_sb[:, :, :], in_=colmask[:, :, :])), W=[bconst])
        P.barrier()
        G(lambda e: e.memset(mhalf[:, :], -0.5), [], [bconst])
        V(lambda e: e.tensor_scalar(out=gnT[:, 0:8], in0=gnT[:, 0:8], scalar1=0.5, scalar2=None, op0=ALU.mult), [bconst], [bconst])

        with ExitStack() as st0:
            cA = sb("cA", [64, D], F32, st0)
            cB = sb("cB", [128, D], F32, st0)
            bc = Buf("c")
            P.dma("sync", "c2", lambda e, inc: inc(e.dma_start(out=cA[:, :], in_=c_all[0:64, :])), W=[bc])
            P.dma("sync", "c2", lambda e, inc: inc(e.dma_start(out=cB[:, :], in_=c_all[64:192, :])), W=[bc])
            P.seal("c2", [bc])
            for k2 in range(4):
                bi, bb = psum()
                specs = []
                for j in range(2):
                    kc = 2 * k2 + j
                    specs.append((banks[bi][:, j * 192:j * 192 + 64], cA[:, kc * 128:(kc + 1) * 128], ident_f[0:64, 0:64]))
                    specs.append((banks[bi][:, j * 192 + 64:j * 192 + 192], cB[:, kc * 128:(kc + 1) * 128], ident_f[:, :]))
                tr(specs, [bc, bconst], [bb])
                A(lambda e, bi=bi, k2=k2: e.activation(
                    out=cT[:, 2 * k2:2 * k2 + 2, :], in_=banks[bi][:, 0:384].rearrange("p (j n) -> p j n", j=2),
                    func=AF.Copy), [bb], [bcT])
            for j in range(4):
                wsl = next_w()
                wv = wslot[wsl][:, 0:4096].rearrange("p (k n) -> p k n", k=8)
                P.dma("gpsimd", f"w{wsl}", lambda e, inc, wv=wv, j=j: inc(e.dma_start(
                    out=wv, in_=w_ada_v[:, :, j * 512:(j + 1) * 512])), W=[bw[wsl]])
                for mm_ in range(4):
                    m = 4 * j + mm_
                    bi, bb = psum()
                    mm(banks[bi][:, 0:65], [(wv[:, kc, mm_ * 128:(mm_ + 1) * 128], cT[:, kc, 0:65]) for kc in range(8)],
                       [bw[wsl], bcT], [bb])
                    V(lambda e, bi=bi, m=m: e.tensor_scalar(
                        out=adaT[:, m, :], in0=banks[bi][:, 0:65], scalar1=badaT[:, m:m + 1],
                        scalar2=(1.0 if m >= 8 else 0.0), op0=ALU.add, op1=ALU.add), [bb, bconst], [bada])
            P.barrier()

        def do_pass(pi, pkeys):
            tiles = []
            col = 0
            for k in pkeys:
                t = TileInfo(k, col)
                tiles.append(t)
                col += t.nt
            NP = col
            groups = []
            cur = []
            for t in tiles:
                if t.samp:
                    if cur:
                        groups.append(cur)
                    groups.append([t])
                    cur = []
                else:
                    cur.append(t)
                    if len(cur) == 4:
                        groups.append(cur)
                        cur = []
            if cur:
                groups.append(cur)
            has_s = any(t.samp for t in tiles)
            main_banks[0] = [0, 1, 2, 3, 4] if has_s else [0, 1, 2, 3, 4, 7]
            ptiles = [t for t in tiles if not t.samp]
            first_is_zero = (ptiles[0].key == 0)
            last_pass = (ptiles[-1].key == 15)

            def issue_unit_w(u):
                gla = u < 4
                h = u % 4
                wsl = next_w()
                if gla:
                    ncol = 784
                    segs = [(0, GQ + h * 128, 128), (128, GK + h * 128, 128), (256, GV + h * 256, 256),
                            (512, GZ + h * 256, 256), (768, LR, 16)]
                else:
                    ncol = 1536
                    segs = [(0, RQ + h * 256, 256), (256, RK + h * 256, 256), (512, RV + h * 512, 512),
                            (1024, RZ + h * 512, 512)]
                wv = wslot[wsl][:, 0:8 * ncol].rearrange("p (k n) -> p k n", k=8)

                def wload(e, inc):
                    for d0, s0, n in segs:
                        inc(e.dma_start(out=wv[:, :, d0:d0 + n], in_=w_in_v[:, :, s0:s0 + n]))
                P.dma("gpsimd", f"w{wsl}", wload, W=[bw[wsl]], n=len(segs))
                return wsl, wv
            pre_w = {0: carry.pop("u0") if "u0" in carry else issue_unit_w(0), 1: issue_unit_w(1)}

            def issue_gate_w(j):
                wsl = next_w()
                wv = wslot[wsl][:, 0:4096].rearrange("p (k n) -> p k n", k=8)
                P.dma("gpsimd", f"w{wsl}", lambda e, inc: inc(e.dma_start(
                    out=wv, in_=w_ada_v[:, :, 2 * D + j * 512:2 * D + (j + 1) * 512])), W=[bw[wsl]])
                return wsl, wv
            pre_fl = {}

            with ExitStack() as stp:
                hT = sb(f"hT{pi}", [128, 8, NP], BF16, stp)
                oT = sb(f"oT{pi}", [128, 24, NP], BF16, stp)
                bhT = Buf("hT")
                boT = [Buf(f"oT{u}") for u in range(8)]

                with ExitStack() as stq:
                    tmpr = Ring(nc, stq, f"htmp{pi}", [128, 64], F32, 2) if has_s else None
                    xring = Ring(nc, stq, f"xtq{pi}", [128, D], F32, 4)
                    xqc = [0]

                    def do_pt(t):
                        xt, bx = xring.next()
                        xsrc = x_s[0:64, :] if t.samp else x_p[t.tok0:t.tok0 + 128, :]
                        xi = xqc[0] % 4
                        xqc[0] += 1
                        nt = t.nt
                        P.dma("sync", f"xq{xi}", lambda e, inc: inc(e.dma_start(out=xt[0:nt, :], in_=xsrc)), W=[bx])
                        for half in range(2):
                            bi, bb = psum()
                            tr([(banks[bi][:, j * 128:j * 128 + nt], xt[0:nt, (half * 4 + j) * 128:(half * 4 + j + 1) * 128],
                                 ident_f[0:nt, 0:nt]) for j in range(4)], [bx, bconst], [bb])
                            for j in range(4):
                                kc = half * 4 + j
                                src = banks[bi][:, j * 128:j * 128 + nt]
                                dst = hT[:, kc, t.col:t.col + nt]
                                if not t.samp:
                                    if j % 2 == 0:
                                        A(lambda e, src=src, dst=dst, kc=kc: e.activation(
                                            out=dst, in_=src, func=AF.Identity, scale=adaT[:, 8 + kc, 64:65],
                                            bias=adaT[:, kc, 64:65]), [bb, bada], [bhT])
                                    else:
                                        V(lambda e, src=src, dst=dst, kc=kc: e.tensor_scalar(
                                            out=dst, in0=src, scalar1=adaT[:, 8 + kc, 64:65], scalar2=adaT[:, kc, 64:65],
                                            op0=ALU.mult, op1=ALU.add), [bb, bada], [bhT])
                                else:
                                    tm, btm = tmpr.next()
                                    V(lambda e, src=src, tm=tm, kc=kc: e.tensor_tensor(
                                        out=tm[:, :], in0=src, in1=adaT[:, 8 + kc, 0:64], op=ALU.mult), [bb, bada], [btm])
                                    V(lambda e, dst=dst, tm=tm, kc=kc: e.tensor_tensor(
                                        out=dst, in0=tm[:, :], in1=adaT[:, kc, 0:64], op=ALU.add), [btm, bada], [bhT])
                    for t in tiles:
                        do_pt(t)
                    P.barrier()

                with ExitStack() as stu:
                    rope_sb = sb(f"rope{pi}", [128, 2, NP], F32, stu)
                    brope = Buf("rope")
                    for cs in range(2):
                        for t in tiles:
                            P.dma("sync", "rope", lambda e, inc, cs=cs, t=t: inc(e.dma_start(
                                out=rope_sb[:, cs, t.col:t.col + t.nt], in_=rope[:, cs, t.tok0:t.tok0 + t.nt])), W=[brope])
                    P.seal("rope", [brope])
                    f32g = Ring(nc, stu, f"f32g{pi}", [128, 512], F32, 4)
                    bfg = Ring(nc, stu, f"bfg{pi}", [128, 2, 512], BF16, 6)
                    lrT_r = Ring(nc, stu, f"lrT{pi}", [16, 512], BF16, 2)
                    sm128 = Ring(nc, stu, f"sm{pi}", [128, 128], F32, 4)
                    vbf_r = Ring(nc, stu, f"vbf{pi}", [128, 512], BF16, 2)
                    th_r = Ring(nc, stu, f"th{pi}", [128, 512], F32, 1)
                    zs_r = Ring(nc, stu, f"zs{pi}", [128, 512], F32, 3)
                    am_r = Ring(nc, stu, f"am{pi}", [128, 128], BF16, 2)
                    kw_r = Ring(nc, stu, f"kw{pi}", [128, 256], BF16, 2)
                    sbf_r = Ring(nc, stu, f"sbf{pi}", [128, 2, 512], BF16, 3)
                    on_r = Ring(nc, stu, f"on{pi}", [128, 512], F32, 3)
                    t512 = Ring(nc, stu, f"t512{pi}", [128, 512], F32, 2)
                    st_r = Ring(nc, stu, f"st{pi}", [128, 16], F32, 4)
                    dsd_r = Ring(nc, stu, f"dsd{pi}", [128, 16], F32, 8)
                    mk_r = Ring(nc, stu, f"mk{pi}", [128, 2, 2, 128], F32, 2)
                    if has_s:
                        sin_r = Ring(nc, stu, f"sin{pi}", [128, 2, 512], F32, 4)
                        qm_r = Ring(nc, stu, f"qm{pi}", [128, 2, 64], BF16, 2)
                        sinb_r = Ring(nc, stu, f"sinb{pi}", [128, 2, 512], BF16, 2)
                        km_r = Ring(nc, stu, f"km{pi}", [64, 256], BF16, 2)
                        oint = sb(f"oint{pi}", [64, 512], F32, stu)
                        boint = Buf("oint")
                    lane_ctr = {"sin": 0, "sout": 0, "mk": 0}
                    if has_s:
                        bfg_s = Ring(nc, stu, f"bfgs{pi}", [128, 2, 64], BF16, 6)
                        dsd_s = Ring(nc, stu, f"dsds{pi}", [128, 16], F32, 2)
                        vbf_s = Ring(nc, stu, f"vbfs{pi}", [64, 512], BF16, 2)
                        zs_s = Ring(nc, stu, f"zss{pi}", [64, 512], F32, 2)
                        am_s = Ring(nc, stu, f"ams{pi}", [64, 64], BF16, 2)
                        kw_s = Ring(nc, stu, f"kws{pi}", [64, 256], BF16, 2)
                    seq_per_step = [NS]

                    def do_unit(u):
                        gla = u < 4
                        h = u % 4
                        dkc = 1 if gla else 2
                        dv = 256 if gla else 512
                        ochunk0 = (h * 2) if gla else (8 + h * 4)
                        nch = dv // 128
                        wsl, wv = pre_w.pop(u) if u in pre_w else issue_unit_w(u)
                        bwu = bw[wsl]
                        mk = bmk = None
                        if not gla:
                            mk, bmk = mk_r.next()
                            ml = lane_ctr["mk"] % 2
                            lane_ctr["mk"] += 1

                            def mkload(e, inc):
                                inc(e.dma_start(out=mk[:, 0, :, :], in_=rmask[:, h, :, :]))
                                inc(e.dma_start(out=mk[:, 1, :, :], in_=rdread[:, h, :, :]))
                            P.dma("sync", f"mk{ml}", mkload, W=[bmk], n=2)
                        bS = bS_gla[h] if gla else bS_ret[h]
                        us = {"valid": not first_is_zero, "sbf": None, "bsbf": None}

                        def init_sbf():
                            sbf0, bsbf0 = sbf_r.next()
                            us["sbf"], us["bsbf"] = sbf0, bsbf0
                            if gla:
                                A(lambda e: e.activation(out=sbf0[:, 0, 0:256], in_=S_gla[:, h, :], func=AF.Copy), [bS], [bsbf0])
                            else:
                                A(lambda e: e.activation(out=sbf0[:, :, :], in_=S_ret[:, h, :, :], func=AF.Copy), [bS], [bsbf0])

                        def do_group(grp):
                            g0 = grp[0].col
                            NG = sum(t.nt for t in grp)
                            gs = slice(g0, g0 + NG)
                            dsd_of = {}
                            if gla:
                                bi0, bb0 = psum()
                                mm(banks[bi0][0:16, 0:NG], [(wv[:, kc, 768:784], hT[:, kc, gs]) for kc in range(8)], [bwu, bhT], [bb0])
                                lrT, blrT = lrT_r.next()
                                A(lambda e: e.activation(out=lrT[:, 0:NG], in_=banks[bi0][0:16, 0:NG], func=AF.Copy), [bb0], [blrT])
                                Ep, bEp = f32g.next()
                                En, bEn = f32g.next()
                                Er, bEr = f32g.next()

                                def do_decay(t):
                                    lc = t.col - g0
                                    nt = t.nt
                                    bi, bb = psum()
                                    mm(banks[bi][0:nt, 0:128], [(lrT[:, lc:lc + nt], wlr2_b[:, h * 128:(h + 1) * 128])], [blrT, bconst], [bb])
                                    zb, bzb = sm128.next()
                                    V(lambda e: e.tensor_tensor(
                                        out=zb[0:nt, :], in0=banks[bi][0:nt, 0:128], in1=blr_bc[0:nt, h * 128:(h + 1) * 128], op=ALU.add),
                                      [bb, bconst], [bzb])
                                    A(lambda e: e.activation(out=zb[0:nt, :], in_=zb[0:nt, :], func=AF.Exp, scale=-1.0), [bzb], [bzb])
                                    lsb, blsb = sm128.next()
                                    A(lambda e: e.activation(out=lsb[0:nt, :], in_=zb[0:nt, :], func=AF.Ln, bias=1.0), [bzb], [blsb])
                                    bi2, bb2 = psum()
                                    mm(banks[bi2][:, 0:272], [(lsb[0:nt, :], tri_sb[0:nt, t.kind, :])], [blsb, bconst], [bb2])
                                    A(lambda e: e.activation(out=Ep[:, lc:lc + nt], in_=banks[bi2][:, 0:nt], func=AF.Exp, scale=-1.0 / 16), [bb2], [bEp])
                                    A(lambda e: e.activation(out=En[:, lc:lc + nt], in_=banks[bi2][:, 0:nt], func=AF.Exp, scale=1.0 / 16), [bb2], [bEn])
                                    A(lambda e: e.activation(out=Er[:, lc:lc + nt], in_=banks[bi2][:, 128:128 + nt], func=AF.Exp, scale=-1.0 / 16), [bb2], [bEr])
                                    dsd, bdsd = (dsd_s if t.samp else dsd_r).next()
                                    A(lambda e: e.activation(out=dsd[:, :], in_=banks[bi2][:, 256:272], func=AF.Exp, scale=-1.0 / 16), [bb2], [bdsd])
                                    dsd_of[t.key] = (dsd, bdsd)
                                for t in grp:
                                    do_decay(t)
                                biq, bbq = psum()
                                mm(banks[biq][:, 0:NG], [(wv[:, kc, 0:128], hT[:, kc, gs]) for kc in range(8)], [bwu, bhT], [bbq])
                                bik, bbk = psum()
                                mm(banks[bik][:, 0:NG], [(wv[:, kc, 128:256], hT[:, kc, gs]) for kc in range(8)], [bwu, bhT], [bbk])
                                bfgx = bfg_s if grp[0].samp else bfg
                                qd, bqd = bfgx.next()
                                kd, bkd = bfgx.next()
                                kwT, bkwT = bfgx.next()
                                V(lambda e: e.scalar_tensor_tensor(
                                    out=qd[:, 0, 0:NG], in0=banks[biq][:, 0:NG], scalar=128.0 ** -0.5, in1=Ep[:, 0:NG],
                                    op0=ALU.mult, op1=ALU.mult), [bbq, bEp], [bqd])
                                V(lambda e: e.tensor_tensor(out=kd[:, 0, 0:NG], in0=banks[bik][:, 0:NG], in1=En[:, 0:NG], op=ALU.mult), [bbk, bEn], [bkd])
                                V(lambda e: e.tensor_tensor(out=kwT[:, 0, 0:NG], in0=banks[bik][:, 0:NG], in1=Er[:, 0:NG], op=ALU.mult), [bbk, bEr], [bkwT])
                                qA, bqA, kA, bkA, qS, bqS, kW, bkW = qd, bqd, kd, bkd, qd, bqd, kwT, bkwT
                            else:
                                cosg = rope_sb[:, 0, gs]
                                sing = rope_sb[:, 1, gs]

                                def do_rot(which):
                                    b0, bb0 = psum()
                                    mm(banks[b0][:, 0:NG], [(wv[:, kc, which * 256:which * 256 + 128], hT[:, kc, gs]) for kc in range(8)],
                                       [bwu, bhT], [bb0])
                                    b1, bb1 = psum()
                                    mm(banks[b1][:, 0:NG], [(wv[:, kc, which * 256 + 128:which * 256 + 256], hT[:, kc, gs]) for kc in range(8)],
                                       [bwu, bhT], [bb1])
                                    t1, bt1 = f32g.next()
                                    t2, bt2 = f32g.next()
                                    t3, bt3 = f32g.next()
                                    t4, bt4 = f32g.next()
                                    V(lambda e: e.tensor_tensor(out=t1[:, 0:NG], in0=banks[b0][:, 0:NG], in1=cosg, op=ALU.mult), [bb0, brope], [bt1])
                                    V(lambda e: e.tensor_tensor(out=t2[:, 0:NG], in0=banks[b1][:, 0:NG], in1=sing, op=ALU.mult), [bb1, brope], [bt2])
                                    V(lambda e: e.tensor_tensor(out=t3[:, 0:NG], in0=banks[b0][:, 0:NG], in1=sing, op=ALU.mult), [bb0, brope], [bt3])
                                    V(lambda e: e.tensor_tensor(out=t4[:, 0:NG], in0=banks[b1][:, 0:NG], in1=cosg, op=ALU.mult), [bb1, brope], [bt4])
                                    rot, brot = (bfg_s if grp[0].samp else bfg).next()
                                    G(lambda e: e.tensor_tensor(out=rot[:, 0, 0:NG], in0=t1[:, 0:NG], in1=t2[:, 0:NG], op=ALU.subtract), [bt1, bt2], [brot])
                                    G(lambda e: e.tensor_tensor(out=rot[:, 1, 0:NG], in0=t3[:, 0:NG], in1=t4[:, 0:NG], op=ALU.add), [bt3, bt4], [brot])
                                    return rot, brot
                                qr, bqr = do_rot(0)
                                kr, bkr = do_rot(1)
                                qrd, bqrd = (bfg_s if grp[0].samp else bfg).next()
                                for t in grp:
                                    V(lambda e, t=t, lc=t.col - g0: e.tensor_tensor(
                                        out=qrd[:, :, lc:lc + t.nt], in0=qr[:, :, lc:lc + t.nt],
                                        in1=mk[:, 1, t.kind:t.kind + 1, 0:t.nt].to_broadcast([128, 2, t.nt]), op=ALU.mult), [bqr, bmk], [bqrd])
                                qA, bqA, kA, bkA, qS, bqS, kW, bkW = qr, bqr, kr, bkr, qrd, bqrd, kr, bkr

                            def do_tile(t):
                                lc = t.col - g0
                                nt = t.nt
                                tsl = slice(t.col, t.col + nt)
                                vbf, bvbf = (vbf_s if t.samp else vbf_r).next()
                                th, bth = th_r.next()
                                zs, bzs = (zs_s if t.samp else zs_r).next()
                                if gla:
                                    biv, bbv = psum()
                                    mm(banks[biv][0:nt, 0:512], [(hT[:, kc, tsl], wv[:, kc, 256:768]) for kc in range(8)], [bhT, bwu], [bbv])
                                    vsrc = banks[biv][0:nt, 0:256]
                                    zsrc = banks[biv][0:nt, 256:512]
                                    bbz = bbv
                                else:
                                    biv, bbv = psum()
                                    mm(banks[biv][0:nt, 0:512], [(hT[:, kc, tsl], wv[:, kc, 512:1024]) for kc in range(8)], [bhT, bwu], [bbv])
                                    vsrc = banks[biv][0:nt, 0:512]
                                    biz, bbz = psum()
                                    mm(banks[biz][0:nt, 0:512], [(hT[:, kc, tsl], wv[:, kc, 1024:1536]) for kc in range(8)], [bhT, bwu], [bbz])
                                    zsrc = banks[biz][0:nt, 0:512]
                                A(lambda e: e.activation(out=vbf[0:nt, 0:dv], in_=vsrc, func=AF.Copy), [bbv], [bvbf])
                                if gla:
                                    A(lambda e: e.activation(out=th[0:nt, 0:dv], in_=zsrc, func=AF.Tanh, scale=0.5), [bbz], [bth])
                                    V(lambda e: e.scalar_tensor_tensor(
                                        out=zs[0:nt, 0:dv], in0=th[0:nt, 0:dv], scalar=1.0, in1=zsrc, op0=ALU.add, op1=ALU.mult), [bth, bbz], [bzs])
                                else:
                                    A(lambda e: e.activation(out=zs[0:nt, 0:dv], in_=zsrc, func=AF.Silu), [bbz], [bzs])
                                bia, bba = psum()
                                mm(banks[bia][0:nt, 0:nt], [(kA[:, c, lc:lc + nt], qA[:, c, lc:lc + nt]) for c in range(dkc)], [bkA, bqA], [bba])
                                am, bam = (am_s if t.samp else am_r).next()
                                if gla:
                                    msk = tri_sb[0:nt, t.kind, 0:nt]
                                    bmsk = bconst
                                else:
                                    msk = mk[0:nt, 0, t.kind, 0:nt]
                                    bmsk = bmk
                                V(lambda e: e.tensor_tensor(out=am[0:nt, 0:nt], in0=banks[bia][0:nt, 0:nt], in1=msk, op=ALU.mult), [bba, bmsk], [bam])
                                osl = {}

                                def issue_o():
                                    bio, bbo = psum_o()
                                    pairs = [(am[0:nt, 0:nt], vbf[0:nt, 0:dv])]
                                    Rl = [bam, bvbf]
                                    if (not t.samp) and us["valid"]:
                                        if us["sbf"] is None:
                                            init_sbf()
                                        sbfc = us["sbf"]
                                        pairs += [(qS[:, c, lc:lc + nt], sbfc[:, c, 0:dv]) for c in range(dkc)]
                                        Rl += [bqS, us["bsbf"]]
                                    mm(banks[bio][0:nt, 0:dv], pairs, Rl, [bbo])
                                    osl["bio"], osl["bbo"] = bio, bbo
                                bit, bbt = psum()
                                tr([(banks_bf[bit][0:nt, c * 128:(c + 1) * 128], kW[:, c, lc:lc + nt], ident_b[:, :]) for c in range(dkc)],
                                   [bkW, bconst], [bbt])
                                kw, bkw = (kw_s if t.samp else kw_r).next()
                                if gla:
                                    A(lambda e: e.activation(out=kw[0:nt, 0:128], in_=banks_bf[bit][0:nt, 0:128], func=AF.Copy), [bbt], [bkw])
                                else:
                                    A(lambda e: e.activation(
                                        out=kw[0:nt, 0:256], in_=banks_bf[bit][0:nt, 0:256], func=AF.Identity,
                                        scale=rdw_sb[0:nt, 2 * h + t.kind:2 * h + t.kind + 1]), [bbt, bconst], [bkw])
                                yield
                                if not t.samp:
                                    issue_o()
                                    o_src = banks[osl["bio"]][0:nt, 0:dv]
                                    bo_src = osl["bbo"]
                                    nsbf, bnsbf = sbf_r.next()
                                    valid = us["valid"]

                                    def do_c(c):
                                        bi, bb = psum()
                                        mm(banks[bi][:, 0:dv], [(kw[0:nt, c * 128:(c + 1) * 128], vbf[0:nt, 0:dv])], [bkw, bvbf], [bb])
                                        Sd = S_gla[:, h, :] if gla else S_ret[:, h, c, :]
                                        if not valid:
                                            V(lambda e: e.tensor_copy(out=Sd, in_=banks[bi][:, 0:dv]), [bb], [bS])
                                        elif gla:
                                            dsd, bdsd = dsd_of[t.key]
                                            V(lambda e: e.scalar_tensor_tensor(
                                                out=Sd, in0=Sd, scalar=dsd[:, 0:1], in1=banks[bi][:, 0:dv], op0=ALU.mult, op1=ALU.add),
                                              [bb, bdsd, bS], [bS])
                                        else:
                                            V(lambda e: e.scalar_tensor_tensor(
                                                out=Sd, in0=Sd, scalar=dchunk[h][0], in1=banks[bi][:, 0:dv], op0=ALU.mult, op1=ALU.add),
                                              [bb, bS], [bS])
                                        A(lambda e: e.activation(out=nsbf[:, c, 0:dv], in_=Sd, func=AF.Copy), [bS], [bnsbf])
                                    for c in range(dkc):
                                        do_c(c)
                                    us["sbf"], us["bsbf"] = nsbf, bnsbf
                                    us["valid"] = True
                                    if last_pass and t.key == 15:
                                        if gla:
                                            P.dma("sync", "stp", lambda e, inc: inc(e.dma_start(out=sgp[h, :, :], in_=S_gla[:, h, :])), R=[bS])
                                        else:
                                            P.dma("sync", "stp", lambda e, inc: inc(e.dma_start(
                                                out=srp[h, :, :].rearrange("(c p) v -> p c v", p=128), in_=S_ret[:, h, :, :])), R=[bS])
                                else:
                                    loads = {}
                                    lanes_of = {}
                                    bbuf[7] = Buf("pb7", prev=bbuf[7])
                                    b7 = bbuf[7]

                                    def issue_load(i):
                                        if gla and i % 4 != 0:
                                            loads[i] = loads[i - 1]
                                            lanes_of[i] = lanes_of[i - 1]
                                            return
                                        sin_, bsin = sin_r.next()
                                        li = lane_ctr["sin"] % 4
                                        lane_ctr["sin"] += 1
                                        lanes_of[i] = li
                                        if gla:
                                            P.dma("sync", f"sin{li}", lambda e, inc: inc(e.dma_start(
                                                out=sin_[:, :, :].rearrange("p c (s v) -> p (c s) v", v=256),
                                                in_=sg_in[i:i + 4, h, :, :].rearrange("s d v -> d s v"))), W=[bsin])
                                        else:
                                            P.dma("sync", f"sin{li}", lambda e, inc: inc(e.dma_start(
                                                out=sin_[:, :, :], in_=sr_in[i, h, :, :].rearrange("(c p) v -> p c v", p=128))), W=[bsin])
                                        loads[i] = (sin_, bsin)
                                    if gla:
                                        for i0 in range(NS):
                                            issue_load(i0)
                                    else:
                                        issue_load(0)
                                        issue_load(1)
                                        issue_load(2)

                                    def st_ap(sin_, i):
                                        if gla:
                                            return sin_[:, :, :].rearrange("p c (s v) -> p (c s) v", v=256)[:, i % 4, :]
                                        return None

                                    preps = {}

                                    def prep_seq(i):
                                        sin_, bsin = loads[i]
                                        qm, bqm = qm_r.next()
                                        V(lambda e: e.tensor_tensor(
                                            out=qm[:, 0:dkc, :], in0=qS[:, 0:dkc, lc:lc + 64],
                                            in1=cmask_sb[:, i:i + 1, :].to_broadcast([128, dkc, 64]), op=ALU.mult), [bqS, bconst], [bqm])
                                        sinb, bsinb = sinb_r.next()
                                        if gla:
                                            A(lambda e: e.activation(out=sinb[:, 0, 0:256], in_=st_ap(sin_, i), func=AF.Copy), [bsin], [bsinb])
                                        else:
                                            A(lambda e: e.activation(out=sinb[:, 0:dkc, 0:dv], in_=sin_[:, 0:dkc, 0:dv], func=AF.Copy), [bsin], [bsinb])
                                        km, bkm = km_r.next()
                                        V(lambda e: e.tensor_scalar(
                                            out=km[:, 0:dkc * 128], in0=kw[0:64, 0:dkc * 128], scalar1=rowm_sb[0:64, i:i + 1], scalar2=None,
                                            op0=ALU.mult), [bkw, bconst], [bkm])
                                        preps[i] = (qm, bqm, sinb, bsinb, km, bkm)
                                    prep_seq(0)

                                    def do_seq(i):
                                        sin_, bsin = loads[i]
                                        if i + 1 < NS:
                                            prep_seq(i + 1)
                                        qm, bqm, sinb, bsinb, km, bkm = preps.pop(i)
                                        mm_pairs = [(qm[:, c, :], sinb[:, c, 0:dv]) for c in range(dkc)]

                                        def fn_oi(e):
                                            for c, (l, r) in enumerate(mm_pairs):
                                                ins = e.matmul(out=banks[7][0:64, 0:dv], lhsT=l, rhs=r,
                                                               start=(i == 0 and c == 0), stop=(i == NS - 1 and c == dkc - 1))
                                            return ins
                                        P.op("tensor", fn_oi, R=[bqm, bsinb], W=[b7])
                                        def do_sc(c):
                                            bi, bb = psum()
                                            mm(banks[bi][:, 0:dv], [(km[:, c * 128:(c + 1) * 128], vbf[0:64, 0:dv])], [bkm, bvbf], [bb])
                                            if gla:
                                                dsd, bdsd = dsd_of[t.key]
                                                V(lambda e: e.scalar_tensor_tensor(
                                                    out=st_ap(sin_, i), in0=st_ap(sin_, i), scalar=dsd[:, i:i + 1], in1=banks[bi][:, 0:dv],
                                                    op0=ALU.mult, op1=ALU.add), [bb, bsin, bdsd], [bsin])
                                            else:
                                                V(lambda e: e.scalar_tensor_tensor(
                                                    out=sin_[:, c, 0:dv], in0=sin_[:, c, 0:dv], scalar=dchunk[h][1], in1=banks[bi][:, 0:dv],
                                                    op0=ALU.mult, op1=ALU.add), [bb, bsin], [bsin])
                                        for c in range(dkc):
                                            do_sc(c)
                                        lo = lanes_of[i]
                                        if gla:
                                            if i % 4 == 3:
                                                P.dma("sync", f"sout{lo}", lambda e, inc: inc(e.dma_start(
                                                    out=sgs[i - 3:i + 1, h, :, :].rearrange("s d v -> d s v"),
                                                    in_=sin_[:, :, :].rearrange("p c (s v) -> p (c s) v", v=256))), R=[bsin])
                                        else:
                                            P.dma("sync", f"sout{lo}", lambda e, inc: inc(e.dma_start(
                                                out=srs[i, h, :, :].rearrange("(c p) v -> p c v", p=128), in_=sin_[:, :, :])), R=[bsin])
                                        if (not gla) and i + 3 < NS:
                                            issue_load(i + 3)
                                    for i in range(NS):
                                        do_seq(i)
                                        if (i + 1) % seq_per_step[0] == 0 or i == NS - 1:
                                            yield
                                    A(lambda e: e.activation(out=oint[:, 0:dv], in_=banks[7][0:64, 0:dv], func=AF.Copy), [b7], [boint])
                                    issue_o()
                                    bio_s, bbo_s = osl["bio"], osl["bbo"]
                                    V(lambda e: e.tensor_tensor(out=oint[:, 0:dv], in0=banks[bio_s][0:64, 0:dv], in1=oint[:, 0:dv], op=ALU.add),
                                      [bbo_s, boint], [boint])
                                    o_src = oint[:, 0:dv]
                                    bo_src = boint
                                yield
                                stt_, bst = st_r.next()
                                on, bon = on_r.next()
                                if gla:
                                    junk, bjunk = t512.next()
                                    A(lambda e: e.activation(
                                        out=junk[0:nt, 0:dv], in_=o_src, func=AF.Square, accum_out=stt_[0:nt, 0:1]), [bo_src], [bjunk, bst])
                                    V(lambda e: e.tensor_scalar(
                                        out=stt_[0:nt, 1:2], in0=stt_[0:nt, 0:1], scalar1=1.0 / dv, scalar2=EPS, op0=ALU.mult, op1=ALU.add), [bst], [bst])
                                    G(lambda e: e.tensor_tensor(out=stt_[0:nt, 2:3], in0=stt_[0:nt, 1:2], in1=mhalf[0:nt, :], op=ALU.pow),
                                      [bst, bconst], [bst])
                                    yield
                                    V(lambda e: e.scalar_tensor_tensor(
                                        out=on[0:nt, 0:dv], in0=o_src, scalar=stt_[0:nt, 2:3], in1=zs[0:nt, 0:dv], op0=ALU.mult, op1=ALU.mult),
                                      [bo_src, bst, bzs], [bon])
                                else:
                                    V(lambda e: e.bn_stats(out=stt_[0:nt, 0:6], in_=o_src), [bo_src], [bst])
                                    V(lambda e: e.bn_aggr(out=stt_[0:nt, 6:8], in_=stt_[0:nt, 0:6]), [bst], [bst])
                                    V(lambda e: e.tensor_scalar(
                                        out=stt_[0:nt, 8:9], in0=stt_[0:nt, 7:8], scalar1=EPS, scalar2=None, op0=ALU.add), [bst], [bst])
                                    G(lambda e: e.tensor_tensor(out=stt_[0:nt, 9:10], in0=stt_[0:nt, 8:9], in1=mhalf[0:nt, :], op=ALU.pow),
                                      [bst, bconst], [bst])
                                    yield
                                    onm, bonm = t512.next()
                                    V(lambda e: e.tensor_scalar(
                                        out=onm[0:nt, :], in0=o_src, scalar1=stt_[0:nt, 6:7], scalar2=stt_[0:nt, 9:10],
                                        op0=ALU.subtract, op1=ALU.mult), [bo_src, bst], [bonm])
                                    G(lambda e: e.tensor_tensor(out=on[0:nt, :], in0=onm[0:nt, :], in1=zs[0:nt, :], op=ALU.mult), [bonm, bzs], [bon])
                                yield
                                bix, bbx = psum()
                                tr([(banks[bix][:, j * 128:j * 128 + nt], on[0:nt, j * 128:(j + 1) * 128], ident_f[0:nt, 0:nt]) for j in range(nch)],
                                   [bon, bconst], [bbx])
                                for j in range(nch):
                                    A(lambda e, j=j, oc=ochunk0 + j: e.activation(
                                        out=oT[:, oc, tsl], in_=banks[bix][:, j * 128:j * 128 + nt], func=AF.Identity, scale=gnT[:, oc:oc + 1]),
                                      [bbx, bconst], [boT[u]])
                            return do_tile
                        return do_group
                    ps = {"s2": None, "s3": None, "s4": None}

                    def pstep(g, adv=None):
                        g3 = ps["s3"]
                        if g3 is not None:
                            next(g3)
                        if adv:
                            adv(0)
                        if g is not None:
                            next(g)
                        if adv:
                            adv(1)
                        if g3 is not None:
                            next(g3)
                        if adv:
                            adv(2)
                        if ps["s2"] is not None:
                            next(ps["s2"])
                        if adv:
                            adv(3)
                        if ps["s4"] is not None:
                            for _ in ps["s4"]:
                                pass
                        if adv:
                            adv(4)
                        ps["s4"] = g3
                        ps["s3"] = ps["s2"]
                        ps["s2"] = g
                    if not has_s:
                        upairs = [(u, gi) for u in range(8) for gi in range(len(groups))]
                        ufn = {0: do_unit(0)}
                        gfn = {(0, 0): ufn[0](groups[0])}
                        for j, (u, gi) in enumerate(upairs):
                            grp = groups[gi]
                            if gi == 0 and u + 1 < 8:
                                ufn[u + 1] = do_unit(u + 1)
                            for i, t in enumerate(grp):
                                last = (i == len(grp) - 1 and j + 1 < len(upairs))
                                if last:
                                    gfn[upairs[j + 1]] = ufn[upairs[j + 1][0]](groups[upairs[j + 1][1]])
                                pstep(gfn[(u, gi)](t))
                    else:
                        sgrp = [g for g in groups if g[0].samp][0]
                        pgrps = [g for g in groups if not g[0].samp]
                        ptl = [(gi, t) for gi, g in enumerate(pgrps) for t in g]
                        seq_per_step[0] = 1
                        quota = -(-NS // len(ptl))
                        sched = [quota // 5] * 5
                        for sl in [1, 3, 0, 2, 4][:quota % 5]:
                            sched[sl] += 1

                        def wrap(g):
                            yield
                            yield from g
                        ufn = {0: do_unit(0)}
                        for u in range(8):
                            if u + 1 < 8:
                                ufn[u + 1] = do_unit(u + 1)
                            gen_s = ufn[u](sgrp)(sgrp[0])
                            next(gen_s)
                            gf = {}
                            st_ = {"done": 0, "first": True}

                            def adv(slot, gen_s=gen_s, st_=st_):
                                sc = sched
                                if st_["first"]:
                                    sc = [0, 0, 0, quota - quota // 2, quota // 2]
                                for _ in range(sc[slot]):
                                    if st_["done"] < NS:
                                        next(gen_s)
                                        st_["done"] += 1
                            for idx, (gi, t) in enumerate(ptl):
                                if gi not in gf:
                                    gf[gi] = ufn[u](pgrps[gi])
                                pstep(gf[gi](t), adv)
                                st_["first"] = False
                            while st_["done"] < NS:
                                next(gen_s)
                                st_["done"] += 1
                            pstep(wrap(gen_s))
                    pre_fl[0] = issue_gate_w(0)
                    pre_fl[1] = issue_gate_w(1)
                    pstep(None)
                    pstep(None)
                    pstep(None)
                    P.barrier(skip=("w0", "w1"))

                with ExitStack() as stf:
                    mT = sb(f"mT{pi}", [128, 8, NP], BF16, stf)
                    bmT = Buf("mT")
                    gate_p = sb(f"gatep{pi}", [128, D], F32, stf)
                    gate_s = sb(f"gates{pi}", [64, D], F32, stf) if has_s else None
                    bgate = Buf("gate")
                    lng = sb(f"lng{pi}", [128, D], F32, stf)
                    lnb = sb(f"lnb{pi}", [128, D], F32, stf)
                    bln = Buf("ln")
                    P.dma("sync", "ln", lambda e, inc: inc(e.dma_start(out=lng[:, :], in_=ln_g.partition_broadcast(128))), W=[bln])
                    P.dma("sync", "ln", lambda e, inc: inc(e.dma_start(out=lnb[:, :], in_=ln_b.partition_broadcast(128))), W=[bln])
                    P.dma("sync", "ln", lambda e, inc: inc(e.dma_start(out=gate_p[:, :], in_=b_ada[2 * D:3 * D].partition_broadcast(128))), W=[bln])
                    if has_s:
                        P.dma("sync", "ln", lambda e, inc: inc(e.dma_start(out=gate_s[:, :], in_=b_ada[2 * D:3 * D].partition_broadcast(64))), W=[bln])
                    P.seal("ln", [bln])
                    V(lambda e: e.tensor_scalar(out=gate_p[:, :], in0=gate_p[:, :], scalar1=0.5, scalar2=None, op0=ALU.mult), [bln], [bln])
                    if has_s:
                        V(lambda e: e.tensor_scalar(out=gate_s[:, :], in0=gate_s[:, :], scalar1=0.5, scalar2=None, op0=ALU.mult), [bln], [bln])
                    f4 = Ring(nc, stf, f"f4{pi}", [128, 512], F32, 4)
                    xringf = Ring(nc, stf, f"xtf{pi}", [128, D], F32, 2)
                    ft = Ring(nc, stf, f"ft{pi}", [128, D], F32, 2)
                    yr = Ring(nc, stf, f"yr{pi}", [128, D], F32, 2)
                    stf_r = Ring(nc, stf, f"stf{pi}", [128, 24], F32, 2)
                    ylane = [0]

                    fl = {"issued": len(pre_fl), "slots": dict(pre_fl)}

                    def fl_gate(j):
                        def f(wsl):
                            wv = wslot[wsl][:, 0:4096].rearrange("p (k n) -> p k n", k=8)
                            P.dma("gpsimd", f"w{wsl}", lambda e, inc: inc(e.dma_start(
                                out=wv, in_=w_ada_v[:, :, 2 * D + j * 512:2 * D + (j + 1) * 512])), W=[bw[wsl]])
                            return wv
                        return f

                    def fl_pair(m):
                        def f(wsl):
                            wv = wslot[wsl][:, 0:40 * 256].rearrange("p (k n) -> p k n", n=256)

                            def wload(e, inc):
                                inc(e.dma_start(out=wv[:, 0:8, :], in_=w_in_v[:, :, MG + m * 128:MG + (m + 2) * 128]))
                                inc(e.dma_start(out=wv[:, 8:16, :], in_=w_in_v[:, :, MR + m * 128:MR + (m + 2) * 128]))
                                inc(e.dma_start(out=wv[:, 16:24, :], in_=wbg_v[:, :, m * 128:(m + 2) * 128]))
                                inc(e.dma_start(out=wv[:, 24:40, :], in_=wbr_v[:, :, m * 128:(m + 2) * 128]))
                            P.dma("gpsimd", f"w{wsl}", wload, W=[bw[wsl]], n=4)
                            return wv
                        return f

                    def fl_out():
                        def f(wsl):
                            wv = wslot[wsl][:, 0:8192].rearrange("p (k n) -> p k n", k=8)
                            P.dma("gpsimd", f"w{wsl}", lambda e, inc: inc(e.dma_start(out=wv, in_=w_out_v[:, :, :])), W=[bw[wsl]])
                            return wv
                        return f
                    fl_list = [fl_gate(0), fl_gate(1), fl_pair(0), fl_pair(2), fl_pair(4), fl_pair(6), fl_out()]

                    def fl_issue_upto(k):
                        while fl["issued"] <= k and fl["issued"] < len(fl_list):
                            j = fl["issued"]
                            wsl = next_w()
                            fl["slots"][j] = (wsl, fl_list[j](wsl))
                            fl["issued"] += 1

                    def fl_get(j):
                        fl_issue_upto(j + 1)
                        return fl["slots"][j]

                    def do_gate(j):
                        wsl, wv = fl_get(j)
                        bi, bb = psum7()
                        mm(banks[bi][:, 0:512], [(cT[:, kc, 64:192], wv[:, kc, :]) for kc in range(8)], [bcT, bw[wsl]], [bb])
                        V(lambda e: e.scalar_tensor_tensor(
                            out=gate_p[:, j * 512:(j + 1) * 512], in0=banks[bi][:, 0:512], scalar=0.5, in1=gate_p[:, j * 512:(j + 1) * 512],
                            op0=ALU.mult, op1=ALU.add), [bb, bln], [bgate])
                        if has_s:
                            bi2, bb2 = psum7()
                            mm(banks[bi2][0:64, 0:512], [(cT[:, kc, 0:64], wv[:, kc, :]) for kc in range(8)], [bcT, bw[wsl]], [bb2])
                            V(lambda e: e.scalar_tensor_tensor(
                                out=gate_s[:, j * 512:(j + 1) * 512], in0=banks[bi2][0:64, 0:512], scalar=0.5, in1=gate_s[:, j * 512:(j + 1) * 512],
                                op0=ALU.mult, op1=ALU.add), [bb2, bln], [bgate])
                    for j in range(2):
                        do_gate(j)

                    wcur = {}

                    def do_m(m):
                        mo = (m % 2) * 128
                        wsl, wv = fl_get(2 + m // 2)

                        def do_fg(grp):
                            g0 = grp[0].col
                            NG = sum(t.nt for t in grp)
                            gs = slice(g0, g0 + NG)
                            b_mg, bb_mg = psum7()
                            mm(banks[b_mg][:, 0:NG], [(wv[:, kc, mo:mo + 128], hT[:, kc, gs]) for kc in range(8)], [bw[wsl], bhT], [bb_mg])
                            b_pg, bb_pg = psum7()
                            mm(banks[b_pg][:, 0:NG], [(wv[:, 16 + kc, mo:mo + 128], oT[:, kc, gs]) for kc in range(8)], [bw[wsl]] + boT[0:4], [bb_pg])
                            b_mr, bb_mr = psum7()
                            mm(banks[b_mr][:, 0:NG], [(wv[:, 8 + kc, mo:mo + 128], hT[:, kc, gs]) for kc in range(8)], [bw[wsl], bhT], [bb_mr])
                            b_pr, bb_pr = psum7()
                            mm(banks[b_pr][:, 0:NG], [(wv[:, 24 + kc, mo:mo + 128], oT[:, 8 + kc, gs]) for kc in range(16)], [bw[wsl]] + boT[4:8], [bb_pr])
                            tg, btg = f4.next()
                            trr, btr = f4.next()
                            A(lambda e: e.activation(out=tg[:, 0:NG], in_=banks[b_mg][:, 0:NG], func=AF.Tanh, scale=0.5), [bb_mg], [btg])
                            A(lambda e: e.activation(out=trr[:, 0:NG], in_=banks[b_mr][:, 0:NG], func=AF.Tanh, scale=0.5), [bb_mr], [btr])
                            V(lambda e: e.scalar_tensor_tensor(
                                out=tg[:, 0:NG], in0=tg[:, 0:NG], scalar=1.0, in1=banks[b_pg][:, 0:NG], op0=ALU.add, op1=ALU.mult), [btg, bb_pg], [btg])
                            V(lambda e: e.scalar_tensor_tensor(
                                out=trr[:, 0:NG], in0=trr[:, 0:NG], scalar=1.0, in1=banks[b_pr][:, 0:NG], op0=ALU.add, op1=ALU.mult), [btr, bb_pr], [btr])
                            G(lambda e: e.tensor_tensor(out=mT[:, m, gs], in0=tg[:, 0:NG], in1=trr[:, 0:NG], op=ALU.add), [btg, btr], [bmT])
                        for grp in groups:
                            do_fg(grp)
                    for m in range(8):
                        do_m(m)
                    wslo, wvo = fl_get(6)
                    if pi + 1 < len(PASSES):
                        carry["u0"] = issue_unit_w(0)

                    def do_ft(t):
                        nt = t.nt
                        tsl = slice(t.col, t.col + nt)
                        xt, bx = xringf.next()
                        xsrc = x_s[0:64, :] if t.samp else x_p[t.tok0:t.tok0 + 128, :]
                        xi = xcnt[0] % 2
                        xcnt[0] += 1
                        P.dma("sync", f"x{xi}", lambda e, inc: inc(e.dma_start(out=xt[0:nt, :], in_=xsrc)), W=[bx])
                        gt = gate_s if t.samp else gate_p
                        vt, bvt = ft.next()
                        for n2 in range(2):
                            bi, bb = psum7()
                            mm(banks[bi][0:nt, 0:512], [(mT[:, kc, tsl], wvo[:, kc, n2 * 512:(n2 + 1) * 512]) for kc in range(8)], [bmT, bw[wslo]], [bb])
                            V(lambda e, bi=bi, n2=n2: e.tensor_tensor(
                                out=vt[0:nt, n2 * 512:(n2 + 1) * 512], in0=banks[bi][0:nt, 0:512], in1=gt[0:nt, n2 * 512:(n2 + 1) * 512], op=ALU.mult),
                              [bb, bgate], [bvt])
                        V(lambda e: e.scalar_tensor_tensor(
                            out=vt[0:nt, :], in0=xt[0:nt, :], scalar=ALPHA, in1=vt[0:nt, :], op0=ALU.mult, op1=ALU.add), [bx, bvt], [bvt])
                        sf, bsf = stf_r.next()
                        V(lambda e: e.bn_stats(out=sf[0:nt, 0:6], in_=vt[0:nt, 0:512]), [bvt], [bsf])
                        V(lambda e: e.bn_stats(out=sf[0:nt, 6:12], in_=vt[0:nt, 512:1024]), [bvt], [bsf])
                        V(lambda e: e.bn_aggr(out=sf[0:nt, 12:14], in_=sf[0:nt, 0:12]), [bsf], [bsf])
                        V(lambda e: e.tensor_scalar(out=sf[0:nt, 14:15], in0=sf[0:nt, 13:14], scalar1=EPS, scalar2=None, op0=ALU.add), [bsf], [bsf])
                        G(lambda e: e.tensor_tensor(out=sf[0:nt, 15:16], in0=sf[0:nt, 14:15], in1=mhalf[0:nt, :], op=ALU.pow), [bsf, bconst], [bsf])
                        V(lambda e: e.scalar_tensor_tensor(
                            out=sf[0:nt, 16:17], in0=sf[0:nt, 12:13], scalar=-1.0, in1=sf[0:nt, 15:16], op0=ALU.mult, op1=ALU.mult), [bsf], [bsf])
                        yt, byt = yr.next()
                        A(lambda e: e.activation(
                            out=yt[0:nt, :], in_=vt[0:nt, :], func=AF.Identity, scale=sf[0:nt, 15:16], bias=sf[0:nt, 16:17]), [bvt, bsf], [byt])
                        V(lambda e: e.tensor_tensor(out=yt[0:nt, :], in0=yt[0:nt, :], in1=lng[0:nt, :], op=ALU.mult), [byt, bln], [byt])
                        G(lambda e: e.tensor_tensor(out=yt[0:nt, :], in0=yt[0:nt, :], in1=lnb[0:nt, :], op=ALU.add), [byt, bln], [byt])
                        ydst = y_s[0:64, :] if t.samp else y_p[t.tok0:t.tok0 + 128, :]
                        yl = ylane[0] % 2
                        ylane[0] += 1
                        P.dma("sync", f"y{yl}", lambda e, inc: inc(e.dma_start(out=ydst, in_=yt[0:nt, :])), R=[byt])
                    for t in tiles:
                        do_ft(t)
                    P.barrier(skip=("w0", "w1"))

        for pi, pkeys in enumerate(PASSES):
            do_pass(pi, pkeys)
        P.barrier()
        P.emit()
    return nc, dbg_outs


def _constants():
    half = 128
    inv_freq = (10000.0 ** (-np.arange(half, dtype=np.float32) / np.float32(half))).astype(np.float32)
    pos = np.concatenate([np.arange(TP, dtype=np.int32),
                          np.tile(16384 + np.arange(TS, dtype=np.int32), NS)]).astype(np.float32)
    ang = (pos[:, None] * inv_freq[None, :]).astype(np.float32)
    rope = np.stack([np.cos(ang).T, np.sin(ang).T], axis=1).astype(np.float32)
    lg = np.log1p(-np.exp2(-5.0 - np.arange(4, dtype=np.float32))).astype(np.float32)
    rmask = np.zeros((128, 4, 2, 128), np.float32)
    rdread = np.zeros((128, 4, 2, 128), np.float32)
    rdwrite = np.zeros((128, 8), np.float32)
    idx = np.arange(128)
    for h in range(4):
        for kind, L in ((0, 128), (1, 4)):
            n = 128 if kind == 0 else 64
            s = idx[:n, None]
            t = idx[None, :n]
            same = (s // L) == (t // L)
            diff = (t - s).astype(np.float32)
            m = np.where(same & (t >= s), np.exp(np.maximum(diff, 0.0) * lg[h]), 0.0).astype(np.float32) / np.float32(16.0)
            rmask[:n, h, kind, :n] = m
            rdread[:, h, kind, :n] = np.exp(((idx[:n] % L) + 1.0).astype(np.float32) * lg[h])[None, :]
            rdwrite[:n, 2 * h + kind] = np.exp((L - 1.0 - (idx[:n] % L)).astype(np.float32) * lg[h]) / np.float32(16.0)
    tri = np.zeros((128, 2, 272), np.float32)
    for kind, L in ((0, 128), (1, 4)):
        n = 128 if kind == 0 else 64
        s = idx[:n, None]
        t = idx[None, :n]
        same = (s // L) == (t // L)
        tri[:n, kind, 0:n] = (same & (s <= t)).astype(np.float32)
        tri[:n, kind, 128:128 + n] = (same & (s > t)).astype(np.float32)
        for i in range(16):
            tri[:n, kind, 256 + i] = ((idx[:n] // L) == i).astype(np.float32)
    colmask = np.zeros((128, 16, 64), np.float32)
    rowmask = np.zeros((128, 16), np.float32)
    for i in range(16):
        colmask[:, i, 4 * i:4 * i + 4] = 1.0
        rowmask[4 * i:4 * i + 4, i] = 1.0
    return dict(rope=rope, rmask=rmask, rdread=rdread, rdwrite=rdwrite, tri=tri, colmask=colmask,
                rowmask=rowmask, ident=np.eye(128, dtype=np.float32))


def _in_maps(inp):
    consts = _constants()
    f = lambda a: np.ascontiguousarray(np.asarray(a, dtype=np.float32))
    shared = dict(
        w_ada=f(inp["w_ada"][0]), b_ada=f(inp["b_ada"][0]), w_in=f(inp["w_in"][0]), w_lr2=f(inp["w_lr2"][0]),
        b_lr2=f(inp["b_lr2"][0]), gla_norm_g=f(inp["gla_norm_g"][0]), ret_norm_g=f(inp["ret_norm_g"][0]),
        w_branch_gla=f(inp["w_branch_gla"][0]), w_branch_ret=f(inp["w_branch_ret"][0]), w_out=f(inp["w_out"][0]),
        ln_g=f(inp["ln_g"][0]), ln_b=f(inp["ln_b"][0]), **consts)
    maps = []
    for c in range(NCORES):
        sl = slice(NS * c, NS * (c + 1))
        c_all = np.concatenate([np.repeat(np.asarray(inp["c_sample"][sl]), TS, axis=0),
                                np.repeat(np.asarray(inp["c_prompt"][c:c + 1]), 128, axis=0)], axis=0)
        m = dict(shared)
        m.update(x_p=f(inp["x_prompt"][c]), x_s=f(np.asarray(inp["x_sample"][sl]).reshape(NS * TS, D)), c_all=f(c_all),
                 sg_in=f(inp["state_gla"][0, sl]), sr_in=f(inp["state_ret"][0, sl]))
        maps.append(m)
    return maps


def kernel(**inputs):
    nc, _ = build_program()
    maps = _in_maps(inputs)
    res = run_bass_kernel_spmd(nc, maps, core_ids=list(range(NCORES)))
    r = res.results
    y_p = np.stack([r[c]["y_p"] for c in range(NCORES)], axis=0)
    y_s = np.concatenate([r[c]["y_s"].reshape(NS, TS, D) for c in range(NCORES)], axis=0)
    sgp = np.stack([r[c]["sgp"] for c in range(NCORES)], axis=0)[None]
    srp = np.stack([r[c]["srp"] for c in range(NCORES)], axis=0)[None]
    sgs = np.concatenate([r[c]["sgs"] for c in range(NCORES)], axis=0)[None]
    srs = np.concatenate([r[c]["srs"] for c in range(NCORES)], axis=0)[None]
    return (y_p.astype(np.float32), y_s.astype(np.float32), sgp.astype(np.float32), srp.astype(np.float32),
            sgs.astype(np.float32), srs.astype(np.float32))
```

```python
import math
from contextlib import ExitStack

import numpy as np
import concourse.bass as bass
import concourse.mybir as mybir
from concourse.bass_utils import run_bass_kernel_spmd

F32 = mybir.dt.float32
BF16 = mybir.dt.bfloat16
AF = mybir.ActivationFunctionType
ALU = mybir.AluOpType

NCORES = 8
D = 1024
TP = 2048
NS = 16
TS = 4
NTOK = TP + NS * TS
GQ, GK, GV, GZ, LR, RQ, RK, RV, RZ, MG, MR = 0, 512, 1024, 2048, 3072, 3088, 4112, 5136, 7184, 9232, 10256
DIN = 11280
ALPHA = 2.0 ** 0.25
EPS = 1e-5
PASSES = [list(range(0, 6)), list(range(6, 13)), list(range(13, 16)) + ["s"]]
WSLOT = 8 * 1536


class Buf:
    __slots__ = ("w", "r", "name", "dead")

    def __init__(self, name="", prev=None):
        self.w = None
        self.r = {}
        self.name = name
        self.dead = False
        if prev is not None:
            self.w = prev.w
            self.r = prev.r
            prev.dead = True


class Prog:
    CE = ("tensor", "vector", "scalar", "gpsimd")
    ENG = ("tensor", "vector", "scalar", "gpsimd", "sync")

    def __init__(self, nc, stack):
        self.nc = nc
        self.stack = stack
        self.q = {e: [] for e in self.ENG}
        self.sems = []
        self.cnt = []
        self.esem = {}
        for e in self.CE:
            self.esem[e] = self._newsem("s_" + e)
        self.seen = {e: {} for e in self.ENG}
        self.lanes = {}

    def _newsem(self, name):
        h = self.stack.enter_context(self.nc.semaphore(name))
        self.sems.append(h)
        self.cnt.append(0)
        return len(self.sems) - 1

    def lane(self, name):
        if name not in self.lanes:
            self.lanes[name] = self._newsem("l_" + name)
        return self.lanes[name]

    def _waits(self, eng, R, W):
        need = {}
        for b in list(R) + list(W):
            assert not b.dead, f"use of recycled buffer {b.name}"

        def add(s, v):
            if need.get(s, 0) < v:
                need[s] = v
        for b in R:
            if b.w is not None:
                add(*b.w)
        for b in W:
            if b.w is not None:
                add(*b.w)
            for s, v in b.r.items():
                add(s, v)
        out = []
        seen = self.seen[eng]
        own = self.esem.get(eng) if eng == "tensor" else None
        for s, v in need.items():
            if s == own:
                continue
            if seen.get(s, 0) < v:
                seen[s] = v
                out.append((s, v))
        return out

    def _commit(self, ev, R, W):
        s, v = ev
        for b in R:
            if b.r.get(s, 0) < v:
                b.r[s] = v
        for b in W:
            b.w = ev
            b.r = {}

    def op(self, eng, fn, R=(), W=()):
        waits = self._waits(eng, R, W)
        s = self.esem[eng]
        self.cnt[s] += 1
        ev = (s, self.cnt[s])
        self.q[eng].append((waits, fn, (s, 1), False))
        self._commit(ev, R, W)
        return ev

    def dma(self, eng, lane, fn, R=(), W=(), n=1):
        waits = self._waits(eng, R, W)
        s = self.lane(lane)
        self.cnt[s] += 16 * n
        ev = (s, self.cnt[s])
        self.q[eng].append((waits, fn, (s, 16), True))
        self._commit(ev, R, W)
        return ev

    def seal(self, lane, bufs):
        s = self.lane(lane)
        for b in bufs:
            b.w = (s, self.cnt[s])

    def barrier(self, skip=()):
        sk = {self.lanes[n] for n in skip if n in self.lanes}
        allev = [(s, c) for s, c in enumerate(self.cnt) if c > 0 and s not in sk]
        for e in self.ENG:
            seen = self.seen[e]
            waits = []
            for s, v in allev:
                if seen.get(s, 0) < v:
                    seen[s] = v
                    waits.append((s, v))
            if waits:
                self.q[e].append((waits, None, None, False))

    def emit(self):
        nc = self.nc
        sems = self.sems

        def replay(name, e):
            for waits, fn, inc, is_dma in self.q[name]:
                for s, v in waits:
                    e.wait_ge(sems[s], v)
                if fn is None:
                    continue
                if is_dma:
                    s, amt = inc
                    fn(e, lambda ins, _s=s, _a=amt: ins.then_inc(sems[_s], _a))
                else:
                    ins = fn(e)
                    ins.then_inc(sems[inc[0]], inc[1])

        with nc.Block() as block:
            @block.tensor
            def _(e):
                replay("tensor", e)

            @block.vector
            def _(e):
                replay("vector", e)

            @block.scalar
            def _(e):
                replay("scalar", e)

            @block.gpsimd
            def _(e):
                replay("gpsimd", e)

            @block.sync
            def _(e):
                replay("sync", e)


class Ring:
    def __init__(self, nc, stack, name, shape, dt, n):
        self.t = [stack.enter_context(nc.sbuf_tensor(f"{name}{i}", list(shape), dt)) for i in range(n)]
        self.b = [Buf(f"{name}{i}") for i in range(n)]
        self.i = 0
        self.n = n

    def next(self):
        i = self.i
        self.i = (i + 1) % self.n
        self.b[i] = Buf(self.b[i].name, prev=self.b[i])
        return self.t[i], self.b[i]


class TileInfo:
    def __init__(self, key, col):
        self.key = key
        self.samp = (key == "s")
        self.kind = 1 if self.samp else 0
        self.nt = 64 if self.samp else 128
        self.tok0 = TP if self.samp else key * 128
        self.col = col


def build_program(debug=None):
    nc = bass.Bass("TRN2", target_bir_lowering=False)
    din = lambda n, s: nc.dram_tensor(n, list(s), F32, kind="ExternalInput").ap()
    dout = lambda n, s: nc.dram_tensor(n, list(s), F32, kind="ExternalOutput").ap()
    x_p = din("x_p", [TP, D])
    x_s = din("x_s", [64, D])
    c_all = din("c_all", [192, D])
    sg_in = din("sg_in", [NS, 4, 128, 256])
    sr_in = din("sr_in", [NS, 4, 256, 512])
    w_ada = din("w_ada", [D, 3 * D])
    b_ada = din("b_ada", [3 * D])
    w_in = din("w_in", [D, DIN])
    w_lr2 = din("w_lr2", [16, 512])
    b_lr2 = din("b_lr2", [512])
    gng = din("gla_norm_g", [1024])
    rng_ = din("ret_norm_g", [2048])
    wbg = din("w_branch_gla", [1024, D])
    wbr = din("w_branch_ret", [2048, D])
    w_out = din("w_out", [D, D])
    ln_g = din("ln_g", [D])
    ln_b = din("ln_b", [D])
    rope = din("rope", [128, 2, NTOK])
    rmask = din("rmask", [128, 4, 2, 128])
    rdread = din("rdread", [128, 4, 2, 128])
    rdwrite = din("rdwrite", [128, 8])
    tri = din("tri", [128, 2, 272])
    colmask = din("colmask", [128, 16, 64])
    rowmask = din("rowmask", [128, 16])
    ident = din("ident", [128, 128])
    y_p = dout("y_p", [TP, D])
    y_s = dout("y_s", [64, D])
    sgp = dout("sgp", [4, 128, 256])
    srp = dout("srp", [4, 256, 512])
    sgs = dout("sgs", [NS, 4, 128, 256])
    srs = dout("srs", [NS, 4, 256, 512])
    dbg_outs = {}
    carry = {}

    w_in_v = w_in.rearrange("(kc p) n -> p kc n", p=128)
    w_ada_v = w_ada.rearrange("(kc p) n -> p kc n", p=128)
    wbg_v = wbg.rearrange("(kc p) n -> p kc n", p=128)
    wbr_v = wbr.rearrange("(kc p) n -> p kc n", p=128)
    w_out_v = w_out.rearrange("(kc p) n -> p kc n", p=128)

    lg = [math.log1p(-2.0 ** (-5 - h)) for h in range(4)]
    dchunk = [[math.exp(128 * lg[h]), math.exp(4 * lg[h])] for h in range(4)]

    with ExitStack() as stack:
        P = Prog(nc, stack)

        def sb(name, shape, dt, st=stack):
            return st.enter_context(nc.sbuf_tensor(name, list(shape), dt))

        banks = [stack.enter_context(nc.psum_tensor(f"pb{i}", [128, 512], F32)) for i in range(8)]
        banks_bf = [b.bitcast(BF16) for b in banks]
        bbuf = [Buf(f"pb{i}") for i in range(8)]
        ring_i = [0]

        main_banks = [[0, 1, 2, 3, 4]]
        oring_i = [0]

        def psum():
            mb = main_banks[0]
            i = mb[ring_i[0] % len(mb)]
            ring_i[0] += 1
            bbuf[i] = Buf(bbuf[i].name, prev=bbuf[i])
            return i, bbuf[i]

        ring7_i = [0]

        def psum7():
            i = ring7_i[0]
            ring7_i[0] = (i + 1) % 8
            bbuf[i] = Buf(bbuf[i].name, prev=bbuf[i])
            return i, bbuf[i]

        def psum_o():
            i = 5 + oring_i[0]
            oring_i[0] = (oring_i[0] + 1) % 2
            bbuf[i] = Buf(bbuf[i].name, prev=bbuf[i])
            return i, bbuf[i]

        def dbg(name, ap, buf, shape, dt=F32):
            if debug is None or name not in debug:
                return
            d = nc.dram_tensor("dbg_" + name, list(shape), dt, kind="ExternalOutput").ap()
            dbg_outs[name] = d
            P.dma("sync", "dbg", lambda e, inc: inc(e.dma_start(out=d, in_=ap)), R=[buf])
            P.seal("dbg", [])

        def mm(out_ap, pairs, R, W):
            def fn(e):
                n = len(pairs)
                for i, (l, r) in enumerate(pairs):
                    ins = e.matmul(out=out_ap, lhsT=l, rhs=r, start=(i == 0), stop=(i == n - 1))
                return ins
            P.op("tensor", fn, R=R, W=W)

        def tr(specs, R, W):
            def fn(e):
                for o, i, idn in specs:
                    ins = e.transpose(out=o, in_=i, identity=idn)
                return ins
            P.op("tensor", fn, R=R, W=W)

        def V(fn, R, W):
            P.op("vector", fn, R=R, W=W)

        def A(fn, R, W):
            P.op("scalar", fn, R=R, W=W)

        def G(fn, R, W):
            P.op("gpsimd", fn, R=R, W=W)

        ident_f = sb("ident_f", [128, 128], F32)
        ident_b = sb("ident_b", [128, 128], BF16)
        tri_sb = sb("tri_sb", [128, 2, 272], F32)
        cmask_sb = sb("cmask_sb", [128, 16, 64], BF16)
        rowm_sb = sb("rowm_sb", [128, 16], F32)
        rdw_sb = sb("rdw_sb", [128, 8], F32)
        blr_bc = sb("blr_bc", [128, 512], F32)
        wlr2_b = sb("wlr2_b", [16, 512], BF16)
        gnT = sb("gnT", [128, 24], F32)
        badaT = sb("badaT", [128, 24], F32)
        mhalf = sb("mhalf", [128, 1], F32)
        adaT = sb("adaT", [128, 16, 65], F32)
        cT = sb("cT", [128, 8, 192], BF16)
        S_gla = sb("S_gla", [128, 4, 256], F32)
        S_ret = sb("S_ret", [128, 4, 2, 512], F32)
        bS_gla = [Buf(f"Sg{h}") for h in range(4)]
        bS_ret = [Buf(f"Sr{h}") for h in range(4)]
        wslot = [sb(f"wslot{i}", [128, WSLOT], BF16) for i in range(2)]
        bw = [Buf("w0"), Buf("w1")]
        wi = [0]

        def next_w():
            i = wi[0]
            wi[0] = 1 - i
            return i

        xcnt = [0]
        bconst = Buf("const")
        bada = Buf("ada")
        bcT = Buf("cT")

        def ld(dst, src, **kw):
            P.dma("sync", "c", lambda e, inc: inc(e.dma_start(out=dst, in_=src, **kw)), W=[bconst])

        ld(ident_f[:, :], ident[:, :])
        ld(tri_sb[:, :, :], tri[:, :, :])
        ld(rowm_sb[:, :], rowmask[:, :])
        ld(rdw_sb[:, :], rdwrite[:, :])
        ld(blr_bc[:, :], b_lr2.partition_broadcast(128))
        ld(gnT[:, 0:8], gng.rearrange("(m p) -> p m", p=128), allow_slow_non_contiguous=True)
        ld(gnT[:, 8:24], rng_.rearrange("(m p) -> p m", p=128), allow_slow_non_contiguous=True)
        ld(badaT[:, :], b_ada.rearrange("(m p) -> p m", p=128), allow_slow_non_contiguous=True)
        P.dma("gpsimd", "cw", lambda e, inc: inc(e.dma_start(out=ident_b[:, :], in_=ident[:, :])), W=[bconst])
        P.dma("gpsimd", "cw", lambda e, inc: inc(e.dma_start(out=wlr2_b[:, :], in_=w_lr2[:, :])), W=[bconst])
        P.dma("gpsimd", "cw", lambda e, inc: inc(e.dma_start(out=cmask_sb[:, :, :], in_=colmask[:, :, :])), W=[bconst])
        P.barrier()
        G(lambda e: e.memset(mhalf[:, :], -0.5), [], [bconst])
        V(lambda e: e.tensor_scalar(out=gnT[:, 0:8], in0=gnT[:, 0:8], scalar1=0.5, scalar2=None, op0=ALU.mult), [bconst], [bconst])

        with ExitStack() as st0:
            cA = sb("cA", [64, D], F32, st0)
            cB = sb("cB", [128, D], F32, st0)
            bc = Buf("c")
            P.dma("sync", "c2", lambda e, inc: inc(e.dma_start(out=cA[:, :], in_=c_all[0:64, :])), W=[bc])
            P.dma("sync", "c2", lambda e, inc: inc(e.dma_start(out=cB[:, :], in_=c_all[64:192, :])), W=[bc])
            P.seal("c2", [bc])
            for k2 in range(4):
                bi, bb = psum()
                specs = []
                for j in range(2):
                    kc = 2 * k2 + j
                    specs.append((banks[bi][:, j * 192:j * 192 + 64], cA[:, kc * 128:(kc + 1) * 128], ident_f[0:64, 0:64]))
                    specs.append((banks[bi][:, j * 192 + 64:j * 192 + 192], cB[:, kc * 128:(kc + 1) * 128], ident_f[:, :]))
                tr(specs, [bc, bconst], [bb])
                A(lambda e, bi=bi, k2=k2: e.activation(
                    out=cT[:, 2 * k2:2 * k2 + 2, :], in_=banks[bi][:, 0:384].rearrange("p (j n) -> p j n", j=2),
                    func=AF.Copy), [bb], [bcT])
            for j in range(4):
                wsl = next_w()
                wv = wslot[wsl][:, 0:4096].rearrange("p (k n) -> p k n", k=8)
                P.dma("gpsimd", f"w{wsl}", lambda e, inc, wv=wv, j=j: inc(e.dma_start(
                    out=wv, in_=w_ada_v[:, :, j * 512:(j + 1) * 512])), W=[bw[wsl]])
                for mm_ in range(4):
                    m = 4 * j + mm_
                    bi, bb = psum()
                    mm(banks[bi][:, 0:65], [(wv[:, kc, mm_ * 128:(mm_ + 1) * 128], cT[:, kc, 0:65]) for kc in range(8)],
                       [bw[wsl], bcT], [bb])
                    V(lambda e, bi=bi, m=m: e.tensor_scalar(
                        out=adaT[:, m, :], in0=banks[bi][:, 0:65], scalar1=badaT[:, m:m + 1],
                        scalar2=(1.0 if m >= 8 else 0.0), op0=ALU.add, op1=ALU.add), [bb, bconst], [bada])
            P.barrier()

        def do_pass(pi, pkeys):
            tiles = []
            col = 0
            for k in pkeys:
                t = TileInfo(k, col)
                tiles.append(t)
                col += t.nt
            NP = col
            groups = []
            cur = []
            for t in tiles:
                if t.samp:
                    if cur:
                        groups.append(cur)
                    groups.append([t])
                    cur = []
                else:
                    cur.append(t)
                    if len(cur) == 4:
                        groups.append(cur)
                        cur = []
            if cur:
                groups.append(cur)
            has_s = any(t.samp for t in tiles)
            main_banks[0] = [0, 1, 2, 3, 4] if has_s else [0, 1, 2, 3, 4, 7]
            ptiles = [t for t in tiles if not t.samp]
            first_is_zero = (ptiles[0].key == 0)
            last_pass = (ptiles[-1].key == 15)

            def issue_unit_w(u):
                gla = u < 4
                h = u % 4
                wsl = next_w()
                if gla:
                    ncol = 784
                    segs = [(0, GQ + h * 128, 128), (128, GK + h * 128, 128), (256, GV + h * 256, 256),
                            (512, GZ + h * 256, 256), (768, LR, 16)]
                else:
                    ncol = 1536
                    segs = [(0, RQ + h * 256, 256), (256, RK + h * 256, 256), (512, RV + h * 512, 512),
                            (1024, RZ + h * 512, 512)]
                wv = wslot[wsl][:, 0:8 * ncol].rearrange("p (k n) -> p k n", k=8)

                def wload(e, inc):
                    for d0, s0, n in segs:
                        inc(e.dma_start(out=wv[:, :, d0:d0 + n], in_=w_in_v[:, :, s0:s0 + n]))
                P.dma("gpsimd", f"w{wsl}", wload, W=[bw[wsl]], n=len(segs))
                return wsl, wv
            pre_w = {0: carry.pop("u0") if "u0" in carry else issue_unit_w(0), 1: issue_unit_w(1)}

            def issue_gate_w():
                wsl = next_w()
                wv = wslot[wsl][:, 0:8192].rearrange("p (k n) -> p k n", k=8)
                P.dma("gpsimd", f"w{wsl}", lambda e, inc: inc(e.dma_start(out=wv, in_=w_ada_v[:, :, 2 * D:3 * D])), W=[bw[wsl]])
                return wsl, wv

            def issue_pair_w(m):
                wsl = next_w()
                wv = wslot[wsl][:, 0:40 * 256].rearrange("p (k n) -> p k n", n=256)

                def wload(e, inc):
                    inc(e.dma_start(out=wv[:, 0:8, :], in_=w_in_v[:, :, MG + m * 128:MG + (m + 2) * 128]))
                    inc(e.dma_start(out=wv[:, 8:16, :], in_=w_in_v[:, :, MR + m * 128:MR + (m + 2) * 128]))
                    inc(e.dma_start(out=wv[:, 16:24, :], in_=wbg_v[:, :, m * 128:(m + 2) * 128]))
                    inc(e.dma_start(out=wv[:, 24:40, :], in_=wbr_v[:, :, m * 128:(m + 2) * 128]))
                P.dma("gpsimd", f"w{wsl}", wload, W=[bw[wsl]], n=4)
                return wsl, wv
            pre_fl = {}

            with ExitStack() as stp:
                hT = sb(f"hT{pi}", [128, 8, NP], BF16, stp)
                oT = sb(f"oT{pi}", [128, 24, NP], BF16, stp)
                bhT = Buf("hT")
                boT = [Buf(f"oT{u}") for u in range(8)]

                with ExitStack() as stq:
                    tmpr = Ring(nc, stq, f"htmp{pi}", [128, 64], F32, 2) if has_s else None
                    xring = Ring(nc, stq, f"xtq{pi}", [128, D], F32, 4)
                    xqc = [0]

                    def do_pt(t):
                        xt, bx = xring.next()
                        xsrc = x_s[0:64, :] if t.samp else x_p[t.tok0:t.tok0 + 128, :]
                        xi = xqc[0] % 4
                        xqc[0] += 1
                        nt = t.nt
                        P.dma("sync", f"xq{xi}", lambda e, inc: inc(e.dma_start(out=xt[0:nt, :], in_=xsrc)), W=[bx])
                        for half in range(2):
                            bi, bb = psum()
                            tr([(banks[bi][:, j * 128:j * 128 + nt], xt[0:nt, (half * 4 + j) * 128:(half * 4 + j + 1) * 128],
                                 ident_f[0:nt, 0:nt]) for j in range(4)], [bx, bconst], [bb])
                            for j in range(4):
                                kc = half * 4 + j
                                src = banks[bi][:, j * 128:j * 128 + nt]
                                dst = hT[:, kc, t.col:t.col + nt]
                                if not t.samp:
                                    if j % 2 == 0:
                                        A(lambda e, src=src, dst=dst, kc=kc: e.activation(
                                            out=dst, in_=src, func=AF.Identity, scale=adaT[:, 8 + kc, 64:65],
                                            bias=adaT[:, kc, 64:65]), [bb, bada], [bhT])
                                    else:
                                        V(lambda e, src=src, dst=dst, kc=kc: e.tensor_scalar(
                                            out=dst, in0=src, scalar1=adaT[:, 8 + kc, 64:65], scalar2=adaT[:, kc, 64:65],
                                            op0=ALU.mult, op1=ALU.add), [bb, bada], [bhT])
                                else:
                                    tm, btm = tmpr.next()
                                    V(lambda e, src=src, tm=tm, kc=kc: e.tensor_tensor(
                                        out=tm[:, :], in0=src, in1=adaT[:, 8 + kc, 0:64], op=ALU.mult), [bb, bada], [btm])
                                    V(lambda e, dst=dst, tm=tm, kc=kc: e.tensor_tensor(
                                        out=dst, in0=tm[:, :], in1=adaT[:, kc, 0:64], op=ALU.add), [btm, bada], [bhT])
                    for t in tiles:
                        do_pt(t)
                    P.barrier()

                with ExitStack() as stu:
                    rope_sb = sb(f"rope{pi}", [128, 2, NP], F32, stu)
                    brope = Buf("rope")
                    for cs in range(2):
                        for t in tiles:
                            P.dma("sync", "rope", lambda e, inc, cs=cs, t=t: inc(e.dma_start(
                                out=rope_sb[:, cs, t.col:t.col + t.nt], in_=rope[:, cs, t.tok0:t.tok0 + t.nt])), W=[brope])
                    P.seal("rope", [brope])
                    f32g = Ring(nc, stu, f"f32g{pi}", [128, 512], F32, 4)
                    bfg = Ring(nc, stu, f"bfg{pi}", [128, 2, 512], BF16, 6)
                    lrT_r = Ring(nc, stu, f"lrT{pi}", [16, 512], BF16, 2)
                    sm128 = Ring(nc, stu, f"sm{pi}", [128, 128], F32, 4)
                    vbf_r = Ring(nc, stu, f"vbf{pi}", [128, 512], BF16, 2)
                    th_r = Ring(nc, stu, f"th{pi}", [128, 512], F32, 1)
                    zs_r = Ring(nc, stu, f"zs{pi}", [128, 512], F32, 3)
                    am_r = Ring(nc, stu, f"am{pi}", [128, 128], BF16, 2)
                    kw_r = Ring(nc, stu, f"kw{pi}", [128, 256], BF16, 2)
                    sbf_r = Ring(nc, stu, f"sbf{pi}", [128, 2, 512], BF16, 3)
                    on_r = Ring(nc, stu, f"on{pi}", [128, 512], F32, 3)
                    t512 = Ring(nc, stu, f"t512{pi}", [128, 512], F32, 2)
                    st_r = Ring(nc, stu, f"st{pi}", [128, 16], F32, 4)
                    dsd_r = Ring(nc, stu, f"dsd{pi}", [128, 16], F32, 8)
                    mk_r = Ring(nc, stu, f"mk{pi}", [128, 2, 2, 128], F32, 2)
                    if has_s:
                        sin_r = Ring(nc, stu, f"sin{pi}", [128, 2, 512], F32, 4)
                        qm_r = Ring(nc, stu, f"qm{pi}", [128, 2, 64], BF16, 2)
                        sinb_r = Ring(nc, stu, f"sinb{pi}", [128, 2, 512], BF16, 2)
                        km_r = Ring(nc, stu, f"km{pi}", [64, 256], BF16, 2)
                        oint = sb(f"oint{pi}", [64, 512], F32, stu)
                        boint = Buf("oint")
                    lane_ctr = {"sin": 0, "sout": 0, "mk": 0}
                    if has_s:
                        bfg_s = Ring(nc, stu, f"bfgs{pi}", [128, 2, 64], BF16, 6)
                        dsd_s = Ring(nc, stu, f"dsds{pi}", [128, 16], F32, 2)
                        vbf_s = Ring(nc, stu, f"vbfs{pi}", [64, 512], BF16, 2)
                        zs_s = Ring(nc, stu, f"zss{pi}", [64, 512], F32, 2)
                        am_s = Ring(nc, stu, f"ams{pi}", [64, 64], BF16, 2)
                        kw_s = Ring(nc, stu, f"kws{pi}", [64, 256], BF16, 2)
                    seq_per_step = [NS]

                    def do_unit(u):
                        gla = u < 4
                        h = u % 4
                        dkc = 1 if gla else 2
                        dv = 256 if gla else 512
                        ochunk0 = (h * 2) if gla else (8 + h * 4)
                        nch = dv // 128
                        wsl, wv = pre_w.pop(u) if u in pre_w else issue_unit_w(u)
                        bwu = bw[wsl]
                        mk = bmk = None
                        if not gla:
                            mk, bmk = mk_r.next()
                            ml = lane_ctr["mk"] % 2
                            lane_ctr["mk"] += 1

                            def mkload(e, inc):
                                inc(e.dma_start(out=mk[:, 0, :, :], in_=rmask[:, h, :, :]))
                                inc(e.dma_start(out=mk[:, 1, :, :], in_=rdread[:, h, :, :]))
                            P.dma("sync", f"mk{ml}", mkload, W=[bmk], n=2)
                        bS = bS_gla[h] if gla else bS_ret[h]
                        us = {"valid": not first_is_zero, "sbf": None, "bsbf": None}

                        def init_sbf():
                            sbf0, bsbf0 = sbf_r.next()
                            us["sbf"], us["bsbf"] = sbf0, bsbf0
                            if gla:
                                A(lambda e: e.activation(out=sbf0[:, 0, 0:256], in_=S_gla[:, h, :], func=AF.Copy), [bS], [bsbf0])
                            else:
                                A(lambda e: e.activation(out=sbf0[:, :, :], in_=S_ret[:, h, :, :], func=AF.Copy), [bS], [bsbf0])

                        def do_group(grp):
                            g0 = grp[0].col
                            NG = sum(t.nt for t in grp)
                            gs = slice(g0, g0 + NG)
                            dsd_of = {}
                            if gla:
                                bi0, bb0 = psum()
                                mm(banks[bi0][0:16, 0:NG], [(wv[:, kc, 768:784], hT[:, kc, gs]) for kc in range(8)], [bwu, bhT], [bb0])
                                lrT, blrT = lrT_r.next()
                                A(lambda e: e.activation(out=lrT[:, 0:NG], in_=banks[bi0][0:16, 0:NG], func=AF.Copy), [bb0], [blrT])
                                Ep, bEp = f32g.next()
                                En, bEn = f32g.next()
                                Er, bEr = f32g.next()

                                def do_decay(t):
                                    lc = t.col - g0
                                    nt = t.nt
                                    bi, bb = psum()
                                    mm(banks[bi][0:nt, 0:128], [(lrT[:, lc:lc + nt], wlr2_b[:, h * 128:(h + 1) * 128])], [blrT, bconst], [bb])
                                    zb, bzb = sm128.next()
                                    V(lambda e: e.tensor_tensor(
                                        out=zb[0:nt, :], in0=banks[bi][0:nt, 0:128], in1=blr_bc[0:nt, h * 128:(h + 1) * 128], op=ALU.add),
                                      [bb, bconst], [bzb])
                                    A(lambda e: e.activation(out=zb[0:nt, :], in_=zb[0:nt, :], func=AF.Exp, scale=-1.0), [bzb], [bzb])
                                    lsb, blsb = sm128.next()
                                    A(lambda e: e.activation(out=lsb[0:nt, :], in_=zb[0:nt, :], func=AF.Ln, bias=1.0), [bzb], [blsb])
                                    bi2, bb2 = psum()
                                    mm(banks[bi2][:, 0:272], [(lsb[0:nt, :], tri_sb[0:nt, t.kind, :])], [blsb, bconst], [bb2])
                                    A(lambda e: e.activation(out=Ep[:, lc:lc + nt], in_=banks[bi2][:, 0:nt], func=AF.Exp, scale=-1.0 / 16), [bb2], [bEp])
                                    A(lambda e: e.activation(out=En[:, lc:lc + nt], in_=banks[bi2][:, 0:nt], func=AF.Exp, scale=1.0 / 16), [bb2], [bEn])
                                    A(lambda e: e.activation(out=Er[:, lc:lc + nt], in_=banks[bi2][:, 128:128 + nt], func=AF.Exp, scale=-1.0 / 16), [bb2], [bEr])
                                    dsd, bdsd = (dsd_s if t.samp else dsd_r).next()
                                    A(lambda e: e.activation(out=dsd[:, :], in_=banks[bi2][:, 256:272], func=AF.Exp, scale=-1.0 / 16), [bb2], [bdsd])
                                    dsd_of[t.key] = (dsd, bdsd)
                                for t in grp:
                                    do_decay(t)
                                biq, bbq = psum()
                                mm(banks[biq][:, 0:NG], [(wv[:, kc, 0:128], hT[:, kc, gs]) for kc in range(8)], [bwu, bhT], [bbq])
                                bik, bbk = psum()
                                mm(banks[bik][:, 0:NG], [(wv[:, kc, 128:256], hT[:, kc, gs]) for kc in range(8)], [bwu, bhT], [bbk])
                                bfgx = bfg_s if grp[0].samp else bfg
                                qd, bqd = bfgx.next()
                                kd, bkd = bfgx.next()
                                kwT, bkwT = bfgx.next()
                                V(lambda e: e.scalar_tensor_tensor(
                                    out=qd[:, 0, 0:NG], in0=banks[biq][:, 0:NG], scalar=128.0 ** -0.5, in1=Ep[:, 0:NG],
                                    op0=ALU.mult, op1=ALU.mult), [bbq, bEp], [bqd])
                                V(lambda e: e.tensor_tensor(out=kd[:, 0, 0:NG], in0=banks[bik][:, 0:NG], in1=En[:, 0:NG], op=ALU.mult), [bbk, bEn], [bkd])
                                V(lambda e: e.tensor_tensor(out=kwT[:, 0, 0:NG], in0=banks[bik][:, 0:NG], in1=Er[:, 0:NG], op=ALU.mult), [bbk, bEr], [bkwT])
                                qA, bqA, kA, bkA, qS, bqS, kW, bkW = qd, bqd, kd, bkd, qd, bqd, kwT, bkwT
                            else:
                                cosg = rope_sb[:, 0, gs]
                                sing = rope_sb[:, 1, gs]

                                def do_rot(which):
                                    b0, bb0 = psum()
                                    mm(banks[b0][:, 0:NG], [(wv[:, kc, which * 256:which * 256 + 128], hT[:, kc, gs]) for kc in range(8)],
                                       [bwu, bhT], [bb0])
                                    b1, bb1 = psum()
                                    mm(banks[b1][:, 0:NG], [(wv[:, kc, which * 256 + 128:which * 256 + 256], hT[:, kc, gs]) for kc in range(8)],
                                       [bwu, bhT], [bb1])
                                    t1, bt1 = f32g.next()
                                    t2, bt2 = f32g.next()
                                    t3, bt3 = f32g.next()
                                    t4, bt4 = f32g.next()
                                    V(lambda e: e.tensor_tensor(out=t1[:, 0:NG], in0=banks[b0][:, 0:NG], in1=cosg, op=ALU.mult), [bb0, brope], [bt1])
                                    V(lambda e: e.tensor_tensor(out=t2[:, 0:NG], in0=banks[b1][:, 0:NG], in1=sing, op=ALU.mult), [bb1, brope], [bt2])
                                    V(lambda e: e.tensor_tensor(out=t3[:, 0:NG], in0=banks[b0][:, 0:NG], in1=sing, op=ALU.mult), [bb0, brope], [bt3])
                                    V(lambda e: e.tensor_tensor(out=t4[:, 0:NG], in0=banks[b1][:, 0:NG], in1=cosg, op=ALU.mult), [bb1, brope], [bt4])
                                    rot, brot = (bfg_s if grp[0].samp else bfg).next()
                                    G(lambda e: e.tensor_tensor(out=rot[:, 0, 0:NG], in0=t1[:, 0:NG], in1=t2[:, 0:NG], op=ALU.subtract), [bt1, bt2], [brot])
                                    G(lambda e: e.tensor_tensor(out=rot[:, 1, 0:NG], in0=t3[:, 0:NG], in1=t4[:, 0:NG], op=ALU.add), [bt3, bt4], [brot])
                                    return rot, brot
                                qr, bqr = do_rot(0)
                                kr, bkr = do_rot(1)
                                qrd, bqrd = (bfg_s if grp[0].samp else bfg).next()
                                for t in grp:
                                    V(lambda e, t=t, lc=t.col - g0: e.tensor_tensor(
                                        out=qrd[:, :, lc:lc + t.nt], in0=qr[:, :, lc:lc + t.nt],
                                        in1=mk[:, 1, t.kind:t.kind + 1, 0:t.nt].to_broadcast([128, 2, t.nt]), op=ALU.mult), [bqr, bmk], [bqrd])
                                qA, bqA, kA, bkA, qS, bqS, kW, bkW = qr, bqr, kr, bkr, qrd, bqrd, kr, bkr

                            def do_tile(t):
                                lc = t.col - g0
                                nt = t.nt
                                tsl = slice(t.col, t.col + nt)
                                vbf, bvbf = (vbf_s if t.samp else vbf_r).next()
                                th, bth = th_r.next()
                                zs, bzs = (zs_s if t.samp else zs_r).next()
                                if gla:
                                    biv, bbv = psum()
                                    mm(banks[biv][0:nt, 0:512], [(hT[:, kc, tsl], wv[:, kc, 256:768]) for kc in range(8)], [bhT, bwu], [bbv])
                                    vsrc = banks[biv][0:nt, 0:256]
                                    zsrc = banks[biv][0:nt, 256:512]
                                    bbz = bbv
                                else:
                                    biv, bbv = psum()
                                    mm(banks[biv][0:nt, 0:512], [(hT[:, kc, tsl], wv[:, kc, 512:1024]) for kc in range(8)], [bhT, bwu], [bbv])
                                    vsrc = banks[biv][0:nt, 0:512]
                                    biz, bbz = psum()
                                    mm(banks[biz][0:nt, 0:512], [(hT[:, kc, tsl], wv[:, kc, 1024:1536]) for kc in range(8)], [bhT, bwu], [bbz])
                                    zsrc = banks[biz][0:nt, 0:512]
                                A(lambda e: e.activation(out=vbf[0:nt, 0:dv], in_=vsrc, func=AF.Copy), [bbv], [bvbf])
                                if gla:
                                    A(lambda e: e.activation(out=th[0:nt, 0:dv], in_=zsrc, func=AF.Tanh, scale=0.5), [bbz], [bth])
                                    V(lambda e: e.scalar_tensor_tensor(
                                        out=zs[0:nt, 0:dv], in0=th[0:nt, 0:dv], scalar=1.0, in1=zsrc, op0=ALU.add, op1=ALU.mult), [bth, bbz], [bzs])
                                else:
                                    A(lambda e: e.activation(out=zs[0:nt, 0:dv], in_=zsrc, func=AF.Silu), [bbz], [bzs])
                                bia, bba = psum()
                                mm(banks[bia][0:nt, 0:nt], [(kA[:, c, lc:lc + nt], qA[:, c, lc:lc + nt]) for c in range(dkc)], [bkA, bqA], [bba])
                                am, bam = (am_s if t.samp else am_r).next()
                                if gla:
                                    msk = tri_sb[0:nt, t.kind, 0:nt]
                                    bmsk = bconst
                                else:
                                    msk = mk[0:nt, 0, t.kind, 0:nt]
                                    bmsk = bmk
                                V(lambda e: e.tensor_tensor(out=am[0:nt, 0:nt], in0=banks[bia][0:nt, 0:nt], in1=msk, op=ALU.mult), [bba, bmsk], [bam])
                                osl = {}

                                def issue_o():
                                    bio, bbo = psum_o()
                                    pairs = [(am[0:nt, 0:nt], vbf[0:nt, 0:dv])]
                                    Rl = [bam, bvbf]
                                    if (not t.samp) and us["valid"]:
                                        if us["sbf"] is None:
                                            init_sbf()
                                        sbfc = us["sbf"]
                                        pairs += [(qS[:, c, lc:lc + nt], sbfc[:, c, 0:dv]) for c in range(dkc)]
                                        Rl += [bqS, us["bsbf"]]
                                    mm(banks[bio][0:nt, 0:dv], pairs, Rl, [bbo])
                                    osl["bio"], osl["bbo"] = bio, bbo
                                bit, bbt = psum()
                                tr([(banks_bf[bit][0:nt, c * 128:(c + 1) * 128], kW[:, c, lc:lc + nt], ident_b[:, :]) for c in range(dkc)],
                                   [bkW, bconst], [bbt])
                                kw, bkw = (kw_s if t.samp else kw_r).next()
                                if gla:
                                    A(lambda e: e.activation(out=kw[0:nt, 0:128], in_=banks_bf[bit][0:nt, 0:128], func=AF.Copy), [bbt], [bkw])
                                else:
                                    A(lambda e: e.activation(
                                        out=kw[0:nt, 0:256], in_=banks_bf[bit][0:nt, 0:256], func=AF.Identity,
                                        scale=rdw_sb[0:nt, 2 * h + t.kind:2 * h + t.kind + 1]), [bbt, bconst], [bkw])
                                yield
                                if not t.samp:
                                    issue_o()
                                    o_src = banks[osl["bio"]][0:nt, 0:dv]
                                    bo_src = osl["bbo"]
                                    nsbf, bnsbf = sbf_r.next()
                                    valid = us["valid"]

                                    def do_c(c):
                                        bi, bb = psum()
                                        mm(banks[bi][:, 0:dv], [(kw[0:nt, c * 128:(c + 1) * 128], vbf[0:nt, 0:dv])], [bkw, bvbf], [bb])
                                        Sd = S_gla[:, h, :] if gla else S_ret[:, h, c, :]
                                        if not valid:
                                            V(lambda e: e.tensor_copy(out=Sd, in_=banks[bi][:, 0:dv]), [bb], [bS])
                                        elif gla:
                                            dsd, bdsd = dsd_of[t.key]
                                            V(lambda e: e.scalar_tensor_tensor(
                                                out=Sd, in0=Sd, scalar=dsd[:, 0:1], in1=banks[bi][:, 0:dv], op0=ALU.mult, op1=ALU.add),
                                              [bb, bdsd, bS], [bS])
                                        else:
                                            V(lambda e: e.scalar_tensor_tensor(
                                                out=Sd, in0=Sd, scalar=dchunk[h][0], in1=banks[bi][:, 0:dv], op0=ALU.mult, op1=ALU.add),
                                              [bb, bS], [bS])
                                        A(lambda e: e.activation(out=nsbf[:, c, 0:dv], in_=Sd, func=AF.Copy), [bS], [bnsbf])
                                    for c in range(dkc):
                                        do_c(c)
                                    us["sbf"], us["bsbf"] = nsbf, bnsbf
                                    us["valid"] = True
                                    if last_pass and t.key == 15:
                                        if gla:
                                            P.dma("sync", "stp", lambda e, inc: inc(e.dma_start(out=sgp[h, :, :], in_=S_gla[:, h, :])), R=[bS])
                                        else:
                                            P.dma("sync", "stp", lambda e, inc: inc(e.dma_start(
                                                out=srp[h, :, :].rearrange("(c p) v -> p c v", p=128), in_=S_ret[:, h, :, :])), R=[bS])
                                else:
                                    loads = {}
                                    lanes_of = {}
                                    bbuf[7] = Buf("pb7", prev=bbuf[7])
                                    b7 = bbuf[7]

                                    def issue_load(i):
                                        if gla and i % 4 != 0:
                                            loads[i] = loads[i - 1]
                                            lanes_of[i] = lanes_of[i - 1]
                                            return
                                        sin_, bsin = sin_r.next()
                                        li = lane_ctr["sin"] % 4
                                        lane_ctr["sin"] += 1
                                        lanes_of[i] = li
                                        if gla:
                                            P.dma("sync", f"sin{li}", lambda e, inc: inc(e.dma_start(
                                                out=sin_[:, :, :].rearrange("p c (s v) -> p (c s) v", v=256),
                                                in_=sg_in[i:i + 4, h, :, :].rearrange("s d v -> d s v"))), W=[bsin])
                                        else:
                                            P.dma("sync", f"sin{li}", lambda e, inc: inc(e.dma_start(
                                                out=sin_[:, :, :], in_=sr_in[i, h, :, :].rearrange("(c p) v -> p c v", p=128))), W=[bsin])
                                        loads[i] = (sin_, bsin)
                                    if gla:
                                        for i0 in range(NS):
                                            issue_load(i0)
                                    else:
                                        issue_load(0)
                                        issue_load(1)
                                        issue_load(2)

                                    def st_ap(sin_, i):
                                        if gla:
                                            return sin_[:, :, :].rearrange("p c (s v) -> p (c s) v", v=256)[:, i % 4, :]
                                        return None

                                    preps = {}

                                    def prep_seq(i):
                                        sin_, bsin = loads[i]
                                        qm, bqm = qm_r.next()
                                        V(lambda e: e.tensor_tensor(
                                            out=qm[:, 0:dkc, :], in0=qS[:, 0:dkc, lc:lc + 64],
                                            in1=cmask_sb[:, i:i + 1, :].to_broadcast([128, dkc, 64]), op=ALU.mult), [bqS, bconst], [bqm])
                                        sinb, bsinb = sinb_r.next()
                                        if gla:
                                            A(lambda e: e.activation(out=sinb[:, 0, 0:256], in_=st_ap(sin_, i), func=AF.Copy), [bsin], [bsinb])
                                        else:
                                            A(lambda e: e.activation(out=sinb[:, 0:dkc, 0:dv], in_=sin_[:, 0:dkc, 0:dv], func=AF.Copy), [bsin], [bsinb])
                                        km, bkm = km_r.next()
                                        V(lambda e: e.tensor_scalar(
                                            out=km[:, 0:dkc * 128], in0=kw[0:64, 0:dkc * 128], scalar1=rowm_sb[0:64, i:i + 1], scalar2=None,
                                            op0=ALU.mult), [bkw, bconst], [bkm])
                                        preps[i] = (qm, bqm, sinb, bsinb, km, bkm)
                                    prep_seq(0)

                                    def do_seq(i):
                                        sin_, bsin = loads[i]
                                        if i + 1 < NS:
                                            prep_seq(i + 1)
                                        qm, bqm, sinb, bsinb, km, bkm = preps.pop(i)
                                        mm_pairs = [(qm[:, c, :], sinb[:, c, 0:dv]) for c in range(dkc)]

                                        def fn_oi(e):
                                            for c, (l, r) in enumerate(mm_pairs):
                                                ins = e.matmul(out=banks[7][0:64, 0:dv], lhsT=l, rhs=r,
                                                               start=(i == 0 and c == 0), stop=(i == NS - 1 and c == dkc - 1))
                                            return ins
                                        P.op("tensor", fn_oi, R=[bqm, bsinb], W=[b7])
                                        def do_sc(c):
                                            bi, bb = psum()
                                            mm(banks[bi][:, 0:dv], [(km[:, c * 128:(c + 1) * 128], vbf[0:64, 0:dv])], [bkm, bvbf], [bb])
                                            if gla:
                                                dsd, bdsd = dsd_of[t.key]
                                                V(lambda e: e.scalar_tensor_tensor(
                                                    out=st_ap(sin_, i), in0=st_ap(sin_, i), scalar=dsd[:, i:i + 1], in1=banks[bi][:, 0:dv],
                                                    op0=ALU.mult, op1=ALU.add), [bb, bsin, bdsd], [bsin])
                                            else:
                                                V(lambda e: e.scalar_tensor_tensor(
                                                    out=sin_[:, c, 0:dv], in0=sin_[:, c, 0:dv], scalar=dchunk[h][1], in1=banks[bi][:, 0:dv],
                                                    op0=ALU.mult, op1=ALU.add), [bb, bsin], [bsin])
                                        for c in range(dkc):
                                            do_sc(c)
                                        lo = lanes_of[i]
                                        if gla:
                                            if i % 4 == 3:
                                                P.dma("sync", f"sout{lo}", lambda e, inc: inc(e.dma_start(
                                                    out=sgs[i - 3:i + 1, h, :, :].rearrange("s d v -> d s v"),
                                                    in_=sin_[:, :, :].rearrange("p c (s v) -> p (c s) v", v=256))), R=[bsin])
                                        else:
                                            P.dma("sync", f"sout{lo}", lambda e, inc: inc(e.dma_start(
                                                out=srs[i, h, :, :].rearrange("(c p) v -> p c v", p=128), in_=sin_[:, :, :])), R=[bsin])
                                        if (not gla) and i + 3 < NS:
                                            issue_load(i + 3)
                                    for i in range(NS):
                                        do_seq(i)
                                        if (i + 1) % seq_per_step[0] == 0 or i == NS - 1:
                                            yield
                                    A(lambda e: e.activation(out=oint[:, 0:dv], in_=banks[7][0:64, 0:dv], func=AF.Copy), [b7], [boint])
                                    issue_o()
                                    bio_s, bbo_s = osl["bio"], osl["bbo"]
                                    V(lambda e: e.tensor_tensor(out=oint[:, 0:dv], in0=banks[bio_s][0:64, 0:dv], in1=oint[:, 0:dv], op=ALU.add),
                                      [bbo_s, boint], [boint])
                                    o_src = oint[:, 0:dv]
                                    bo_src = boint
                                yield
                                stt_, bst = st_r.next()
                                on, bon = on_r.next()
                                if gla:
                                    junk, bjunk = t512.next()
                                    A(lambda e: e.activation(
                                        out=junk[0:nt, 0:dv], in_=o_src, func=AF.Square, accum_out=stt_[0:nt, 0:1]), [bo_src], [bjunk, bst])
                                    V(lambda e: e.tensor_scalar(
                                        out=stt_[0:nt, 1:2], in0=stt_[0:nt, 0:1], scalar1=1.0 / dv, scalar2=EPS, op0=ALU.mult, op1=ALU.add), [bst], [bst])
                                    G(lambda e: e.tensor_tensor(out=stt_[0:nt, 2:3], in0=stt_[0:nt, 1:2], in1=mhalf[0:nt, :], op=ALU.pow),
                                      [bst, bconst], [bst])
                                    yield
                                    V(lambda e: e.scalar_tensor_tensor(
                                        out=on[0:nt, 0:dv], in0=o_src, scalar=stt_[0:nt, 2:3], in1=zs[0:nt, 0:dv], op0=ALU.mult, op1=ALU.mult),
                                      [bo_src, bst, bzs], [bon])
                                else:
                                    V(lambda e: e.bn_stats(out=stt_[0:nt, 0:6], in_=o_src), [bo_src], [bst])
                                    V(lambda e: e.bn_aggr(out=stt_[0:nt, 6:8], in_=stt_[0:nt, 0:6]), [bst], [bst])
                                    V(lambda e: e.tensor_scalar(
                                        out=stt_[0:nt, 8:9], in0=stt_[0:nt, 7:8], scalar1=EPS, scalar2=None, op0=ALU.add), [bst], [bst])
                                    G(lambda e: e.tensor_tensor(out=stt_[0:nt, 9:10], in0=stt_[0:nt, 8:9], in1=mhalf[0:nt, :], op=ALU.pow),
                                      [bst, bconst], [bst])
                                    yield
                                    onm, bonm = t512.next()
                                    V(lambda e: e.tensor_scalar(
                                        out=onm[0:nt, :], in0=o_src, scalar1=stt_[0:nt, 6:7], scalar2=stt_[0:nt, 9:10],
                                        op0=ALU.subtract, op1=ALU.mult), [bo_src, bst], [bonm])
                                    G(lambda e: e.tensor_tensor(out=on[0:nt, :], in0=onm[0:nt, :], in1=zs[0:nt, :], op=ALU.mult), [bonm, bzs], [bon])
                                yield
                                bix, bbx = psum()
                                tr([(banks[bix][:, j * 128:j * 128 + nt], on[0:nt, j * 128:(j + 1) * 128], ident_f[0:nt, 0:nt]) for j in range(nch)],
                                   [bon, bconst], [bbx])
                                for j in range(nch):
                                    A(lambda e, j=j, oc=ochunk0 + j: e.activation(
                                        out=oT[:, oc, tsl], in_=banks[bix][:, j * 128:j * 128 + nt], func=AF.Identity, scale=gnT[:, oc:oc + 1]),
                                      [bbx, bconst], [boT[u]])
                            return do_tile
                        return do_group
                    ps = {"s2": None, "s3": None, "s4": None}

                    def pstep(g, adv=None):
                        g3 = ps["s3"]
                        if g3 is not None:
                            next(g3)
                        if adv:
                            adv(0)
                        if g is not None:
                            next(g)
                        if adv:
                            adv(1)
                        if g3 is not None:
                            next(g3)
                        if adv:
                            adv(2)
                        if ps["s2"] is not None:
                            next(ps["s2"])
                        if adv:
                            adv(3)
                        if ps["s4"] is not None:
                            for _ in ps["s4"]:
                                pass
                        if adv:
                            adv(4)
                        ps["s4"] = g3
                        ps["s3"] = ps["s2"]
                        ps["s2"] = g
                    if not has_s:
                        upairs = [(u, gi) for u in range(8) for gi in range(len(groups))]
                        ufn = {0: do_unit(0)}
                        gfn = {(0, 0): ufn[0](groups[0])}
                        for j, (u, gi) in enumerate(upairs):
                            grp = groups[gi]
                            if gi == 0 and u + 1 < 8:
                                ufn[u + 1] = do_unit(u + 1)
                            for i, t in enumerate(grp):
                                last = (i == len(grp) - 1 and j + 1 < len(upairs))
                                if last:
                                    gfn[upairs[j + 1]] = ufn[upairs[j + 1][0]](groups[upairs[j + 1][1]])
                                pstep(gfn[(u, gi)](t))
                    else:
                        sgrp = [g for g in groups if g[0].samp][0]
                        pgrps = [g for g in groups if not g[0].samp]
                        ptl = [(gi, t) for gi, g in enumerate(pgrps) for t in g]
                        seq_per_step[0] = 1
                        quota = -(-NS // len(ptl))
                        sched = [quota // 5] * 5
                        for sl in [1, 3, 0, 2, 4][:quota % 5]:
                            sched[sl] += 1

                        def wrap(g):
                            yield
                            yield from g
                        ufn = {0: do_unit(0)}
                        for u in range(8):
                            if u + 1 < 8:
                                ufn[u + 1] = do_unit(u + 1)
                            gen_s = ufn[u](sgrp)(sgrp[0])
                            next(gen_s)
                            gf = {}
                            st_ = {"done": 0, "first": True}

                            def adv(slot, gen_s=gen_s, st_=st_):
                                sc = sched
                                if st_["first"]:
                                    sc = [0, 0, 0, quota - quota // 2, quota // 2]
                                for _ in range(sc[slot]):
                                    if st_["done"] < NS:
                                        next(gen_s)
                                        st_["done"] += 1
                            for idx, (gi, t) in enumerate(ptl):
                                if gi not in gf:
                                    gf[gi] = ufn[u](pgrps[gi])
                                pstep(gf[gi](t), adv)
                                st_["first"] = False
                            while st_["done"] < NS:
                                next(gen_s)
                                st_["done"] += 1
                            pstep(wrap(gen_s))
                    pre_fl[0] = issue_gate_w()
                    pre_fl[1] = issue_pair_w(0)
                    pstep(None)
                    pstep(None)
                    pstep(None)
                    P.barrier(skip=("w0", "w1"))

                with ExitStack() as stf:
                    mT = sb(f"mT{pi}", [128, 8, NP], BF16, stf)
                    bmT = Buf("mT")
                    gate_p = sb(f"gatep{pi}", [128, D], F32, stf)
                    gate_s = sb(f"gates{pi}", [64, D], F32, stf) if has_s else None
                    bgate = Buf("gate")
                    lng = sb(f"lng{pi}", [128, D], F32, stf)
                    lnb = sb(f"lnb{pi}", [128, D], F32, stf)
                    bln = Buf("ln")
                    P.dma("sync", "ln", lambda e, inc: inc(e.dma_start(out=lng[:, :], in_=ln_g.partition_broadcast(128))), W=[bln])
                    P.dma("sync", "ln", lambda e, inc: inc(e.dma_start(out=lnb[:, :], in_=ln_b.partition_broadcast(128))), W=[bln])
                    P.dma("sync", "ln", lambda e, inc: inc(e.dma_start(out=gate_p[:, :], in_=b_ada[2 * D:3 * D].partition_broadcast(128))), W=[bln])
                    if has_s:
                        P.dma("sync", "ln", lambda e, inc: inc(e.dma_start(out=gate_s[:, :], in_=b_ada[2 * D:3 * D].partition_broadcast(64))), W=[bln])
                    P.seal("ln", [bln])
                    V(lambda e: e.tensor_scalar(out=gate_p[:, :], in0=gate_p[:, :], scalar1=0.5, scalar2=None, op0=ALU.mult), [bln], [bln])
                    if has_s:
                        V(lambda e: e.tensor_scalar(out=gate_s[:, :], in0=gate_s[:, :], scalar1=0.5, scalar2=None, op0=ALU.mult), [bln], [bln])
                    f4 = Ring(nc, stf, f"f4{pi}", [128, 512], F32, 4)
                    xringf = Ring(nc, stf, f"xtf{pi}", [128, D], F32, 2)
                    ft = Ring(nc, stf, f"ft{pi}", [128, D], F32, 2)
                    yr = Ring(nc, stf, f"yr{pi}", [128, D], F32, 2)
                    stf_r = Ring(nc, stf, f"stf{pi}", [128, 24], F32, 2)
                    ylane = [0]

                    fl = {"issued": len(pre_fl), "slots": dict(pre_fl)}

                    def fl_gate(j):
                        def f(wsl):
                            wv = wslot[wsl][:, 0:4096].rearrange("p (k n) -> p k n", k=8)
                            P.dma("gpsimd", f"w{wsl}", lambda e, inc: inc(e.dma_start(
                                out=wv, in_=w_ada_v[:, :, 2 * D + j * 512:2 * D + (j + 1) * 512])), W=[bw[wsl]])
                            return wv
                        return f

                    def fl_pair(m):
                        def f(wsl):
                            wv = wslot[wsl][:, 0:40 * 256].rearrange("p (k n) -> p k n", n=256)

                            def wload(e, inc):
                                inc(e.dma_start(out=wv[:, 0:8, :], in_=w_in_v[:, :, MG + m * 128:MG + (m + 2) * 128]))
                                inc(e.dma_start(out=wv[:, 8:16, :], in_=w_in_v[:, :, MR + m * 128:MR + (m + 2) * 128]))
                                inc(e.dma_start(out=wv[:, 16:24, :], in_=wbg_v[:, :, m * 128:(m + 2) * 128]))
                                inc(e.dma_start(out=wv[:, 24:40, :], in_=wbr_v[:, :, m * 128:(m + 2) * 128]))
                            P.dma("gpsimd", f"w{wsl}", wload, W=[bw[wsl]], n=4)
                            return wv
                        return f

                    def fl_out():
                        def f(wsl):
                            wv = wslot[wsl][:, 0:8192].rearrange("p (k n) -> p k n", k=8)
                            P.dma("gpsimd", f"w{wsl}", lambda e, inc: inc(e.dma_start(out=wv, in_=w_out_v[:, :, :])), W=[bw[wsl]])
                            return wv
                        return f
                    fl_list = [None, None, fl_pair(2), fl_pair(4), fl_pair(6), fl_out()]

                    def fl_issue_upto(k):
                        while fl["issued"] <= k and fl["issued"] < len(fl_list):
                            j = fl["issued"]
                            wsl = next_w()
                            fl["slots"][j] = (wsl, fl_list[j](wsl))
                            fl["issued"] += 1

                    def fl_get(j):
                        fl_issue_upto(j + 1)
                        return fl["slots"][j]

                    def do_gate(j):
                        wsl, wvg = fl_get(0)
                        wv = wvg[:, :, j * 512:(j + 1) * 512]
                        bi, bb = psum7()
                        mm(banks[bi][:, 0:512], [(cT[:, kc, 64:192], wv[:, kc, :]) for kc in range(8)], [bcT, bw[wsl]], [bb])
                        V(lambda e: e.scalar_tensor_tensor(
                            out=gate_p[:, j * 512:(j + 1) * 512], in0=banks[bi][:, 0:512], scalar=0.5, in1=gate_p[:, j * 512:(j + 1) * 512],
                            op0=ALU.mult, op1=ALU.add), [bb, bln], [bgate])
                        if has_s:
                            bi2, bb2 = psum7()
                            mm(banks[bi2][0:64, 0:512], [(cT[:, kc, 0:64], wv[:, kc, :]) for kc in range(8)], [bcT, bw[wsl]], [bb2])
                            V(lambda e: e.scalar_tensor_tensor(
                                out=gate_s[:, j * 512:(j + 1) * 512], in0=banks[bi2][0:64, 0:512], scalar=0.5, in1=gate_s[:, j * 512:(j + 1) * 512],
                                op0=ALU.mult, op1=ALU.add), [bb2, bln], [bgate])
                    for j in range(2):
                        do_gate(j)

                    wcur = {}

                    def do_m(m):
                        mo = (m % 2) * 128
                        wsl, wv = fl_get(1 + m // 2)

                        def do_fg(grp):
                            g0 = grp[0].col
                            NG = sum(t.nt for t in grp)
                            gs = slice(g0, g0 + NG)
                            b_mg, bb_mg = psum7()
                            mm(banks[b_mg][:, 0:NG], [(wv[:, kc, mo:mo + 128], hT[:, kc, gs]) for kc in range(8)], [bw[wsl], bhT], [bb_mg])
                            b_pg, bb_pg = psum7()
                            mm(banks[b_pg][:, 0:NG], [(wv[:, 16 + kc, mo:mo + 128], oT[:, kc, gs]) for kc in range(8)], [bw[wsl]] + boT[0:4], [bb_pg])
                            b_mr, bb_mr = psum7()
                            mm(banks[b_mr][:, 0:NG], [(wv[:, 8 + kc, mo:mo + 128], hT[:, kc, gs]) for kc in range(8)], [bw[wsl], bhT], [bb_mr])
                            b_pr, bb_pr = psum7()
                            mm(banks[b_pr][:, 0:NG], [(wv[:, 24 + kc, mo:mo + 128], oT[:, 8 + kc, gs]) for kc in range(16)], [bw[wsl]] + boT[4:8], [bb_pr])
                            tg, btg = f4.next()
                            trr, btr = f4.next()
                            A(lambda e: e.activation(out=tg[:, 0:NG], in_=banks[b_mg][:, 0:NG], func=AF.Tanh, scale=0.5), [bb_mg], [btg])
                            A(lambda e: e.activation(out=trr[:, 0:NG], in_=banks[b_mr][:, 0:NG], func=AF.Tanh, scale=0.5), [bb_mr], [btr])
                            V(lambda e: e.scalar_tensor_tensor(
                                out=tg[:, 0:NG], in0=tg[:, 0:NG], scalar=1.0, in1=banks[b_pg][:, 0:NG], op0=ALU.add, op1=ALU.mult), [btg, bb_pg], [btg])
                            V(lambda e: e.scalar_tensor_tensor(
                                out=trr[:, 0:NG], in0=trr[:, 0:NG], scalar=1.0, in1=banks[b_pr][:, 0:NG], op0=ALU.add, op1=ALU.mult), [btr, bb_pr], [btr])
                            G(lambda e: e.tensor_tensor(out=mT[:, m, gs], in0=tg[:, 0:NG], in1=trr[:, 0:NG], op=ALU.add), [btg, btr], [bmT])
                        for grp in groups:
                            do_fg(grp)
                    for m in range(8):
                        do_m(m)
                    wslo, wvo = fl_get(5)
                    if pi + 1 < len(PASSES):
                        carry["u0"] = issue_unit_w(0)

                    def do_ft(t):
                        nt = t.nt
                        tsl = slice(t.col, t.col + nt)
                        xt, bx = xringf.next()
                        xsrc = x_s[0:64, :] if t.samp else x_p[t.tok0:t.tok0 + 128, :]
                        xi = xcnt[0] % 2
                        xcnt[0] += 1
                        P.dma("sync", f"x{xi}", lambda e, inc: inc(e.dma_start(out=xt[0:nt, :], in_=xsrc)), W=[bx])
                        gt = gate_s if t.samp else gate_p
                        vt, bvt = ft.next()
                        for n2 in range(2):
                            bi, bb = psum7()
                            mm(banks[bi][0:nt, 0:512], [(mT[:, kc, tsl], wvo[:, kc, n2 * 512:(n2 + 1) * 512]) for kc in range(8)], [bmT, bw[wslo]], [bb])
                            V(lambda e, bi=bi, n2=n2: e.tensor_tensor(
                                out=vt[0:nt, n2 * 512:(n2 + 1) * 512], in0=banks[bi][0:nt, 0:512], in1=gt[0:nt, n2 * 512:(n2 + 1) * 512], op=ALU.mult),
                              [bb, bgate], [bvt])
                        V(lambda e: e.scalar_tensor_tensor(
                            out=vt[0:nt, :], in0=xt[0:nt, :], scalar=ALPHA, in1=vt[0:nt, :], op0=ALU.mult, op1=ALU.add), [bx, bvt], [bvt])
                        sf, bsf = stf_r.next()
                        V(lambda e: e.bn_stats(out=sf[0:nt, 0:6], in_=vt[0:nt, 0:512]), [bvt], [bsf])
                        V(lambda e: e.bn_stats(out=sf[0:nt, 6:12], in_=vt[0:nt, 512:1024]), [bvt], [bsf])
                        V(lambda e: e.bn_aggr(out=sf[0:nt, 12:14], in_=sf[0:nt, 0:12]), [bsf], [bsf])
                        V(lambda e: e.tensor_scalar(out=sf[0:nt, 14:15], in0=sf[0:nt, 13:14], scalar1=EPS, scalar2=None, op0=ALU.add), [bsf], [bsf])
                        G(lambda e: e.tensor_tensor(out=sf[0:nt, 15:16], in0=sf[0:nt, 14:15], in1=mhalf[0:nt, :], op=ALU.pow), [bsf, bconst], [bsf])
                        V(lambda e: e.scalar_tensor_tensor(
                            out=sf[0:nt, 16:17], in0=sf[0:nt, 12:13], scalar=-1.0, in1=sf[0:nt, 15:16], op0=ALU.mult, op1=ALU.mult), [bsf], [bsf])
                        yt, byt = yr.next()
                        A(lambda e: e.activation(
                            out=yt[0:nt, :], in_=vt[0:nt, :], func=AF.Identity, scale=sf[0:nt, 15:16], bias=sf[0:nt, 16:17]), [bvt, bsf], [byt])
                        V(lambda e: e.tensor_tensor(out=yt[0:nt, :], in0=yt[0:nt, :], in1=lng[0:nt, :], op=ALU.mult), [byt, bln], [byt])
                        G(lambda e: e.tensor_tensor(out=yt[0:nt, :], in0=yt[0:nt, :], in1=lnb[0:nt, :], op=ALU.add), [byt, bln], [byt])
                        ydst = y_s[0:64, :] if t.samp else y_p[t.tok0:t.tok0 + 128, :]
                        yl = ylane[0] % 2
                        ylane[0] += 1
                        P.dma("sync", f"y{yl}", lambda e, inc: inc(e.dma_start(out=ydst, in_=yt[0:nt, :])), R=[byt])
                    for t in tiles:
                        do_ft(t)
                    P.barrier(skip=("w0", "w1"))

        for pi, pkeys in enumerate(PASSES):
            do_pass(pi, pkeys)
        P.barrier()
        P.emit()
    return nc, dbg_outs


def _constants():
    half = 128
    inv_freq = (10000.0 ** (-np.arange(half, dtype=np.float32) / np.float32(half))).astype(np.float32)
    pos = np.concatenate([np.arange(TP, dtype=np.int32),
                          np.tile(16384 + np.arange(TS, dtype=np.int32), NS)]).astype(np.float32)
    ang = (pos[:, None] * inv_freq[None, :]).astype(np.float32)
    rope = np.stack([np.cos(ang).T, np.sin(ang).T], axis=1).astype(np.float32)
    lg = np.log1p(-np.exp2(-5.0 - np.arange(4, dtype=np.float32))).astype(np.float32)
    rmask = np.zeros((128, 4, 2, 128), np.float32)
    rdread = np.zeros((128, 4, 2, 128), np.float32)
    rdwrite = np.zeros((128, 8), np.float32)
    idx = np.arange(128)
    for h in range(4):
        for kind, L in ((0, 128), (1, 4)):
            n = 128 if kind == 0 else 64
            s = idx[:n, None]
            t = idx[None, :n]
            same = (s // L) == (t // L)
            diff = (t - s).astype(np.float32)
            m = np.where(same & (t >= s), np.exp(np.maximum(diff, 0.0) * lg[h]), 0.0).astype(np.float32) / np.float32(16.0)
            rmask[:n, h, kind, :n] = m
            rdread[:, h, kind, :n] = np.exp(((idx[:n] % L) + 1.0).astype(np.float32) * lg[h])[None, :]
            rdwrite[:n, 2 * h + kind] = np.exp((L - 1.0 - (idx[:n] % L)).astype(np.float32) * lg[h]) / np.float32(16.0)
    tri = np.zeros((128, 2, 272), np.float32)
    for kind, L in ((0, 128), (1, 4)):
        n = 128 if kind == 0 else 64
        s = idx[:n, None]
        t = idx[None, :n]
        same = (s // L) == (t // L)
        tri[:n, kind, 0:n] = (same & (s <= t)).astype(np.float32)
        tri[:n, kind, 128:128 + n] = (same & (s > t)).astype(np.float32)
        for i in range(16):
            tri[:n, kind, 256 + i] = ((idx[:n] // L) == i).astype(np.float32)
    colmask = np.zeros((128, 16, 64), np.float32)
    rowmask = np.zeros((128, 16), np.float32)
    for i in range(16):
        colmask[:, i, 4 * i:4 * i + 4] = 1.0
        rowmask[4 * i:4 * i + 4, i] = 1.0
    return dict(rope=rope, rmask=rmask, rdread=rdread, rdwrite=rdwrite, tri=tri, colmask=colmask,
                rowmask=rowmask, ident=np.eye(128, dtype=np.float32))


def _in_maps(inp):
    consts = _constants()
    f = lambda a: np.ascontiguousarray(np.asarray(a, dtype=np.float32))
    shared = dict(
        w_ada=f(inp["w_ada"][0]), b_ada=f(inp["b_ada"][0]), w_in=f(inp["w_in"][0]), w_lr2=f(inp["w_lr2"][0]),
        b_lr2=f(inp["b_lr2"][0]), gla_norm_g=f(inp["gla_norm_g"][0]), ret_norm_g=f(inp["ret_norm_g"][0]),
        w_branch_gla=f(inp["w_branch_gla"][0]), w_branch_ret=f(inp["w_branch_ret"][0]), w_out=f(inp["w_out"][0]),
        ln_g=f(inp["ln_g"][0]), ln_b=f(inp["ln_b"][0]), **consts)
    maps = []
    for c in range(NCORES):
        sl = slice(NS * c, NS * (c + 1))
        c_all = np.concatenate([np.repeat(np.asarray(inp["c_sample"][sl]), TS, axis=0),
                                np.repeat(np.asarray(inp["c_prompt"][c:c + 1]), 128, axis=0)], axis=0)
        m = dict(shared)
        m.update(x_p=f(inp["x_prompt"][c]), x_s=f(np.asarray(inp["x_sample"][sl]).reshape(NS * TS, D)), c_all=f(c_all),
                 sg_in=f(inp["state_gla"][0, sl]), sr_in=f(inp["state_ret"][0, sl]))
        maps.append(m)
    return maps


def kernel(**inputs):
    nc, _ = build_program()
    maps = _in_maps(inputs)
    res = run_bass_kernel_spmd(nc, maps, core_ids=list(range(NCORES)))
    r = res.results
    y_p = np.stack([r[c]["y_p"] for c in range(NCORES)], axis=0)
    y_s = np.concatenate([r[c]["y_s"].reshape(NS, TS, D) for c in range(NCORES)], axis=0)
    sgp = np.stack([r[c]["sgp"] for c in range(NCORES)], axis=0)[None]
    srp = np.stack([r[c]["srp"] for c in range(NCORES)], axis=0)[None]
    sgs = np.concatenate([r[c]["sgs"] for c in range(NCORES)], axis=0)[None]
    srs = np.concatenate([r[c]["srs"] for c in range(NCORES)], axis=0)[None]
    return (y_p.astype(np.float32), y_s.astype(np.float32), sgp.astype(np.float32), srp.astype(np.float32),
            sgs.astype(np.float32), srs.astype(np.float32))
```

```python
import math
from contextlib import ExitStack

import numpy as np
import concourse.bass as bass
import concourse.mybir as mybir
from concourse.bass_utils import run_bass_kernel_spmd

F32 = mybir.dt.float32
BF16 = mybir.dt.bfloat16
AF = mybir.ActivationFunctionType
ALU = mybir.AluOpType

NCORES = 8
D = 1024
TP = 2048
NS = 16
TS = 4
NTOK = TP + NS * TS
GQ, GK, GV, GZ, LR, RQ, RK, RV, RZ, MG, MR = 0, 512, 1024, 2048, 3072, 3088, 4112, 5136, 7184, 9232, 10256
DIN = 11280
ALPHA = 2.0 ** 0.25
EPS = 1e-5
PASSES = [list(range(0, 6)), list(range(6, 13)), list(range(13, 16)) + ["s"]]
WSLOT = 8 * 1536


class Buf:
    __slots__ = ("w", "r", "name", "dead")

    def __init__(self, name="", prev=None):
        self.w = None
        self.r = {}
        self.name = name
        self.dead = False
        if prev is not None:
            self.w = prev.w
            self.r = prev.r
            prev.dead = True


class Prog:
    CE = ("tensor", "vector", "scalar", "gpsimd")
    ENG = ("tensor", "vector", "scalar", "gpsimd", "sync")

    def __init__(self, nc, stack):
        self.nc = nc
        self.stack = stack
        self.q = {e: [] for e in self.ENG}
        self.sems = []
        self.cnt = []
        self.esem = {}
        for e in self.CE:
            self.esem[e] = self._newsem("s_" + e)
        self.seen = {e: {} for e in self.ENG}
        self.lanes = {}

    def _newsem(self, name):
        h = self.stack.enter_context(self.nc.semaphore(name))
        self.sems.append(h)
        self.cnt.append(0)
        return len(self.sems) - 1

    def lane(self, name):
        if name not in self.lanes:
            self.lanes[name] = self._newsem("l_" + name)
        return self.lanes[name]

    def _waits(self, eng, R, W):
        need = {}
        for b in list(R) + list(W):
            assert not b.dead, f"use of recycled buffer {b.name}"

        def add(s, v):
            if need.get(s, 0) < v:
                need[s] = v
        for b in R:
            if b.w is not None:
                add(*b.w)
        for b in W:
            if b.w is not None:
                add(*b.w)
            for s, v in b.r.items():
                add(s, v)
        out = []
        seen = self.seen[eng]
        own = self.esem.get(eng) if eng == "tensor" else None
        for s, v in need.items():
            if s == own:
                continue
            if seen.get(s, 0) < v:
                seen[s] = v
                out.append((s, v))
        return out

    def _commit(self, ev, R, W):
        s, v = ev
        for b in R:
            if b.r.get(s, 0) < v:
                b.r[s] = v
        for b in W:
            b.w = ev
            b.r = {}

    def op(self, eng, fn, R=(), W=()):
        waits = self._waits(eng, R, W)
        s = self.esem[eng]
        self.cnt[s] += 1
        ev = (s, self.cnt[s])
        self.q[eng].append((waits, fn, (s, 1), False))
        self._commit(ev, R, W)
        return ev

    def dma(self, eng, lane, fn, R=(), W=(), n=1):
        waits = self._waits(eng, R, W)
        s = self.lane(lane)
        self.cnt[s] += 16 * n
        ev = (s, self.cnt[s])
        self.q[eng].append((waits, fn, (s, 16), True))
        self._commit(ev, R, W)
        return ev

    def seal(self, lane, bufs):
        s = self.lane(lane)
        for b in bufs:
            b.w = (s, self.cnt[s])

    def barrier(self):
        allev = [(s, c) for s, c in enumerate(self.cnt) if c > 0]
        for e in self.ENG:
            seen = self.seen[e]
            waits = []
            for s, v in allev:
                if seen.get(s, 0) < v:
                    seen[s] = v
                    waits.append((s, v))
            if waits:
                self.q[e].append((waits, None, None, False))

    def emit(self):
        nc = self.nc
        sems = self.sems

        def replay(name, e):
            for waits, fn, inc, is_dma in self.q[name]:
                for s, v in waits:
                    e.wait_ge(sems[s], v)
                if fn is None:
                    continue
                if is_dma:
                    s, amt = inc
                    fn(e, lambda ins, _s=s, _a=amt: ins.then_inc(sems[_s], _a))
                else:
                    ins = fn(e)
                    ins.then_inc(sems[inc[0]], inc[1])

        with nc.Block() as block:
            @block.tensor
            def _(e):
                replay("tensor", e)

            @block.vector
            def _(e):
                replay("vector", e)

            @block.scalar
            def _(e):
                replay("scalar", e)

            @block.gpsimd
            def _(e):
                replay("gpsimd", e)

            @block.sync
            def _(e):
                replay("sync", e)


class Ring:
    def __init__(self, nc, stack, name, shape, dt, n):
        self.t = [stack.enter_context(nc.sbuf_tensor(f"{name}{i}", list(shape), dt)) for i in range(n)]
        self.b = [Buf(f"{name}{i}") for i in range(n)]
        self.i = 0
        self.n = n

    def next(self):
        i = self.i
        self.i = (i + 1) % self.n
        self.b[i] = Buf(self.b[i].name, prev=self.b[i])
        return self.t[i], self.b[i]


class TileInfo:
    def __init__(self, key, col):
        self.key = key
        self.samp = (key == "s")
        self.kind = 1 if self.samp else 0
        self.nt = 64 if self.samp else 128
        self.tok0 = TP if self.samp else key * 128
        self.col = col


def build_program(debug=None):
    nc = bass.Bass("TRN2", target_bir_lowering=False)
    din = lambda n, s: nc.dram_tensor(n, list(s), F32, kind="ExternalInput").ap()
    dout = lambda n, s: nc.dram_tensor(n, list(s), F32, kind="ExternalOutput").ap()
    x_p = din("x_p", [TP, D])
    x_s = din("x_s", [64, D])
    c_all = din("c_all", [192, D])
    sg_in = din("sg_in", [NS, 4, 128, 256])
    sr_in = din("sr_in", [NS, 4, 256, 512])
    w_ada = din("w_ada", [D, 3 * D])
    b_ada = din("b_ada", [3 * D])
    w_in = din("w_in", [D, DIN])
    w_lr2 = din("w_lr2", [16, 512])
    b_lr2 = din("b_lr2", [512])
    gng = din("gla_norm_g", [1024])
    rng_ = din("ret_norm_g", [2048])
    wbg = din("w_branch_gla", [1024, D])
    wbr = din("w_branch_ret", [2048, D])
    w_out = din("w_out", [D, D])
    ln_g = din("ln_g", [D])
    ln_b = din("ln_b", [D])
    rope = din("rope", [128, 2, NTOK])
    rmask = din("rmask", [128, 4, 2, 128])
    rdread = din("rdread", [128, 4, 2, 128])
    rdwrite = din("rdwrite", [128, 8])
    tri = din("tri", [128, 2, 272])
    colmask = din("colmask", [128, 16, 64])
    rowmask = din("rowmask", [128, 16])
    ident = din("ident", [128, 128])
    y_p = dout("y_p", [TP, D])
    y_s = dout("y_s", [64, D])
    sgp = dout("sgp", [4, 128, 256])
    srp = dout("srp", [4, 256, 512])
    sgs = dout("sgs", [NS, 4, 128, 256])
    srs = dout("srs", [NS, 4, 256, 512])
    dbg_outs = {}

    w_in_v = w_in.rearrange("(kc p) n -> p kc n", p=128)
    w_ada_v = w_ada.rearrange("(kc p) n -> p kc n", p=128)
    wbg_v = wbg.rearrange("(kc p) n -> p kc n", p=128)
    wbr_v = wbr.rearrange("(kc p) n -> p kc n", p=128)
    w_out_v = w_out.rearrange("(kc p) n -> p kc n", p=128)

    lg = [math.log1p(-2.0 ** (-5 - h)) for h in range(4)]
    dchunk = [[math.exp(128 * lg[h]), math.exp(4 * lg[h])] for h in range(4)]

    with ExitStack() as stack:
        P = Prog(nc, stack)

        def sb(name, shape, dt, st=stack):
            return st.enter_context(nc.sbuf_tensor(name, list(shape), dt))

        banks = [stack.enter_context(nc.psum_tensor(f"pb{i}", [128, 512], F32)) for i in range(8)]
        banks_bf = [b.bitcast(BF16) for b in banks]
        bbuf = [Buf(f"pb{i}") for i in range(8)]
        ring_i = [0]

        main_banks = [[0, 1, 2, 3, 4]]
        oring_i = [0]

        def psum():
            mb = main_banks[0]
            i = mb[ring_i[0] % len(mb)]
            ring_i[0] += 1
            bbuf[i] = Buf(bbuf[i].name, prev=bbuf[i])
            return i, bbuf[i]

        ring7_i = [0]

        def psum7():
            i = ring7_i[0]
            ring7_i[0] = (i + 1) % 8
            bbuf[i] = Buf(bbuf[i].name, prev=bbuf[i])
            return i, bbuf[i]

        def psum_o():
            i = 5 + oring_i[0]
            oring_i[0] = (oring_i[0] + 1) % 2
            bbuf[i] = Buf(bbuf[i].name, prev=bbuf[i])
            return i, bbuf[i]

        def dbg(name, ap, buf, shape, dt=F32):
            if debug is None or name not in debug:
                return
            d = nc.dram_tensor("dbg_" + name, list(shape), dt, kind="ExternalOutput").ap()
            dbg_outs[name] = d
            P.dma("sync", "dbg", lambda e, inc: inc(e.dma_start(out=d, in_=ap)), R=[buf])
            P.seal("dbg", [])

        def mm(out_ap, pairs, R, W):
            def fn(e):
                n = len(pairs)
                for i, (l, r) in enumerate(pairs):
                    ins = e.matmul(out=out_ap, lhsT=l, rhs=r, start=(i == 0), stop=(i == n - 1))
                return ins
            P.op("tensor", fn, R=R, W=W)

        def tr(specs, R, W):
            def fn(e):
                for o, i, idn in specs:
                    ins = e.transpose(out=o, in_=i, identity=idn)
                return ins
            P.op("tensor", fn, R=R, W=W)

        def V(fn, R, W):
            P.op("vector", fn, R=R, W=W)

        def A(fn, R, W):
            P.op("scalar", fn, R=R, W=W)

        def G(fn, R, W):
            P.op("gpsimd", fn, R=R, W=W)

        ident_f = sb("ident_f", [128, 128], F32)
        ident_b = sb("ident_b", [128, 128], BF16)
        tri_sb = sb("tri_sb", [128, 2, 272], F32)
        cmask_sb = sb("cmask_sb", [128, 16, 64], BF16)
        rowm_sb = sb("rowm_sb", [128, 16], F32)
        rdw_sb = sb("rdw_sb", [128, 8], F32)
        blr_bc = sb("blr_bc", [128, 512], F32)
        wlr2_b = sb("wlr2_b", [16, 512], BF16)
        gnT = sb("gnT", [128, 24], F32)
        badaT = sb("badaT", [128, 24], F32)
        mhalf = sb("mhalf", [128, 1], F32)
        adaT = sb("adaT", [128, 16, 65], F32)
        cT = sb("cT", [128, 8, 192], BF16)
        S_gla = sb("S_gla", [128, 4, 256], F32)
        S_ret = sb("S_ret", [128, 4, 2, 512], F32)
        bS_gla = [Buf(f"Sg{h}") for h in range(4)]
        bS_ret = [Buf(f"Sr{h}") for h in range(4)]
        wslot = [sb(f"wslot{i}", [128, WSLOT], BF16) for i in range(2)]
        bw = [Buf("w0"), Buf("w1")]
        wi = [0]

        def next_w():
            i = wi[0]
            wi[0] = 1 - i
            return i

        xcnt = [0]
        bconst = Buf("const")
        bada = Buf("ada")
        bcT = Buf("cT")

        def ld(dst, src, **kw):
            P.dma("sync", "c", lambda e, inc: inc(e.dma_start(out=dst, in_=src, **kw)), W=[bconst])

        ld(ident_f[:, :], ident[:, :])
        ld(tri_sb[:, :, :], tri[:, :, :])
        ld(rowm_sb[:, :], rowmask[:, :])
        ld(rdw_sb[:, :], rdwrite[:, :])
        ld(blr_bc[:, :], b_lr2.partition_broadcast(128))
        ld(gnT[:, 0:8], gng.rearrange("(m p) -> p m", p=128), allow_slow_non_contiguous=True)
        ld(gnT[:, 8:24], rng_.rearrange("(m p) -> p m", p=128), allow_slow_non_contiguous=True)
        ld(badaT[:, :], b_ada.rearrange("(m p) -> p m", p=128), allow_slow_non_contiguous=True)
        P.dma("gpsimd", "cw", lambda e, inc: inc(e.dma_start(out=ident_b[:, :], in_=ident[:, :])), W=[bconst])
        P.dma("gpsimd", "cw", lambda e, inc: inc(e.dma_start(out=wlr2_b[:, :], in_=w_lr2[:, :])), W=[bconst])
        P.dma("gpsimd", "cw", lambda e, inc: inc(e.dma_start(out=cmask_sb[:, :, :], in_=colmask[:, :, :])), W=[bconst])
        P.barrier()
        G(lambda e: e.memset(mhalf[:, :], -0.5), [], [bconst])
        V(lambda e: e.tensor_scalar(out=gnT[:, 0:8], in0=gnT[:, 0:8], scalar1=0.5, scalar2=None, op0=ALU.mult), [bconst], [bconst])

        with ExitStack() as st0:
            cA = sb("cA", [64, D], F32, st0)
            cB = sb("cB", [128, D], F32, st0)
            bc = Buf("c")
            P.dma("sync", "c2", lambda e, inc: inc(e.dma_start(out=cA[:, :], in_=c_all[0:64, :])), W=[bc])
            P.dma("sync", "c2", lambda e, inc: inc(e.dma_start(out=cB[:, :], in_=c_all[64:192, :])), W=[bc])
            P.seal("c2", [bc])
            for k2 in range(4):
                bi, bb = psum()
                specs = []
                for j in range(2):
                    kc = 2 * k2 + j
                    specs.append((banks[bi][:, j * 192:j * 192 + 64], cA[:, kc * 128:(kc + 1) * 128], ident_f[0:64, 0:64]))
                    specs.append((banks[bi][:, j * 192 + 64:j * 192 + 192], cB[:, kc * 128:(kc + 1) * 128], ident_f[:, :]))
                tr(specs, [bc, bconst], [bb])
                A(lambda e, bi=bi, k2=k2: e.activation(
                    out=cT[:, 2 * k2:2 * k2 + 2, :], in_=banks[bi][:, 0:384].rearrange("p (j n) -> p j n", j=2),
                    func=AF.Copy), [bb], [bcT])
            for j in range(4):
                wsl = next_w()
                wv = wslot[wsl][:, 0:4096].rearrange("p (k n) -> p k n", k=8)
                P.dma("gpsimd", f"w{wsl}", lambda e, inc, wv=wv, j=j: inc(e.dma_start(
                    out=wv, in_=w_ada_v[:, :, j * 512:(j + 1) * 512])), W=[bw[wsl]])
                for mm_ in range(4):
                    m = 4 * j + mm_
                    bi, bb = psum()
                    mm(banks[bi][:, 0:65], [(wv[:, kc, mm_ * 128:(mm_ + 1) * 128], cT[:, kc, 0:65]) for kc in range(8)],
                       [bw[wsl], bcT], [bb])
                    V(lambda e, bi=bi, m=m: e.tensor_scalar(
                        out=adaT[:, m, :], in0=banks[bi][:, 0:65], scalar1=badaT[:, m:m + 1],
                        scalar2=(1.0 if m >= 8 else 0.0), op0=ALU.add, op1=ALU.add), [bb, bconst], [bada])
            P.barrier()

        def do_pass(pi, pkeys):
            tiles = []
            col = 0
            for k in pkeys:
                t = TileInfo(k, col)
                tiles.append(t)
                col += t.nt
            NP = col
            groups = []
            cur = []
            for t in tiles:
                if t.samp:
                    if cur:
                        groups.append(cur)
                    groups.append([t])
                    cur = []
                else:
                    cur.append(t)
                    if len(cur) == 4:
                        groups.append(cur)
                        cur = []
            if cur:
                groups.append(cur)
            has_s = any(t.samp for t in tiles)
            main_banks[0] = [0, 1, 2, 3, 4] if has_s else [0, 1, 2, 3, 4, 7]
            ptiles = [t for t in tiles if not t.samp]
            first_is_zero = (ptiles[0].key == 0)
            last_pass = (ptiles[-1].key == 15)

            def issue_unit_w(u):
                gla = u < 4
                h = u % 4
                wsl = next_w()
                if gla:
                    ncol = 784
                    segs = [(0, GQ + h * 128, 128), (128, GK + h * 128, 128), (256, GV + h * 256, 256),
                            (512, GZ + h * 256, 256), (768, LR, 16)]
                else:
                    ncol = 1536
                    segs = [(0, RQ + h * 256, 256), (256, RK + h * 256, 256), (512, RV + h * 512, 512),
                            (1024, RZ + h * 512, 512)]
                wv = wslot[wsl][:, 0:8 * ncol].rearrange("p (k n) -> p k n", k=8)

                def wload(e, inc):
                    for d0, s0, n in segs:
                        inc(e.dma_start(out=wv[:, :, d0:d0 + n], in_=w_in_v[:, :, s0:s0 + n]))
                P.dma("gpsimd", f"w{wsl}", wload, W=[bw[wsl]], n=len(segs))
                return wsl, wv
            pre_w = {0: issue_unit_w(0), 1: issue_unit_w(1)}

            with ExitStack() as stp:
                hT = sb(f"hT{pi}", [128, 8, NP], BF16, stp)
                oT = sb(f"oT{pi}", [128, 24, NP], BF16, stp)
                bhT = Buf("hT")
                boT = [Buf(f"oT{u}") for u in range(8)]

                with ExitStack() as stq:
                    tmpr = Ring(nc, stq, f"htmp{pi}", [128, 64], F32, 2) if has_s else None
                    xring = Ring(nc, stq, f"xtq{pi}", [128, D], F32, 4)
                    xqc = [0]

                    def do_pt(t):
                        xt, bx = xring.next()
                        xsrc = x_s[0:64, :] if t.samp else x_p[t.tok0:t.tok0 + 128, :]
                        xi = xqc[0] % 4
                        xqc[0] += 1
                        nt = t.nt
                        P.dma("sync", f"xq{xi}", lambda e, inc: inc(e.dma_start(out=xt[0:nt, :], in_=xsrc)), W=[bx])
                        for half in range(2):
                            bi, bb = psum()
                            tr([(banks[bi][:, j * 128:j * 128 + nt], xt[0:nt, (half * 4 + j) * 128:(half * 4 + j + 1) * 128],
                                 ident_f[0:nt, 0:nt]) for j in range(4)], [bx, bconst], [bb])
                            for j in range(4):
                                kc = half * 4 + j
                                src = banks[bi][:, j * 128:j * 128 + nt]
                                dst = hT[:, kc, t.col:t.col + nt]
                                if not t.samp:
                                    if j % 2 == 0:
                                        A(lambda e, src=src, dst=dst, kc=kc: e.activation(
                                            out=dst, in_=src, func=AF.Identity, scale=adaT[:, 8 + kc, 64:65],
                                            bias=adaT[:, kc, 64:65]), [bb, bada], [bhT])
                                    else:
                                        V(lambda e, src=src, dst=dst, kc=kc: e.tensor_scalar(
                                            out=dst, in0=src, scalar1=adaT[:, 8 + kc, 64:65], scalar2=adaT[:, kc, 64:65],
                                            op0=ALU.mult, op1=ALU.add), [bb, bada], [bhT])
                                else:
                                    tm, btm = tmpr.next()
                                    V(lambda e, src=src, tm=tm, kc=kc: e.tensor_tensor(
                                        out=tm[:, :], in0=src, in1=adaT[:, 8 + kc, 0:64], op=ALU.mult), [bb, bada], [btm])
                                    V(lambda e, dst=dst, tm=tm, kc=kc: e.tensor_tensor(
                                        out=dst, in0=tm[:, :], in1=adaT[:, kc, 0:64], op=ALU.add), [btm, bada], [bhT])
                    for t in tiles:
                        do_pt(t)
                    P.barrier()

                with ExitStack() as stu:
                    rope_sb = sb(f"rope{pi}", [128, 2, NP], F32, stu)
                    brope = Buf("rope")
                    for cs in range(2):
                        for t in tiles:
                            P.dma("sync", "rope", lambda e, inc, cs=cs, t=t: inc(e.dma_start(
                                out=rope_sb[:, cs, t.col:t.col + t.nt], in_=rope[:, cs, t.tok0:t.tok0 + t.nt])), W=[brope])
                    P.seal("rope", [brope])
                    f32g = Ring(nc, stu, f"f32g{pi}", [128, 512], F32, 4)
                    bfg = Ring(nc, stu, f"bfg{pi}", [128, 2, 512], BF16, 6)
                    lrT_r = Ring(nc, stu, f"lrT{pi}", [16, 512], BF16, 2)
                    sm128 = Ring(nc, stu, f"sm{pi}", [128, 128], F32, 4)
                    vbf_r = Ring(nc, stu, f"vbf{pi}", [128, 512], BF16, 2)
                    th_r = Ring(nc, stu, f"th{pi}", [128, 512], F32, 1)
                    zs_r = Ring(nc, stu, f"zs{pi}", [128, 512], F32, 3)
                    am_r = Ring(nc, stu, f"am{pi}", [128, 128], BF16, 2)
                    kw_r = Ring(nc, stu, f"kw{pi}", [128, 256], BF16, 2)
                    sbf_r = Ring(nc, stu, f"sbf{pi}", [128, 2, 512], BF16, 3)
                    on_r = Ring(nc, stu, f"on{pi}", [128, 512], F32, 3)
                    t512 = Ring(nc, stu, f"t512{pi}", [128, 512], F32, 2)
                    st_r = Ring(nc, stu, f"st{pi}", [128, 16], F32, 4)
                    dsd_r = Ring(nc, stu, f"dsd{pi}", [128, 16], F32, 8)
                    mk_r = Ring(nc, stu, f"mk{pi}", [128, 2, 2, 128], F32, 2)
                    if has_s:
                        sin_r = Ring(nc, stu, f"sin{pi}", [128, 2, 512], F32, 4)
                        qm_r = Ring(nc, stu, f"qm{pi}", [128, 2, 64], BF16, 2)
                        sinb_r = Ring(nc, stu, f"sinb{pi}", [128, 2, 512], BF16, 2)
                        km_r = Ring(nc, stu, f"km{pi}", [64, 256], BF16, 2)
                        oint = sb(f"oint{pi}", [64, 512], F32, stu)
                        boint = Buf("oint")
                    lane_ctr = {"sin": 0, "sout": 0, "mk": 0}
                    if has_s:
                        bfg_s = Ring(nc, stu, f"bfgs{pi}", [128, 2, 64], BF16, 6)
                        dsd_s = Ring(nc, stu, f"dsds{pi}", [128, 16], F32, 2)
                        vbf_s = Ring(nc, stu, f"vbfs{pi}", [64, 512], BF16, 2)
                        zs_s = Ring(nc, stu, f"zss{pi}", [64, 512], F32, 2)
                        am_s = Ring(nc, stu, f"ams{pi}", [64, 64], BF16, 2)
                        kw_s = Ring(nc, stu, f"kws{pi}", [64, 256], BF16, 2)
                    seq_per_step = [NS]

                    def do_unit(u):
                        gla = u < 4
                        h = u % 4
                        dkc = 1 if gla else 2
                        dv = 256 if gla else 512
                        ochunk0 = (h * 2) if gla else (8 + h * 4)
                        nch = dv // 128
                        wsl, wv = pre_w.pop(u) if u in pre_w else issue_unit_w(u)
                        bwu = bw[wsl]
                        mk = bmk = None
                        if not gla:
                            mk, bmk = mk_r.next()
                            ml = lane_ctr["mk"] % 2
                            lane_ctr["mk"] += 1

                            def mkload(e, inc):
                                inc(e.dma_start(out=mk[:, 0, :, :], in_=rmask[:, h, :, :]))
                                inc(e.dma_start(out=mk[:, 1, :, :], in_=rdread[:, h, :, :]))
                            P.dma("sync", f"mk{ml}", mkload, W=[bmk], n=2)
                        bS = bS_gla[h] if gla else bS_ret[h]
                        us = {"valid": not first_is_zero, "sbf": None, "bsbf": None}

                        def init_sbf():
                            sbf0, bsbf0 = sbf_r.next()
                            us["sbf"], us["bsbf"] = sbf0, bsbf0
                            if gla:
                                A(lambda e: e.activation(out=sbf0[:, 0, 0:256], in_=S_gla[:, h, :], func=AF.Copy), [bS], [bsbf0])
                            else:
                                A(lambda e: e.activation(out=sbf0[:, :, :], in_=S_ret[:, h, :, :], func=AF.Copy), [bS], [bsbf0])

                        def do_group(grp):
                            g0 = grp[0].col
                            NG = sum(t.nt for t in grp)
                            gs = slice(g0, g0 + NG)
                            dsd_of = {}
                            if gla:
                                bi0, bb0 = psum()
                                mm(banks[bi0][0:16, 0:NG], [(wv[:, kc, 768:784], hT[:, kc, gs]) for kc in range(8)], [bwu, bhT], [bb0])
                                lrT, blrT = lrT_r.next()
                                A(lambda e: e.activation(out=lrT[:, 0:NG], in_=banks[bi0][0:16, 0:NG], func=AF.Copy), [bb0], [blrT])
                                Ep, bEp = f32g.next()
                                En, bEn = f32g.next()
                                Er, bEr = f32g.next()

                                def do_decay(t):
                                    lc = t.col - g0
                                    nt = t.nt
                                    bi, bb = psum()
                                    mm(banks[bi][0:nt, 0:128], [(lrT[:, lc:lc + nt], wlr2_b[:, h * 128:(h + 1) * 128])], [blrT, bconst], [bb])
                                    zb, bzb = sm128.next()
                                    V(lambda e: e.tensor_tensor(
                                        out=zb[0:nt, :], in0=banks[bi][0:nt, 0:128], in1=blr_bc[0:nt, h * 128:(h + 1) * 128], op=ALU.add),
                                      [bb, bconst], [bzb])
                                    A(lambda e: e.activation(out=zb[0:nt, :], in_=zb[0:nt, :], func=AF.Exp, scale=-1.0), [bzb], [bzb])
                                    lsb, blsb = sm128.next()
                                    A(lambda e: e.activation(out=lsb[0:nt, :], in_=zb[0:nt, :], func=AF.Ln, bias=1.0), [bzb], [blsb])
                                    bi2, bb2 = psum()
                                    mm(banks[bi2][:, 0:272], [(lsb[0:nt, :], tri_sb[0:nt, t.kind, :])], [blsb, bconst], [bb2])
                                    A(lambda e: e.activation(out=Ep[:, lc:lc + nt], in_=banks[bi2][:, 0:nt], func=AF.Exp, scale=-1.0 / 16), [bb2], [bEp])
                                    A(lambda e: e.activation(out=En[:, lc:lc + nt], in_=banks[bi2][:, 0:nt], func=AF.Exp, scale=1.0 / 16), [bb2], [bEn])
                                    A(lambda e: e.activation(out=Er[:, lc:lc + nt], in_=banks[bi2][:, 128:128 + nt], func=AF.Exp, scale=-1.0 / 16), [bb2], [bEr])
                                    dsd, bdsd = (dsd_s if t.samp else dsd_r).next()
                                    A(lambda e: e.activation(out=dsd[:, :], in_=banks[bi2][:, 256:272], func=AF.Exp, scale=-1.0 / 16), [bb2], [bdsd])
                                    dsd_of[t.key] = (dsd, bdsd)
                                for t in grp:
                                    do_decay(t)
                                biq, bbq = psum()
                                mm(banks[biq][:, 0:NG], [(wv[:, kc, 0:128], hT[:, kc, gs]) for kc in range(8)], [bwu, bhT], [bbq])
                                bik, bbk = psum()
                                mm(banks[bik][:, 0:NG], [(wv[:, kc, 128:256], hT[:, kc, gs]) for kc in range(8)], [bwu, bhT], [bbk])
                                bfgx = bfg_s if grp[0].samp else bfg
                                qd, bqd = bfgx.next()
                                kd, bkd = bfgx.next()
                                kwT, bkwT = bfgx.next()
                                V(lambda e: e.scalar_tensor_tensor(
                                    out=qd[:, 0, 0:NG], in0=banks[biq][:, 0:NG], scalar=128.0 ** -0.5, in1=Ep[:, 0:NG],
                                    op0=ALU.mult, op1=ALU.mult), [bbq, bEp], [bqd])
                                V(lambda e: e.tensor_tensor(out=kd[:, 0, 0:NG], in0=banks[bik][:, 0:NG], in1=En[:, 0:NG], op=ALU.mult), [bbk, bEn], [bkd])
                                V(lambda e: e.tensor_tensor(out=kwT[:, 0, 0:NG], in0=banks[bik][:, 0:NG], in1=Er[:, 0:NG], op=ALU.mult), [bbk, bEr], [bkwT])
                                qA, bqA, kA, bkA, qS, bqS, kW, bkW = qd, bqd, kd, bkd, qd, bqd, kwT, bkwT
                            else:
                                cosg = rope_sb[:, 0, gs]
                                sing = rope_sb[:, 1, gs]

                                def do_rot(which):
                                    b0, bb0 = psum()
                                    mm(banks[b0][:, 0:NG], [(wv[:, kc, which * 256:which * 256 + 128], hT[:, kc, gs]) for kc in range(8)],
                                       [bwu, bhT], [bb0])
                                    b1, bb1 = psum()
                                    mm(banks[b1][:, 0:NG], [(wv[:, kc, which * 256 + 128:which * 256 + 256], hT[:, kc, gs]) for kc in range(8)],
                                       [bwu, bhT], [bb1])
                                    t1, bt1 = f32g.next()
                                    t2, bt2 = f32g.next()
                                    t3, bt3 = f32g.next()
                                    t4, bt4 = f32g.next()
                                    V(lambda e: e.tensor_tensor(out=t1[:, 0:NG], in0=banks[b0][:, 0:NG], in1=cosg, op=ALU.mult), [bb0, brope], [bt1])
                                    V(lambda e: e.tensor_tensor(out=t2[:, 0:NG], in0=banks[b1][:, 0:NG], in1=sing, op=ALU.mult), [bb1, brope], [bt2])
                                    V(lambda e: e.tensor_tensor(out=t3[:, 0:NG], in0=banks[b0][:, 0:NG], in1=sing, op=ALU.mult), [bb0, brope], [bt3])
                                    V(lambda e: e.tensor_tensor(out=t4[:, 0:NG], in0=banks[b1][:, 0:NG], in1=cosg, op=ALU.mult), [bb1, brope], [bt4])
                                    rot, brot = (bfg_s if grp[0].samp else bfg).next()
                                    G(lambda e: e.tensor_tensor(out=rot[:, 0, 0:NG], in0=t1[:, 0:NG], in1=t2[:, 0:NG], op=ALU.subtract), [bt1, bt2], [brot])
                                    G(lambda e: e.tensor_tensor(out=rot[:, 1, 0:NG], in0=t3[:, 0:NG], in1=t4[:, 0:NG], op=ALU.add), [bt3, bt4], [brot])
                                    return rot, brot
                                qr, bqr = do_rot(0)
                                kr, bkr = do_rot(1)
                                qrd, bqrd = (bfg_s if grp[0].samp else bfg).next()
                                for t in grp:
                                    V(lambda e, t=t, lc=t.col - g0: e.tensor_tensor(
                                        out=qrd[:, :, lc:lc + t.nt], in0=qr[:, :, lc:lc + t.nt],
                                        in1=mk[:, 1, t.kind:t.kind + 1, 0:t.nt].to_broadcast([128, 2, t.nt]), op=ALU.mult), [bqr, bmk], [bqrd])
                                qA, bqA, kA, bkA, qS, bqS, kW, bkW = qr, bqr, kr, bkr, qrd, bqrd, kr, bkr

                            def do_tile(t):
                                lc = t.col - g0
                                nt = t.nt
                                tsl = slice(t.col, t.col + nt)
                                vbf, bvbf = (vbf_s if t.samp else vbf_r).next()
                                th, bth = th_r.next()
                                zs, bzs = (zs_s if t.samp else zs_r).next()
                                if gla:
                                    biv, bbv = psum()
                                    mm(banks[biv][0:nt, 0:512], [(hT[:, kc, tsl], wv[:, kc, 256:768]) for kc in range(8)], [bhT, bwu], [bbv])
                                    vsrc = banks[biv][0:nt, 0:256]
                                    zsrc = banks[biv][0:nt, 256:512]
                                    bbz = bbv
                                else:
                                    biv, bbv = psum()
                                    mm(banks[biv][0:nt, 0:512], [(hT[:, kc, tsl], wv[:, kc, 512:1024]) for kc in range(8)], [bhT, bwu], [bbv])
                                    vsrc = banks[biv][0:nt, 0:512]
                                    biz, bbz = psum()
                                    mm(banks[biz][0:nt, 0:512], [(hT[:, kc, tsl], wv[:, kc, 1024:1536]) for kc in range(8)], [bhT, bwu], [bbz])
                                    zsrc = banks[biz][0:nt, 0:512]
                                A(lambda e: e.activation(out=vbf[0:nt, 0:dv], in_=vsrc, func=AF.Copy), [bbv], [bvbf])
                                if gla:
                                    A(lambda e: e.activation(out=th[0:nt, 0:dv], in_=zsrc, func=AF.Tanh, scale=0.5), [bbz], [bth])
                                    V(lambda e: e.scalar_tensor_tensor(
                                        out=zs[0:nt, 0:dv], in0=th[0:nt, 0:dv], scalar=1.0, in1=zsrc, op0=ALU.add, op1=ALU.mult), [bth, bbz], [bzs])
                                else:
                                    A(lambda e: e.activation(out=zs[0:nt, 0:dv], in_=zsrc, func=AF.Silu), [bbz], [bzs])
                                bia, bba = psum()
                                mm(banks[bia][0:nt, 0:nt], [(kA[:, c, lc:lc + nt], qA[:, c, lc:lc + nt]) for c in range(dkc)], [bkA, bqA], [bba])
                                am, bam = (am_s if t.samp else am_r).next()
                                if gla:
                                    msk = tri_sb[0:nt, t.kind, 0:nt]
                                    bmsk = bconst
                                else:
                                    msk = mk[0:nt, 0, t.kind, 0:nt]
                                    bmsk = bmk
                                V(lambda e: e.tensor_tensor(out=am[0:nt, 0:nt], in0=banks[bia][0:nt, 0:nt], in1=msk, op=ALU.mult), [bba, bmsk], [bam])
                                osl = {}

                                def issue_o():
                                    bio, bbo = psum_o()
                                    pairs = [(am[0:nt, 0:nt], vbf[0:nt, 0:dv])]
                                    Rl = [bam, bvbf]
                                    if (not t.samp) and us["valid"]:
                                        if us["sbf"] is None:
                                            init_sbf()
                                        sbfc = us["sbf"]
                                        pairs += [(qS[:, c, lc:lc + nt], sbfc[:, c, 0:dv]) for c in range(dkc)]
                                        Rl += [bqS, us["bsbf"]]
                                    mm(banks[bio][0:nt, 0:dv], pairs, Rl, [bbo])
                                    osl["bio"], osl["bbo"] = bio, bbo
                                bit, bbt = psum()
                                tr([(banks_bf[bit][0:nt, c * 128:(c + 1) * 128], kW[:, c, lc:lc + nt], ident_b[:, :]) for c in range(dkc)],
                                   [bkW, bconst], [bbt])
                                kw, bkw = (kw_s if t.samp else kw_r).next()
                                if gla:
                                    A(lambda e: e.activation(out=kw[0:nt, 0:128], in_=banks_bf[bit][0:nt, 0:128], func=AF.Copy), [bbt], [bkw])
                                else:
                                    A(lambda e: e.activation(
                                        out=kw[0:nt, 0:256], in_=banks_bf[bit][0:nt, 0:256], func=AF.Identity,
                                        scale=rdw_sb[0:nt, 2 * h + t.kind:2 * h + t.kind + 1]), [bbt, bconst], [bkw])
                                yield
                                if not t.samp:
                                    issue_o()
                                    o_src = banks[osl["bio"]][0:nt, 0:dv]
                                    bo_src = osl["bbo"]
                                    nsbf, bnsbf = sbf_r.next()
                                    valid = us["valid"]

                                    def do_c(c):
                                        bi, bb = psum()
                                        mm(banks[bi][:, 0:dv], [(kw[0:nt, c * 128:(c + 1) * 128], vbf[0:nt, 0:dv])], [bkw, bvbf], [bb])
                                        Sd = S_gla[:, h, :] if gla else S_ret[:, h, c, :]
                                        if not valid:
                                            V(lambda e: e.tensor_copy(out=Sd, in_=banks[bi][:, 0:dv]), [bb], [bS])
                                        elif gla:
                                            dsd, bdsd = dsd_of[t.key]
                                            V(lambda e: e.scalar_tensor_tensor(
                                                out=Sd, in0=Sd, scalar=dsd[:, 0:1], in1=banks[bi][:, 0:dv], op0=ALU.mult, op1=ALU.add),
                                              [bb, bdsd, bS], [bS])
                                        else:
                                            V(lambda e: e.scalar_tensor_tensor(
                                                out=Sd, in0=Sd, scalar=dchunk[h][0], in1=banks[bi][:, 0:dv], op0=ALU.mult, op1=ALU.add),
                                              [bb, bS], [bS])
                                        A(lambda e: e.activation(out=nsbf[:, c, 0:dv], in_=Sd, func=AF.Copy), [bS], [bnsbf])
                                    for c in range(dkc):
                                        do_c(c)
                                    us["sbf"], us["bsbf"] = nsbf, bnsbf
                                    us["valid"] = True
                                    if last_pass and t.key == 15:
                                        if gla:
                                            P.dma("sync", "stp", lambda e, inc: inc(e.dma_start(out=sgp[h, :, :], in_=S_gla[:, h, :])), R=[bS])
                                        else:
                                            P.dma("sync", "stp", lambda e, inc: inc(e.dma_start(
                                                out=srp[h, :, :].rearrange("(c p) v -> p c v", p=128), in_=S_ret[:, h, :, :])), R=[bS])
                                else:
                                    loads = {}
                                    lanes_of = {}
                                    bbuf[7] = Buf("pb7", prev=bbuf[7])
                                    b7 = bbuf[7]

                                    def issue_load(i):
                                        if gla and i % 4 != 0:
                                            loads[i] = loads[i - 1]
                                            lanes_of[i] = lanes_of[i - 1]
                                            return
                                        sin_, bsin = sin_r.next()
                                        li = lane_ctr["sin"] % 4
                                        lane_ctr["sin"] += 1
                                        lanes_of[i] = li
                                        if gla:
                                            P.dma("sync", f"sin{li}", lambda e, inc: inc(e.dma_start(
                                                out=sin_[:, :, :].rearrange("p c (s v) -> p (c s) v", v=256),
                                                in_=sg_in[i:i + 4, h, :, :].rearrange("s d v -> d s v"))), W=[bsin])
                                        else:
                                            P.dma("sync", f"sin{li}", lambda e, inc: inc(e.dma_start(
                                                out=sin_[:, :, :], in_=sr_in[i, h, :, :].rearrange("(c p) v -> p c v", p=128))), W=[bsin])
                                        loads[i] = (sin_, bsin)
                                    if gla:
                                        for i0 in range(NS):
                                            issue_load(i0)
                                    else:
                                        issue_load(0)
                                        issue_load(1)
                                        issue_load(2)

                                    def st_ap(sin_, i):
                                        if gla:
                                            return sin_[:, :, :].rearrange("p c (s v) -> p (c s) v", v=256)[:, i % 4, :]
                                        return None

                                    preps = {}

                                    def prep_seq(i):
                                        sin_, bsin = loads[i]
                                        qm, bqm = qm_r.next()
                                        V(lambda e: e.tensor_tensor(
                                            out=qm[:, 0:dkc, :], in0=qS[:, 0:dkc, lc:lc + 64],
                                            in1=cmask_sb[:, i:i + 1, :].to_broadcast([128, dkc, 64]), op=ALU.mult), [bqS, bconst], [bqm])
                                        sinb, bsinb = sinb_r.next()
                                        if gla:
                                            A(lambda e: e.activation(out=sinb[:, 0, 0:256], in_=st_ap(sin_, i), func=AF.Copy), [bsin], [bsinb])
                                        else:
                                            A(lambda e: e.activation(out=sinb[:, 0:dkc, 0:dv], in_=sin_[:, 0:dkc, 0:dv], func=AF.Copy), [bsin], [bsinb])
                                        km, bkm = km_r.next()
                                        V(lambda e: e.tensor_scalar(
                                            out=km[:, 0:dkc * 128], in0=kw[0:64, 0:dkc * 128], scalar1=rowm_sb[0:64, i:i + 1], scalar2=None,
                                            op0=ALU.mult), [bkw, bconst], [bkm])
                                        preps[i] = (qm, bqm, sinb, bsinb, km, bkm)
                                    prep_seq(0)

                                    def do_seq(i):
                                        sin_, bsin = loads[i]
                                        if i + 1 < NS:
                                            prep_seq(i + 1)
                                        qm, bqm, sinb, bsinb, km, bkm = preps.pop(i)
                                        mm_pairs = [(qm[:, c, :], sinb[:, c, 0:dv]) for c in range(dkc)]

                                        def fn_oi(e):
                                            for c, (l, r) in enumerate(mm_pairs):
                                                ins = e.matmul(out=banks[7][0:64, 0:dv], lhsT=l, rhs=r,
                                                               start=(i == 0 and c == 0), stop=(i == NS - 1 and c == dkc - 1))
                                            return ins
                                        P.op("tensor", fn_oi, R=[bqm, bsinb], W=[b7])
                                        def do_sc(c):
                                            bi, bb = psum()
                                            mm(banks[bi][:, 0:dv], [(km[:, c * 128:(c + 1) * 128], vbf[0:64, 0:dv])], [bkm, bvbf], [bb])
                                            if gla:
                                                dsd, bdsd = dsd_of[t.key]
                                                V(lambda e: e.scalar_tensor_tensor(
                                                    out=st_ap(sin_, i), in0=st_ap(sin_, i), scalar=dsd[:, i:i + 1], in1=banks[bi][:, 0:dv],
                                                    op0=ALU.mult, op1=ALU.add), [bb, bsin, bdsd], [bsin])
                                            else:
                                                V(lambda e: e.scalar_tensor_tensor(
                                                    out=sin_[:, c, 0:dv], in0=sin_[:, c, 0:dv], scalar=dchunk[h][1], in1=banks[bi][:, 0:dv],
                                                    op0=ALU.mult, op1=ALU.add), [bb, bsin], [bsin])
                                        for c in range(dkc):
                                            do_sc(c)
                                        lo = lanes_of[i]
                                        if gla:
                                            if i % 4 == 3:
                                                P.dma("sync", f"sout{lo}", lambda e, inc: inc(e.dma_start(
                                                    out=sgs[i - 3:i + 1, h, :, :].rearrange("s d v -> d s v"),
                                                    in_=sin_[:, :, :].rearrange("p c (s v) -> p (c s) v", v=256))), R=[bsin])
                                        else:
                                            P.dma("sync", f"sout{lo}", lambda e, inc: inc(e.dma_start(
                                                out=srs[i, h, :, :].rearrange("(c p) v -> p c v", p=128), in_=sin_[:, :, :])), R=[bsin])
                                        if (not gla) and i + 3 < NS:
                                            issue_load(i + 3)
                                    for i in range(NS):
                                        do_seq(i)
                                        if (i + 1) % seq_per_step[0] == 0 or i == NS - 1:
                                            yield
                                    A(lambda e: e.activation(out=oint[:, 0:dv], in_=banks[7][0:64, 0:dv], func=AF.Copy), [b7], [boint])
                                    issue_o()
                                    bio_s, bbo_s = osl["bio"], osl["bbo"]
                                    V(lambda e: e.tensor_tensor(out=oint[:, 0:dv], in0=banks[bio_s][0:64, 0:dv], in1=oint[:, 0:dv], op=ALU.add),
                                      [bbo_s, boint], [boint])
                                    o_src = oint[:, 0:dv]
                                    bo_src = boint
                                yield
                                stt_, bst = st_r.next()
                                on, bon = on_r.next()
                                if gla:
                                    junk, bjunk = t512.next()
                                    A(lambda e: e.activation(
                                        out=junk[0:nt, 0:dv], in_=o_src, func=AF.Square, accum_out=stt_[0:nt, 0:1]), [bo_src], [bjunk, bst])
                                    V(lambda e: e.tensor_scalar(
                                        out=stt_[0:nt, 1:2], in0=stt_[0:nt, 0:1], scalar1=1.0 / dv, scalar2=EPS, op0=ALU.mult, op1=ALU.add), [bst], [bst])
                                    G(lambda e: e.tensor_tensor(out=stt_[0:nt, 2:3], in0=stt_[0:nt, 1:2], in1=mhalf[0:nt, :], op=ALU.pow),
                                      [bst, bconst], [bst])
                                    yield
                                    V(lambda e: e.scalar_tensor_tensor(
                                        out=on[0:nt, 0:dv], in0=o_src, scalar=stt_[0:nt, 2:3], in1=zs[0:nt, 0:dv], op0=ALU.mult, op1=ALU.mult),
                                      [bo_src, bst, bzs], [bon])
                                else:
                                    V(lambda e: e.bn_stats(out=stt_[0:nt, 0:6], in_=o_src), [bo_src], [bst])
                                    V(lambda e: e.bn_aggr(out=stt_[0:nt, 6:8], in_=stt_[0:nt, 0:6]), [bst], [bst])
                                    V(lambda e: e.tensor_scalar(
                                        out=stt_[0:nt, 8:9], in0=stt_[0:nt, 7:8], scalar1=EPS, scalar2=None, op0=ALU.add), [bst], [bst])
                                    G(lambda e: e.tensor_tensor(out=stt_[0:nt, 9:10], in0=stt_[0:nt, 8:9], in1=mhalf[0:nt, :], op=ALU.pow),
                                      [bst, bconst], [bst])
                                    yield
                                    onm, bonm = t512.next()
                                    V(lambda e: e.tensor_scalar(
                                        out=onm[0:nt, :], in0=o_src, scalar1=stt_[0:nt, 6:7], scalar2=stt_[0:nt, 9:10],
                                        op0=ALU.subtract, op1=ALU.mult), [bo_src, bst], [bonm])
                                    G(lambda e: e.tensor_tensor(out=on[0:nt, :], in0=onm[0:nt, :], in1=zs[0:nt, :], op=ALU.mult), [bonm, bzs], [bon])
                                yield
                                bix, bbx = psum()
                                tr([(banks[bix][:, j * 128:j * 128 + nt], on[0:nt, j * 128:(j + 1) * 128], ident_f[0:nt, 0:nt]) for j in range(nch)],
                                   [bon, bconst], [bbx])
                                for j in range(nch):
                                    A(lambda e, j=j, oc=ochunk0 + j: e.activation(
                                        out=oT[:, oc, tsl], in_=banks[bix][:, j * 128:j * 128 + nt], func=AF.Identity, scale=gnT[:, oc:oc + 1]),
                                      [bbx, bconst], [boT[u]])
                            return do_tile
                        return do_group
                    ps = {"s2": None, "s3": None, "s4": None}

                    def pstep(g, adv=None):
                        g3 = ps["s3"]
                        if g3 is not None:
                            next(g3)
                        if adv:
                            adv(0)
                        if g is not None:
                            next(g)
                        if adv:
                            adv(1)
                        if g3 is not None:
                            next(g3)
                        if adv:
                            adv(2)
                        if ps["s2"] is not None:
                            next(ps["s2"])
                        if adv:
                            adv(3)
                        if ps["s4"] is not None:
                            for _ in ps["s4"]:
                                pass
                        if adv:
                            adv(4)
                        ps["s4"] = g3
                        ps["s3"] = ps["s2"]
                        ps["s2"] = g
                    if not has_s:
                        upairs = [(u, gi) for u in range(8) for gi in range(len(groups))]
                        ufn = {0: do_unit(0)}
                        gfn = {(0, 0): ufn[0](groups[0])}
                        for j, (u, gi) in enumerate(upairs):
                            grp = groups[gi]
                            if gi == 0 and u + 1 < 8:
                                ufn[u + 1] = do_unit(u + 1)
                            for i, t in enumerate(grp):
                                last = (i == len(grp) - 1 and j + 1 < len(upairs))
                                if last:
                                    gfn[upairs[j + 1]] = ufn[upairs[j + 1][0]](groups[upairs[j + 1][1]])
                                pstep(gfn[(u, gi)](t))
                    else:
                        sgrp = [g for g in groups if g[0].samp][0]
                        pgrps = [g for g in groups if not g[0].samp]
                        ptl = [(gi, t) for gi, g in enumerate(pgrps) for t in g]
                        seq_per_step[0] = 1
                        quota = -(-NS // len(ptl))
                        sched = [quota // 5] * 5
                        for sl in [1, 3, 0, 2, 4][:quota % 5]:
                            sched[sl] += 1

                        def wrap(g):
                            yield
                            yield from g
                        ufn = {0: do_unit(0)}
                        for u in range(8):
                            if u + 1 < 8:
                                ufn[u + 1] = do_unit(u + 1)
                            gen_s = ufn[u](sgrp)(sgrp[0])
                            next(gen_s)
                            gf = {}
                            st_ = {"done": 0, "first": True}

                            def adv(slot, gen_s=gen_s, st_=st_):
                                sc = sched
                                if st_["first"]:
                                    sc = [0, 0, 0, quota - quota // 2, quota // 2]
                                for _ in range(sc[slot]):
                                    if st_["done"] < NS:
                                        next(gen_s)
                                        st_["done"] += 1
                            for idx, (gi, t) in enumerate(ptl):
                                if gi not in gf:
                                    gf[gi] = ufn[u](pgrps[gi])
                                pstep(gf[gi](t), adv)
                                st_["first"] = False
                            while st_["done"] < NS:
                                next(gen_s)
                                st_["done"] += 1
                            pstep(wrap(gen_s))
                    pstep(None)
                    pstep(None)
                    pstep(None)
                    P.barrier()

                with ExitStack() as stf:
                    mT = sb(f"mT{pi}", [128, 8, NP], BF16, stf)
                    bmT = Buf("mT")
                    gate_p = sb(f"gatep{pi}", [128, D], F32, stf)
                    gate_s = sb(f"gates{pi}", [64, D], F32, stf) if has_s else None
                    bgate = Buf("gate")
                    lng = sb(f"lng{pi}", [128, D], F32, stf)
                    lnb = sb(f"lnb{pi}", [128, D], F32, stf)
                    bln = Buf("ln")
                    P.dma("sync", "ln", lambda e, inc: inc(e.dma_start(out=lng[:, :], in_=ln_g.partition_broadcast(128))), W=[bln])
                    P.dma("sync", "ln", lambda e, inc: inc(e.dma_start(out=lnb[:, :], in_=ln_b.partition_broadcast(128))), W=[bln])
                    P.dma("sync", "ln", lambda e, inc: inc(e.dma_start(out=gate_p[:, :], in_=b_ada[2 * D:3 * D].partition_broadcast(128))), W=[bln])
                    if has_s:
                        P.dma("sync", "ln", lambda e, inc: inc(e.dma_start(out=gate_s[:, :], in_=b_ada[2 * D:3 * D].partition_broadcast(64))), W=[bln])
                    P.seal("ln", [bln])
                    V(lambda e: e.tensor_scalar(out=gate_p[:, :], in0=gate_p[:, :], scalar1=0.5, scalar2=None, op0=ALU.mult), [bln], [bln])
                    if has_s:
                        V(lambda e: e.tensor_scalar(out=gate_s[:, :], in0=gate_s[:, :], scalar1=0.5, scalar2=None, op0=ALU.mult), [bln], [bln])
                    f4 = Ring(nc, stf, f"f4{pi}", [128, 512], F32, 4)
                    xringf = Ring(nc, stf, f"xtf{pi}", [128, D], F32, 4)
                    xf = {}

                    def issue_x(idx):
                        t_ = tiles[idx]
                        xt_, bx_ = xringf.next()
                        xsrc_ = x_s[0:64, :] if t_.samp else x_p[t_.tok0:t_.tok0 + 128, :]
                        P.dma("sync", f"xf{idx % 4}", lambda e, inc: inc(e.dma_start(out=xt_[0:t_.nt, :], in_=xsrc_)), W=[bx_])
                        xf[idx] = (xt_, bx_)
                    for idx0 in range(min(4, len(tiles))):
                        issue_x(idx0)
                    ft = Ring(nc, stf, f"ft{pi}", [128, D], F32, 2)
                    yr = Ring(nc, stf, f"yr{pi}", [128, D], F32, 2)
                    stf_r = Ring(nc, stf, f"stf{pi}", [128, 24], F32, 2)
                    ylane = [0]

                    fl = {"issued": 0, "slots": {}}

                    def fl_gate(j):
                        def f(wsl):
                            wv = wslot[wsl][:, 0:4096].rearrange("p (k n) -> p k n", k=8)
                            P.dma("gpsimd", f"w{wsl}", lambda e, inc: inc(e.dma_start(
                                out=wv, in_=w_ada_v[:, :, 2 * D + j * 512:2 * D + (j + 1) * 512])), W=[bw[wsl]])
                            return wv
                        return f

                    def fl_pair(m):
                        def f(wsl):
                            wv = wslot[wsl][:, 0:40 * 256].rearrange("p (k n) -> p k n", n=256)

                            def wload(e, inc):
                                inc(e.dma_start(out=wv[:, 0:8, :], in_=w_in_v[:, :, MG + m * 128:MG + (m + 2) * 128]))
                                inc(e.dma_start(out=wv[:, 8:16, :], in_=w_in_v[:, :, MR + m * 128:MR + (m + 2) * 128]))
                                inc(e.dma_start(out=wv[:, 16:24, :], in_=wbg_v[:, :, m * 128:(m + 2) * 128]))
                                inc(e.dma_start(out=wv[:, 24:40, :], in_=wbr_v[:, :, m * 128:(m + 2) * 128]))
                            P.dma("gpsimd", f"w{wsl}", wload, W=[bw[wsl]], n=4)
                            return wv
                        return f

                    def fl_out():
                        def f(wsl):
                            wv = wslot[wsl][:, 0:8192].rearrange("p (k n) -> p k n", k=8)
                            P.dma("gpsimd", f"w{wsl}", lambda e, inc: inc(e.dma_start(out=wv, in_=w_out_v[:, :, :])), W=[bw[wsl]])
                            return wv
                        return f
                    fl_list = [fl_gate(0), fl_gate(1), fl_pair(0), fl_pair(2), fl_pair(4), fl_pair(6), fl_out()]

                    def fl_issue_upto(k):
                        while fl["issued"] <= k and fl["issued"] < len(fl_list):
                            j = fl["issued"]
                            wsl = next_w()
                            fl["slots"][j] = (wsl, fl_list[j](wsl))
                            fl["issued"] += 1

                    def fl_get(j):
                        fl_issue_upto(j + 1)
                        return fl["slots"][j]

                    def do_gate(j):
                        wsl, wv = fl_get(j)
                        bi, bb = psum7()
                        mm(banks[bi][:, 0:512], [(cT[:, kc, 64:192], wv[:, kc, :]) for kc in range(8)], [bcT, bw[wsl]], [bb])
                        V(lambda e: e.scalar_tensor_tensor(
                            out=gate_p[:, j * 512:(j + 1) * 512], in0=banks[bi][:, 0:512], scalar=0.5, in1=gate_p[:, j * 512:(j + 1) * 512],
                            op0=ALU.mult, op1=ALU.add), [bb, bln], [bgate])
                        if has_s:
                            bi2, bb2 = psum7()
                            mm(banks[bi2][0:64, 0:512], [(cT[:, kc, 0:64], wv[:, kc, :]) for kc in range(8)], [bcT, bw[wsl]], [bb2])
                            V(lambda e: e.scalar_tensor_tensor(
                                out=gate_s[:, j * 512:(j + 1) * 512], in0=banks[bi2][0:64, 0:512], scalar=0.5, in1=gate_s[:, j * 512:(j + 1) * 512],
                                op0=ALU.mult, op1=ALU.add), [bb2, bln], [bgate])
                    for j in range(2):
                        do_gate(j)

                    wcur = {}

                    def do_m(m):
                        mo = (m % 2) * 128
                        wsl, wv = fl_get(2 + m // 2)

                        def do_fg(grp):
                            g0 = grp[0].col
                            NG = sum(t.nt for t in grp)
                            gs = slice(g0, g0 + NG)
                            b_mg, bb_mg = psum7()
                            mm(banks[b_mg][:, 0:NG], [(wv[:, kc, mo:mo + 128], hT[:, kc, gs]) for kc in range(8)], [bw[wsl], bhT], [bb_mg])
                            b_pg, bb_pg = psum7()
                            mm(banks[b_pg][:, 0:NG], [(wv[:, 16 + kc, mo:mo + 128], oT[:, kc, gs]) for kc in range(8)], [bw[wsl]] + boT[0:4], [bb_pg])
                            b_mr, bb_mr = psum7()
                            mm(banks[b_mr][:, 0:NG], [(wv[:, 8 + kc, mo:mo + 128], hT[:, kc, gs]) for kc in range(8)], [bw[wsl], bhT], [bb_mr])
                            b_pr, bb_pr = psum7()
                            mm(banks[b_pr][:, 0:NG], [(wv[:, 24 + kc, mo:mo + 128], oT[:, 8 + kc, gs]) for kc in range(16)], [bw[wsl]] + boT[4:8], [bb_pr])
                            tg, btg = f4.next()
                            trr, btr = f4.next()
                            A(lambda e: e.activation(out=tg[:, 0:NG], in_=banks[b_mg][:, 0:NG], func=AF.Tanh, scale=0.5), [bb_mg], [btg])
                            A(lambda e: e.activation(out=trr[:, 0:NG], in_=banks[b_mr][:, 0:NG], func=AF.Tanh, scale=0.5), [bb_mr], [btr])
                            V(lambda e: e.scalar_tensor_tensor(
                                out=tg[:, 0:NG], in0=tg[:, 0:NG], scalar=1.0, in1=banks[b_pg][:, 0:NG], op0=ALU.add, op1=ALU.mult), [btg, bb_pg], [btg])
                            V(lambda e: e.scalar_tensor_tensor(
                                out=trr[:, 0:NG], in0=trr[:, 0:NG], scalar=1.0, in1=banks[b_pr][:, 0:NG], op0=ALU.add, op1=ALU.mult), [btr, bb_pr], [btr])
                            G(lambda e: e.tensor_tensor(out=mT[:, m, gs], in0=tg[:, 0:NG], in1=trr[:, 0:NG], op=ALU.add), [btg, btr], [bmT])
                        for grp in groups:
                            do_fg(grp)
                    for m in range(8):
                        do_m(m)
                    wslo, wvo = fl_get(6)

                    def do_ft(t):
                        nt = t.nt
                        tsl = slice(t.col, t.col + nt)
                        tidx = tiles.index(t)
                        xt, bx = xf[tidx]
                        gt = gate_s if t.samp else gate_p
                        vt, bvt = ft.next()
                        for n2 in range(2):
                            bi, bb = psum7()
                            mm(banks[bi][0:nt, 0:512], [(mT[:, kc, tsl], wvo[:, kc, n2 * 512:(n2 + 1) * 512]) for kc in range(8)], [bmT, bw[wslo]], [bb])
                            V(lambda e, bi=bi, n2=n2: e.tensor_tensor(
                                out=vt[0:nt, n2 * 512:(n2 + 1) * 512], in0=banks[bi][0:nt, 0:512], in1=gt[0:nt, n2 * 512:(n2 + 1) * 512], op=ALU.mult),
                              [bb, bgate], [bvt])
                        V(lambda e: e.scalar_tensor_tensor(
                            out=vt[0:nt, :], in0=xt[0:nt, :], scalar=ALPHA, in1=vt[0:nt, :], op0=ALU.mult, op1=ALU.add), [bx, bvt], [bvt])
                        if tidx + 4 < len(tiles):
                            issue_x(tidx + 4)
                        sf, bsf = stf_r.next()
                        V(lambda e: e.bn_stats(out=sf[0:nt, 0:6], in_=vt[0:nt, 0:512]), [bvt], [bsf])
                        V(lambda e: e.bn_stats(out=sf[0:nt, 6:12], in_=vt[0:nt, 512:1024]), [bvt], [bsf])
                        V(lambda e: e.bn_aggr(out=sf[0:nt, 12:14], in_=sf[0:nt, 0:12]), [bsf], [bsf])
                        V(lambda e: e.tensor_scalar(out=sf[0:nt, 14:15], in0=sf[0:nt, 13:14], scalar1=EPS, scalar2=None, op0=ALU.add), [bsf], [bsf])
                        G(lambda e: e.tensor_tensor(out=sf[0:nt, 15:16], in0=sf[0:nt, 14:15], in1=mhalf[0:nt, :], op=ALU.pow), [bsf, bconst], [bsf])
                        yield
                        V(lambda e: e.scalar_tensor_tensor(
                            out=sf[0:nt, 16:17], in0=sf[0:nt, 12:13], scalar=-1.0, in1=sf[0:nt, 15:16], op0=ALU.mult, op1=ALU.mult), [bsf], [bsf])
                        yt, byt = yr.next()
                        A(lambda e: e.activation(
                            out=yt[0:nt, :], in_=vt[0:nt, :], func=AF.Identity, scale=sf[0:nt, 15:16], bias=sf[0:nt, 16:17]), [bvt, bsf], [byt])
                        V(lambda e: e.tensor_tensor(out=yt[0:nt, :], in0=yt[0:nt, :], in1=lng[0:nt, :], op=ALU.mult), [byt, bln], [byt])
                        G(lambda e: e.tensor_tensor(out=yt[0:nt, :], in0=yt[0:nt, :], in1=lnb[0:nt, :], op=ALU.add), [byt, bln], [byt])
                        ydst = y_s[0:64, :] if t.samp else y_p[t.tok0:t.tok0 + 128, :]
                        yl = ylane[0] % 2
                        ylane[0] += 1
                        P.dma("sync", f"y{yl}", lambda e, inc: inc(e.dma_start(out=ydst, in_=yt[0:nt, :])), R=[byt])
                    prev_g = None
                    for t in tiles:
                        g_ = do_ft(t)
                        next(g_)
                        if prev_g is not None:
                            for _ in prev_g:
                                pass
                        prev_g = g_
                    for _ in prev_g:
                        pass
                    P.barrier()

        for pi, pkeys in enumerate(PASSES):
            do_pass(pi, pkeys)
        P.barrier()
        P.emit()
    return nc, dbg_outs


def _constants():
    half = 128
    inv_freq = (10000.0 ** (-np.arange(half, dtype=np.float32) / np.float32(half))).astype(np.float32)
    pos = np.concatenate([np.arange(TP, dtype=np.int32),
                          np.tile(16384 + np.arange(TS, dtype=np.int32), NS)]).astype(np.float32)
    ang = (pos[:, None] * inv_freq[None, :]).astype(np.float32)
    rope = np.stack([np.cos(ang).T, np.sin(ang).T], axis=1).astype(np.float32)
    lg = np.log1p(-np.exp2(-5.0 - np.arange(4, dtype=np.float32))).astype(np.float32)
    rmask = np.zeros((128, 4, 2, 128), np.float32)
    rdread = np.zeros((128, 4, 2, 128), np.float32)
    rdwrite = np.zeros((128, 8), np.float32)
    idx = np.arange(128)
    for h in range(4):
        for kind, L in ((0, 128), (1, 4)):
            n = 128 if kind == 0 else 64
            s = idx[:n, None]
            t = idx[None, :n]
            same = (s // L) == (t // L)
            diff = (t - s).astype(np.float32)
            m = np.where(same & (t >= s), np.exp(np.maximum(diff, 0.0) * lg[h]), 0.0).astype(np.float32) / np.float32(16.0)
            rmask[:n, h, kind, :n] = m
            rdread[:, h, kind, :n] = np.exp(((idx[:n] % L) + 1.0).astype(np.float32) * lg[h])[None, :]
            rdwrite[:n, 2 * h + kind] = np.exp((L - 1.0 - (idx[:n] % L)).astype(np.float32) * lg[h]) / np.float32(16.0)
    tri = np.zeros((128, 2, 272), np.float32)
    for kind, L in ((0, 128), (1, 4)):
        n = 128 if kind == 0 else 64
        s = idx[:n, None]
        t = idx[None, :n]
        same = (s // L) == (t // L)
        tri[:n, kind, 0:n] = (same & (s <= t)).astype(np.float32)
        tri[:n, kind, 128:128 + n] = (same & (s > t)).astype(np.float32)
        for i in range(16):
            tri[:n, kind, 256 + i] = ((idx[:n] // L) == i).astype(np.float32)
    colmask = np.zeros((128, 16, 64), np.float32)
    rowmask = np.zeros((128, 16), np.float32)
    for i in range(16):
        colmask[:, i, 4 * i:4 * i + 4] = 1.0
        rowmask[4 * i:4 * i + 4, i] = 1.0
    return dict(rope=rope, rmask=rmask, rdread=rdread, rdwrite=rdwrite, tri=tri, colmask=colmask,
                rowmask=rowmask, ident=np.eye(128, dtype=np.float32))


def _in_maps(inp):
    consts = _constants()
    f = lambda a: np.ascontiguousarray(np.asarray(a, dtype=np.float32))
    shared = dict(
        w_ada=f(inp["w_ada"][0]), b_ada=f(inp["b_ada"][0]), w_in=f(inp["w_in"][0]), w_lr2=f(inp["w_lr2"][0]),
        b_lr2=f(inp["b_lr2"][0]), gla_norm_g=f(inp["gla_norm_g"][0]), ret_norm_g=f(inp["ret_norm_g"][0]),
        w_branch_gla=f(inp["w_branch_gla"][0]), w_branch_ret=f(inp["w_branch_ret"][0]), w_out=f(inp["w_out"][0]),
        ln_g=f(inp["ln_g"][0]), ln_b=f(inp["ln_b"][0]), **consts)
    maps = []
    for c in range(NCORES):
        sl = slice(NS * c, NS * (c + 1))
        c_all = np.concatenate([np.repeat(np.asarray(inp["c_sample"][sl]), TS, axis=0),
                                np.repeat(np.asarray(inp["c_prompt"][c:c + 1]), 128, axis=0)], axis=0)
        m = dict(shared)
        m.update(x_p=f(inp["x_prompt"][c]), x_s=f(np.asarray(inp["x_sample"][sl]).reshape(NS * TS, D)), c_all=f(c_all),
                 sg_in=f(inp["state_gla"][0, sl]), sr_in=f(inp["state_ret"][0, sl]))
        maps.append(m)
    return maps


def kernel(**inputs):
    nc, _ = build_program()
    maps = _in_maps(inputs)
    res = run_bass_kernel_spmd(nc, maps, core_ids=list(range(NCORES)))
    r = res.results
    y_p = np.stack([r[c]["y_p"] for c in range(NCORES)], axis=0)
    y_s = np.concatenate([r[c]["y_s"].reshape(NS, TS, D) for c in range(NCORES)], axis=0)
    sgp = np.stack([r[c]["sgp"] for c in range(NCORES)], axis=0)[None]
    srp = np.stack([r[c]["srp"] for c in range(NCORES)], axis=0)[None]
    sgs = np.concatenate([r[c]["sgs"] for c in range(NCORES)], axis=0)[None]
    srs = np.concatenate([r[c]["srs"] for c in range(NCORES)], axis=0)[None]
    return (y_p.astype(np.float32), y_s.astype(np.float32), sgp.astype(np.float32), srp.astype(np.float32),
            sgs.astype(np.float32), srs.astype(np.float32))
```
